# Optimizing a Trainium2 kernel written in Bass

```python
import math
import numpy as np
import jax
import jax.numpy as jnp
from jax import lax

D_MODEL = 1024
BATCH = 8
SEQ = 4096
DEPTH = 2
DEC_BATCH = 32
DEC_SEQ = 1
PAST_LEN = 16384
PAGE_SIZE = 128

DN_ALPHA = (2.0 * DEPTH) ** 0.25
DN_BETA = (8.0 * DEPTH) ** -0.25
LN_EPS = 1e-5
D_FF = 2816
FFN_RES = 0.5
S5_WIDTH = 512
S5_GROUP = 16
S5_GROUPS = S5_WIDTH // S5_GROUP
S5_STATE = 64
S5_DT_MIN = 1e-3
S5_DT_MAX = 1e-1
NSA_HEADS = 8
NSA_KV_HEADS = 2
NSA_HEAD_DIM = 64
NSA_GQ = NSA_HEADS // NSA_KV_HEADS
CMP_LEN = 32
CMP_STRIDE = 16
CMP_HIDDEN = 128
SEL_LEN = 64
SEL_TOP = 16
WINDOW = 512
NSA_QBLOCK = 64
ROPE_THETA = 500000.0
ROT_DIM = NSA_HEAD_DIM // 4
FORCE_SCORE = 1e9
CONV_CH = 512
CONV_K = 3
N_BRANCH = 3
KV_WIDTH = NSA_KV_HEADS * NSA_HEAD_DIM
IN_WIDTHS = (S5_WIDTH, NSA_HEADS * NSA_HEAD_DIM, 6 * KV_WIDTH, 3 * NSA_HEADS, 3 * CONV_CH, N_BRANCH * D_MODEL)
IN_OFFSETS = tuple(int(o) for o in np.cumsum(IN_WIDTHS)[:-1])
D_IN = int(sum(IN_WIDTHS))

kernel_name = 'hybrid_s5_nsa_shortconv_decoder_step'


def layer_norm(x, g, b):
    xf = x.astype(jnp.float32)
    mu = xf.mean(-1, keepdims=True)
    var = jnp.square(xf - mu).mean(-1, keepdims=True)
    return ((xf - mu) * lax.rsqrt(var + LN_EPS) * g + b).astype(x.dtype)


def swiglu(x, w_gu, w_down):
    g, u = jnp.split(x @ w_gu, 2, axis=-1)
    return (jax.nn.silu(g) * u) @ w_down


def partial_rope(x, pos):
    half = ROT_DIM // 2
    inv_freq = ROPE_THETA ** (-jnp.arange(half, dtype=jnp.float32) / half)
    ang = pos.astype(jnp.float32)[:, None] * inv_freq
    cos = jnp.cos(ang)[:, None, :]
    sin = jnp.sin(ang)[:, None, :]
    xr = x[..., :ROT_DIM].astype(jnp.float32)
    x1, x2 = xr[..., :half], xr[..., half:]
    rot = jnp.concatenate([x1 * cos - x2 * sin, x2 * cos + x1 * sin], axis=-1).astype(x.dtype)
    return jnp.concatenate([rot, x[..., ROT_DIM:]], axis=-1)


def masked_softmax(s, mask):
    s = jnp.where(mask, s.astype(jnp.float32), -jnp.inf)
    m = jnp.max(s, axis=-1, keepdims=True)
    m = jnp.where(jnp.isfinite(m), m, 0.0)
    e = jnp.where(mask, jnp.exp(s - m), 0.0)
    return e / jnp.maximum(e.sum(-1, keepdims=True), 1e-30)


def s5_discretize(lam_re, lam_im, log_dt, b_re, b_im):
    dt = jnp.exp(log_dt.astype(jnp.float32))[:, None]
    lr = lam_re.astype(jnp.float32)
    li = lam_im.astype(jnp.float32)
    mag = jnp.exp(lr * dt)
    a_re = mag * jnp.cos(li * dt)
    a_im = mag * jnp.sin(li * dt)
    den = lr * lr + li * li
    r_re = ((a_re - 1.0) * lr + a_im * li) / den
    r_im = (a_im * lr - (a_re - 1.0) * li) / den
    br = b_re.astype(jnp.float32)
    bi = b_im.astype(jnp.float32)
    bb_re = r_re[..., None] * br - r_im[..., None] * bi
    bb_im = r_re[..., None] * bi + r_im[..., None] * br
    return a_re, a_im, bb_re, bb_im


def complex_affine_combine(e1, e2):
    a1r, a1i, b1r, b1i = e1
    a2r, a2i, b2r, b2i = e2
    return (a2r * a1r - a2i * a1i,
            a2r * a1i + a2i * a1r,
            a2r * b1r - a2i * b1i + b2r,
            a2r * b1i + a2i * b1r + b2i)


def s5_scan(u, s0, lam_re, lam_im, log_dt, b_re, b_im, c_re, c_im, d_skip):
    B, T, _ = u.shape
    a_re, a_im, bb_re, bb_im = s5_discretize(lam_re, lam_im, log_dt, b_re, b_im)
    uf = u.astype(jnp.float32)
    ug = uf.reshape(B, T, S5_GROUPS, S5_GROUP)
    bu_re = jnp.einsum('btgi,gpi->tbgp', ug, bb_re)
    bu_im = jnp.einsum('btgi,gpi->tbgp', ug, bb_im)
    a_shape = (T, 1, S5_GROUPS, S5_STATE)
    acc_re, acc_im, s_re, s_im = lax.associative_scan(
        complex_affine_combine,
        (jnp.broadcast_to(a_re, a_shape), jnp.broadcast_to(a_im, a_shape), bu_re, bu_im), axis=0)
    if s0 is not None:
        s0_re = s0[0].astype(jnp.float32)[None]
        s0_im = s0[1].astype(jnp.float32)[None]
        s_re, s_im = (s_re + acc_re * s0_re - acc_im * s0_im,
                      s_im + acc_re * s0_im + acc_im * s0_re)
    y = (jnp.einsum('tbgp,gop->btgo', s_re, c_re.astype(jnp.float32))
         - jnp.einsum('tbgp,gop->btgo', s_im, c_im.astype(jnp.float32)))
    y = y.reshape(B, T, S5_WIDTH) + d_skip.astype(jnp.float32) * uf
    return y.astype(u.dtype), s_re[-1].astype(u.dtype), s_im[-1].astype(u.dtype)


def causal_short_conv(v, w, buf):
    B, T, C = v.shape
    if buf is None:
        buf = jnp.zeros((B, CONV_K - 1, C), v.dtype)
    vf = jnp.concatenate([buf.astype(v.dtype), v], axis=1)
    y = lax.conv_general_dilated(vf, w[:, None, :].astype(v.dtype), window_strides=(1,), padding='VALID',
                                 dimension_numbers=('NWC', 'WIO', 'NWC'), feature_group_count=C)
    return y, vf[:, -(CONV_K - 1):]


def nsa_compress(x_raw, pe, w1, w2):
    B, T = x_raw.shape[:2]
    n_ch = -(-T // CMP_STRIDE)
    x = jnp.pad(x_raw, ((0, 0), (0, n_ch * CMP_STRIDE - T), (0, 0), (0, 0)))
    ch = x.reshape(B, n_ch, CMP_STRIDE, NSA_KV_HEADS, NSA_HEAD_DIM)
    w1 = w1.reshape(CMP_LEN, NSA_HEAD_DIM, CMP_HIDDEN)
    h_a = jnp.einsum('bcskd,sdh->bckh', ch, w1[:CMP_STRIDE])
    h_b = jnp.einsum('bcskd,sdh->bckh', ch, w1[CMP_STRIDE:])
    h_pe = jnp.einsum('sd,sdh->h', pe, w1)
    h = jax.nn.gelu(h_a[:, :-1] + h_b[:, 1:] + h_pe)
    return h @ w2


def cmp_to_sel_map(n_cmp, n_sel):
    c0 = np.arange(n_cmp) * CMP_STRIDE
    s0 = np.arange(n_sel) * SEL_LEN
    m = (c0[:, None] < s0[None, :] + SEL_LEN) & (s0[None, :] < c0[:, None] + CMP_LEN)
    return jnp.asarray(m.astype(np.float32))


def to_sel_blocks(x):
    B, T = x.shape[:2]
    ns = -(-T // SEL_LEN)
    x = jnp.pad(x, ((0, 0), (0, ns * SEL_LEN - T), (0, 0), (0, 0)))
    return x.reshape(B, ns, SEL_LEN, NSA_KV_HEADS, NSA_HEAD_DIM).transpose(0, 3, 1, 2, 4)


def nsa_core(q, q_pos, gates, kc, vc, c_end, ksb, vsb, kw, vw, kw_pos):
    B, Tq = q.shape[:2]
    qg = q.reshape(B, Tq, NSA_KV_HEADS, NSA_GQ, NSA_HEAD_DIM) * (NSA_HEAD_DIM ** -0.5)
    s_c = jnp.einsum('btkgd,bnkd->btkgn', qg, kc)
    p_c = masked_softmax(s_c, (c_end[None, :] <= q_pos[:, None])[None, :, None, None, :])
    o_c = jnp.einsum('btkgn,bnkd->btkgd', p_c.astype(vc.dtype), vc)
    n_sel = ksb.shape[2]
    imp = jnp.einsum('btkn,nj->btkj', p_c.sum(3), cmp_to_sel_map(kc.shape[1], n_sel))
    blk = jnp.arange(n_sel)[None, :]
    cur = (q_pos // SEL_LEN)[:, None]
    valid = blk <= cur
    forced = valid & ((blk == 0) | (blk >= cur - 1))
    score = jnp.where(forced[None, :, None, :], FORCE_SCORE,
                      jnp.where(valid[None, :, None, :], imp, -jnp.inf))
    _, idx = lax.top_k(score, min(SEL_TOP, n_sel))
    b_i = jnp.arange(B)[:, None, None, None]
    k_i = jnp.arange(NSA_KV_HEADS)[None, None, :, None]
    ks_sel = ksb[b_i, k_i, idx]
    vs_sel = vsb[b_i, k_i, idx]
    k_pos = idx[..., None] * SEL_LEN + jnp.arange(SEL_LEN)
    s_s = jnp.einsum('btkgd,btknsd->btkgns', qg, ks_sel)
    m_s = jnp.broadcast_to((k_pos <= q_pos[None, :, None, None, None])[:, :, :, None], s_s.shape)
    flat = (B, Tq, NSA_KV_HEADS, NSA_GQ, -1)
    p_s = masked_softmax(s_s.reshape(flat), m_s.reshape(flat))
    o_s = jnp.einsum('btkgm,btkmd->btkgd', p_s.astype(vsb.dtype),
                     vs_sel.reshape(B, Tq, NSA_KV_HEADS, -1, NSA_HEAD_DIM))
    s_w = jnp.einsum('btkgd,bwkd->btkgw', qg, kw)
    dpos = q_pos[:, None] - kw_pos[None, :]
    m_w = (dpos >= 0) & (dpos < WINDOW) & (kw_pos[None, :] >= 0)
    p_w = masked_softmax(s_w, m_w[None, :, None, None, :])
    o_w = jnp.einsum('btkgw,bwkd->btkgd', p_w.astype(vw.dtype), vw)
    g = gates.reshape(B, Tq, NSA_KV_HEADS, NSA_GQ, 3)
    o = g[..., 0:1] * o_c + g[..., 1:2] * o_s + g[..., 2:3] * o_w
    return o.reshape(B, Tq, NSA_HEADS * NSA_HEAD_DIM)


def nsa_prompt_blocks(q, gates, kc, vc, c_end, ksb, vsb, kw, vw):
    B, T = q.shape[:2]
    nb = T // NSA_QBLOCK
    qb = q.reshape(B, nb, NSA_QBLOCK, NSA_HEADS, NSA_HEAD_DIM).swapaxes(0, 1)
    gb = gates.reshape(B, nb, NSA_QBLOCK, NSA_HEADS, 3).swapaxes(0, 1)
    pad = ((0, 0), (WINDOW, 0), (0, 0), (0, 0))
    kw_pad = jnp.pad(kw, pad)
    vw_pad = jnp.pad(vw, pad)
    span = WINDOW + NSA_QBLOCK

    def one_block(args):
        n, q_n, g_n = args
        start = n * NSA_QBLOCK
        q_pos = start + jnp.arange(NSA_QBLOCK)
        kw_n = lax.dynamic_slice_in_dim(kw_pad, start, span, axis=1)
        vw_n = lax.dynamic_slice_in_dim(vw_pad, start, span, axis=1)
        kw_pos = start - WINDOW + jnp.arange(span)
        return nsa_core(q_n, q_pos, g_n, kc, vc, c_end, ksb, vsb, kw_n, vw_n, kw_pos)

    out = lax.map(one_block, (jnp.arange(nb), qb, gb))
    return out.swapaxes(0, 1).reshape(B, T, NSA_HEADS * NSA_HEAD_DIM)


def gather_pages(pool, layer, page_table):
    g = pool[layer, page_table]
    B, n_pages = page_table.shape
    return g.reshape(B, n_pages * PAGE_SIZE, 2, NSA_KV_HEADS, NSA_HEAD_DIM)


def token_mixer(h, lp, past):
    B, T, _ = h.shape
    z = h @ lp['w_in']
    u, q, kv, g_nsa, conv_in, g_mrg = jnp.split(z, IN_OFFSETS, axis=-1)
    if past is None:
        layer = None
        offset = 0
    else:
        layer = past['layer']
        offset = past['page_table'].shape[1] * PAGE_SIZE
    pos = offset + jnp.arange(T)

    s0 = None if past is None else (past['s5_re'][layer], past['s5_im'][layer])
    y_a, s5_re, s5_im = s5_scan(u, s0, lp['s5_lambda_re'], lp['s5_lambda_im'], lp['s5_log_dt'],
                                lp['s5_b_re'], lp['s5_b_im'], lp['s5_c_re'], lp['s5_c_im'], lp['s5_d'])
    y_a = jax.nn.gelu(y_a)
    y_a = y_a * jax.nn.sigmoid(y_a @ lp['s5_w_glu'] + lp['s5_b_glu'])

    c_b, c_c, c_h = jnp.split(conv_in, 3, axis=-1)
    conv_y, conv_state = causal_short_conv(c_c * c_h, lp['conv_w'], None if past is None else past['conv'][layer])
    y_c = c_b * conv_y

    qh = partial_rope(q.reshape(B, T, NSA_HEADS, NSA_HEAD_DIM), pos)
    kc_raw, vc_raw, ks, vs, kw, vw = [t.reshape(B, T, NSA_KV_HEADS, NSA_HEAD_DIM) for t in jnp.split(kv, 6, axis=-1)]
    ks = partial_rope(ks, pos)
    kw = partial_rope(kw, pos)
    gates = jax.nn.sigmoid(g_nsa).reshape(B, T, NSA_HEADS, 3)
    cmp_rows = jnp.stack([kc_raw, vc_raw], axis=2)
    sel_rows = jnp.stack([ks, vs], axis=2)
    win_rows = jnp.stack([kw, vw], axis=2)
    if past is None:
        kc_all, vc_all, ks_all, vs_all = kc_raw, vc_raw, ks, vs
        win_all = win_rows
        keep = min(WINDOW, T)
    else:
        past_cmp = gather_pages(past['cmp'], layer, past['page_table'])
        past_sel = gather_pages(past['sel'], layer, past['page_table'])
        kc_all = jnp.concatenate([past_cmp[:, :, 0], kc_raw], axis=1)
        vc_all = jnp.concatenate([past_cmp[:, :, 1], vc_raw], axis=1)
        ks_all = jnp.concatenate([past_sel[:, :, 0], ks], axis=1)
        vs_all = jnp.concatenate([past_sel[:, :, 1], vs], axis=1)
        win_all = jnp.concatenate([past['win'][layer], win_rows], axis=1)
        keep = past['win'].shape[2]
    kc = nsa_compress(kc_all, lp['nsa_pe_k'], lp['nsa_phi_k1'], lp['nsa_phi_k2'])
    vc = nsa_compress(vc_all, lp['nsa_pe_v'], lp['nsa_phi_v1'], lp['nsa_phi_v2'])
    c_end = jnp.arange(kc.shape[1]) * CMP_STRIDE + CMP_LEN - 1
    kc = partial_rope(kc, c_end)
    ksb = to_sel_blocks(ks_all)
    vsb = to_sel_blocks(vs_all)
    if past is None:
        y_b = nsa_prompt_blocks(qh, gates, kc, vc, c_end, ksb, vsb, kw, vw)
    else:
        kw_pos = offset - keep + jnp.arange(keep + T)
        y_b = nsa_core(qh, pos, gates, kc, vc, c_end, ksb, vsb, win_all[:, :, 0], win_all[:, :, 1], kw_pos)
    win_state = win_all[:, -keep:]

    g = jax.nn.sigmoid(g_mrg).reshape(B, T, N_BRANCH, D_MODEL)
    merged = (g[:, :, 0] * (y_a @ lp['w_proj_a'])
              + g[:, :, 1] * (y_b @ lp['w_proj_b'])
              + g[:, :, 2] * (y_c @ lp['w_proj_c']))
    return merged @ lp['w_o'], (cmp_rows, sel_rows, win_state, conv_state, s5_re, s5_im)


def decoder_layer(x, lp, past):
    x = layer_norm(DN_ALPHA * x + FFN_RES * swiglu(x, lp['ffn1_w_gu'], lp['ffn1_w_down']), lp['ln_g'][0], lp['ln_b'][0])
    m, states = token_mixer(x, lp, past)
    x = layer_norm(DN_ALPHA * x + m, lp['ln_g'][1], lp['ln_b'][1])
    x = layer_norm(DN_ALPHA * x + FFN_RES * swiglu(x, lp['ffn2_w_gu'], lp['ffn2_w_down']), lp['ln_g'][2], lp['ln_b'][2])
    return x, states


def run_trunk(x, weights, past):
    per_layer = []
    for l in range(DEPTH):
        lp = {name: w[l] for name, w in weights.items()}
        x, st = decoder_layer(x, lp, None if past is None else dict(past, layer=l))
        per_layer.append(st)
    stacked = [jnp.stack(s, axis=0) for s in zip(*per_layer)]
    return x, stacked


def setup_inputs(seed: int = 0) -> dict:
    key = jax.random.key(seed)
    ks = jax.random.split(key, 48)
    cnt = [0]

    def nxt():
        cnt[0] += 1
        return ks[cnt[0] - 1]

    def nrm(shape, scale):
        return jax.random.normal(nxt(), shape, jnp.float32) * scale

    n_pages = PAST_LEN // PAGE_SIZE
    n_used = DEC_BATCH * n_pages
    n_pool = n_used + n_used // 4
    win_keep = min(WINDOW, PAST_LEN)
    kv_row = (2, NSA_KV_HEADS, NSA_HEAD_DIM)
    page_table = jax.random.permutation(nxt(), n_pool)[:n_used].reshape(DEC_BATCH, n_pages).astype(jnp.int32)
    lam_n = jnp.arange(S5_STATE, dtype=jnp.float32)
    gp = (DEPTH, S5_GROUPS, S5_STATE)
    return {
        'x_prompt': nrm((BATCH, SEQ, D_MODEL), 1.0),
        'x_sample': nrm((DEC_BATCH, DEC_SEQ, D_MODEL), 1.0),
        'cache_cmp_kv': nrm((DEPTH, n_pool, PAGE_SIZE) + kv_row, 1.0),
        'cache_sel_kv': nrm((DEPTH, n_pool, PAGE_SIZE) + kv_row, 1.0),
        'cache_win_kv': nrm((DEPTH, DEC_BATCH, win_keep) + kv_row, 1.0),
        'cache_conv': nrm((DEPTH, DEC_BATCH, CONV_K - 1, CONV_CH), 1.0),
        'state_s5_re': nrm((DEPTH, DEC_BATCH, S5_GROUPS, S5_STATE), 0.3),
        'state_s5_im': nrm((DEPTH, DEC_BATCH, S5_GROUPS, S5_STATE), 0.3),
        'page_table': page_table,
        'ln_g': 1.0 + nrm((DEPTH, 3, D_MODEL), 0.01),
        'ln_b': nrm((DEPTH, 3, D_MODEL), 0.01),
        'ffn1_w_gu': nrm((DEPTH, D_MODEL, 2 * D_FF), DN_BETA * D_MODEL ** -0.5),
        'ffn1_w_down': nrm((DEPTH, D_FF, D_MODEL), DN_BETA * D_FF ** -0.5),
        'ffn2_w_gu': nrm((DEPTH, D_MODEL, 2 * D_FF), DN_BETA * D_MODEL ** -0.5),
        'ffn2_w_down': nrm((DEPTH, D_FF, D_MODEL), DN_BETA * D_FF ** -0.5),
        'w_in': nrm((DEPTH, D_MODEL, D_IN), D_MODEL ** -0.5),
        's5_lambda_re': -0.5 + nrm(gp, 0.01),
        's5_lambda_im': math.pi * lam_n + nrm(gp, 0.01),
        's5_log_dt': jax.random.uniform(nxt(), (DEPTH, S5_GROUPS), jnp.float32,
                                        minval=math.log(S5_DT_MIN), maxval=math.log(S5_DT_MAX)),
        's5_b_re': nrm(gp + (S5_GROUP,), (2.0 * S5_GROUP) ** -0.5),
        's5_b_im': nrm(gp + (S5_GROUP,), (2.0 * S5_GROUP) ** -0.5),
        's5_c_re': nrm((DEPTH, S5_GROUPS, S5_GROUP, S5_STATE), (2.0 * S5_STATE) ** -0.5),
        's5_c_im': nrm((DEPTH, S5_GROUPS, S5_GROUP, S5_STATE), (2.0 * S5_STATE) ** -0.5),
        's5_d': nrm((DEPTH, S5_WIDTH), 1.0),
        's5_w_glu': nrm((DEPTH, S5_WIDTH, S5_WIDTH), S5_WIDTH ** -0.5),
        's5_b_glu': nrm((DEPTH, S5_WIDTH), 0.01),
        'nsa_pe_k': nrm((DEPTH, CMP_LEN, NSA_HEAD_DIM), 0.02),
        'nsa_pe_v': nrm((DEPTH, CMP_LEN, NSA_HEAD_DIM), 0.02),
        'nsa_phi_k1': nrm((DEPTH, CMP_LEN * NSA_HEAD_DIM, CMP_HIDDEN), (CMP_LEN * NSA_HEAD_DIM) ** -0.5),
        'nsa_phi_k2': nrm((DEPTH, CMP_HIDDEN, NSA_HEAD_DIM), CMP_HIDDEN ** -0.5),
        'nsa_phi_v1': nrm((DEPTH, CMP_LEN * NSA_HEAD_DIM, CMP_HIDDEN), (CMP_LEN * NSA_HEAD_DIM) ** -0.5),
        'nsa_phi_v2': nrm((DEPTH, CMP_HIDDEN, NSA_HEAD_DIM), CMP_HIDDEN ** -0.5),
        'conv_w': nrm((DEPTH, CONV_K, CONV_CH), CONV_K ** -0.5),
        'w_proj_a': nrm((DEPTH, S5_WIDTH, D_MODEL), DN_BETA * S5_WIDTH ** -0.5),
        'w_proj_b': nrm((DEPTH, NSA_HEADS * NSA_HEAD_DIM, D_MODEL), DN_BETA * (NSA_HEADS * NSA_HEAD_DIM) ** -0.5),
        'w_proj_c': nrm((DEPTH, CONV_CH, D_MODEL), DN_BETA * CONV_CH ** -0.5),
        'w_o': nrm((DEPTH, D_MODEL, D_MODEL), DN_BETA * D_MODEL ** -0.5),
    }


def reference(x_prompt, x_sample, cache_cmp_kv, cache_sel_kv, cache_win_kv, cache_conv, state_s5_re, state_s5_im,
              page_table, ln_g, ln_b, ffn1_w_gu, ffn1_w_down, ffn2_w_gu, ffn2_w_down, w_in,
              s5_lambda_re, s5_lambda_im, s5_log_dt, s5_b_re, s5_b_im, s5_c_re, s5_c_im, s5_d, s5_w_glu, s5_b_glu,
              nsa_pe_k, nsa_pe_v, nsa_phi_k1, nsa_phi_k2, nsa_phi_v1, nsa_phi_v2,
              conv_w, w_proj_a, w_proj_b, w_proj_c, w_o):
    weights = {
        'ln_g': ln_g, 'ln_b': ln_b,
        'ffn1_w_gu': ffn1_w_gu, 'ffn1_w_down': ffn1_w_down,
        'ffn2_w_gu': ffn2_w_gu, 'ffn2_w_down': ffn2_w_down,
        'w_in': w_in,
        's5_lambda_re': s5_lambda_re, 's5_lambda_im': s5_lambda_im, 's5_log_dt': s5_log_dt,
        's5_b_re': s5_b_re, 's5_b_im': s5_b_im, 's5_c_re': s5_c_re, 's5_c_im': s5_c_im,
        's5_d': s5_d, 's5_w_glu': s5_w_glu, 's5_b_glu': s5_b_glu,
        'nsa_pe_k': nsa_pe_k, 'nsa_pe_v': nsa_pe_v,
        'nsa_phi_k1': nsa_phi_k1, 'nsa_phi_k2': nsa_phi_k2, 'nsa_phi_v1': nsa_phi_v1, 'nsa_phi_v2': nsa_phi_v2,
        'conv_w': conv_w, 'w_proj_a': w_proj_a, 'w_proj_b': w_proj_b, 'w_proj_c': w_proj_c, 'w_o': w_o,
    }
    past = {
        'cmp': cache_cmp_kv, 'sel': cache_sel_kv, 'win': cache_win_kv, 'conv': cache_conv,
        's5_re': state_s5_re, 's5_im': state_s5_im, 'page_table': page_table,
    }
    y_prompt, (p_cmp, p_sel, p_win, p_conv, p_re, p_im) = run_trunk(x_prompt, weights, None)
    y_sample, (s_cmp, s_sel, s_win, s_conv, s_re, s_im) = run_trunk(x_sample, weights, past)
    return (y_prompt, y_sample, p_cmp, s_cmp, p_sel, s_sel, p_win, s_win, p_conv, s_conv, p_re, s_re, p_im, s_im)
```

```python
import numpy as np
from contextlib import ExitStack
import concourse.bass as bass
import concourse.mybir as mybir
from concourse.bass_utils import run_bass_kernel_spmd

F32 = mybir.dt.float32
BF16 = mybir.dt.bfloat16
I32 = mybir.dt.int32
AF = mybir.ActivationFunctionType
ALU = mybir.AluOpType

NCORES = 8
D = 1024
T = 4096
DFF = 2816
DIN = 6424
DEPTH = 2
TT = 256
NSUB = TT // 128
LCH = 128
S5X = 1092
NTILE = T // TT
ALPHA = (2.0 * DEPTH) ** 0.25
EPS = 1e-5

DEBUG = {}
FLAGS = {'ffn': True, 'ln': True}


class Sched:
    SEM_ROLL = 30000

    def __init__(self, nc, es):
        self.nc = nc
        self.es = es
        self.engs = {'pe': nc.tensor, 'dve': nc.vector, 'act': nc.scalar, 'pool': nc.gpsimd, 'sp': nc.sync}
        self.cur = {}
        self.known = {e: {} for e in self.engs}
        self.bufs = {}
        self.nsem = 0
        self.nops = 0
        self.nwaits = 0

    def _newsem(self, name):
        self.nsem += 1
        return self.es.enter_context(self.nc.semaphore(f"s{self.nsem}_{name}"))

    def _tick(self, stream, inc):
        c = self.cur.get(stream)
        if c is None or c[1] + inc > self.SEM_ROLL:
            c = [self._newsem(stream), 0]
            self.cur[stream] = c
        c[1] += inc
        return (c[0], c[1])

    def _deps(self, eng, reads, writes, tag=None):
        need = {}

        def add(tok, kind):
            sem, val, teng = tok[0], tok[1], tok[2]
            if teng == eng and eng == 'pe':
                if not (kind == 'waw' and len(tok) > 3 and tok[3] != tag):
                    return
            if val > need.get(id(sem), (None, 0))[1]:
                need[id(sem)] = (sem, val)

        for k in reads:
            b = self.bufs.get(k)
            if b is not None and b[0] is not None:
                add(b[0], 'raw')
            if b is not None and k.startswith('bank'):
                for r in b[1]:
                    add(r, 'war')
        for k in writes:
            b = self.bufs.get(k)
            if b is not None:
                if b[0] is not None:
                    add(b[0], 'waw')
                for r in b[1]:
                    add(r, 'war')
        kn = self.known[eng]
        e = self.engs[eng]
        for sid, (sem, val) in need.items():
            if kn.get(sid, 0) < val:
                e.wait_ge(sem, val)
                kn[sid] = val
                self.nwaits += 1

    def _post(self, tok, reads, writes):
        for k in reads:
            b = self.bufs.setdefault(k, [None, []])
            b[1].append(tok)
            if len(b[1]) > 64:
                b[1] = self._compact(b[1])
        for k in writes:
            self.bufs[k] = [tok, []]
        self.nops += 1

    @staticmethod
    def _compact(toks):
        best = {}
        for t in toks:
            k = (id(t[0]), t[2])
            if k not in best or best[k][1] < t[1]:
                best[k] = t
        return list(best.values())

    def op(self, eng, fn, reads=(), writes=(), tag=None):
        self._deps(eng, reads, writes, tag)
        inst = fn(self.engs[eng])
        sem, val = self._tick(eng, 1)
        inst.then_inc(sem, 1)
        self._post((sem, val, eng, tag), reads, writes)

    def dma(self, q, slot, out, in_, reads=(), writes=(), **kw):
        self._deps(q, reads, writes)
        self._own(q, 'dma_' + slot)
        inst = self.engs[q].dma_start(out=out, in_=in_, **kw)
        sem, val = self._tick('dma_' + slot, 16)
        inst.then_inc(sem, 16)
        self._post((sem, val, 'dma_' + slot), reads, writes)

    def _own(self, q, stream):
        c = self.cur.get(stream)
        if c is not None and c[1] > 0 and self.known[q].get(id(c[0]), 0) < c[1]:
            self.engs[q].wait_ge(c[0], c[1])
            self.known[q][id(c[0])] = c[1]

    def idma(self, slot, out, in_, idx_ap, reads=(), writes=()):
        self._deps('pool', reads, writes)
        self._own('pool', 'dma_' + slot)
        inst = self.nc.gpsimd.indirect_dma_start(out=out, out_offset=None, in_=in_, in_offset=bass.IndirectOffsetOnAxis(ap=idx_ap, axis=0))
        sem, val = self._tick('dma_' + slot, 16)
        inst.then_inc(sem, 16)
        self._post((sem, val, 'dma_' + slot), reads, writes)

    def barrier(self):
        for eng, e in self.engs.items():
            kn = self.known[eng]
            for stream, (sem, cnt) in self.cur.items():
                if cnt > 0 and kn.get(id(sem), 0) < cnt:
                    e.wait_ge(sem, cnt)
                    kn[id(sem)] = cnt
        self.bufs = {k: v for k, v in self.bufs.items() if k.startswith('OUT_')}

    def wait_all(self, eng, keys):
        self._deps(eng, keys, ())


class K:
    pass


def build_program(dbg=None, ntile=NTILE):
    dbg = dbg or {}
    nc = bass.Bass("TRN2", target_bir_lowering=False)
    g = K()
    g.nc = nc

    def din(name, shape, dt=F32):
        return nc.dram_tensor(name, list(shape), dt, kind="ExternalInput").ap()

    def dout(name, shape, dt=F32):
        return nc.dram_tensor(name, list(shape), dt, kind="ExternalOutput").ap()

    xp = din("xp", [T, D])
    ident_d = din("ident", [128, 128])
    onesm_d = din("onesm", [128, 128])
    lng_d = din("lng", [128, 48])
    lnb_d = din("lnb", [128, 48])
    W = {}
    if FLAGS['ffn']:
        W['ffn1_gu'] = din("ffn1_w_gu", [DEPTH, D, 2 * DFF])
        W['ffn1_dn'] = din("ffn1_w_down", [DEPTH, DFF, D])
        W['ffn2_gu'] = din("ffn2_w_gu", [DEPTH, D, 2 * DFF])
        W['ffn2_dn'] = din("ffn2_w_down", [DEPTH, DFF, D])
    if FLAGS.get('win', True):
        W['w_in'] = din("w_in", [DEPTH, D, DIN])
        ropec_d = din("ropec", [128, 32, 8])
        ropes_d = din("ropes", [128, 32, 8])
        s5p_d = din("s5p", [DEPTH, 128, S5X])
        mmask_d = din("mmask", [128, 16, 8])
        iota_d = din("iota128", [128, LCH])
        W['glu'] = din("s5_w_glu", [DEPTH, 512, 512])
    def dscr(name, shape, dt=F32):
        return nc.dram_tensor(name, list(shape), dt, kind="Internal").ap()
    x1_s = dscr("x1_s", [NTILE, 128, 8, TT])
    ya_s = dscr("ya_s", [NTILE, 128, 4, TT], BF16)
    yc_s = dscr("yc_s", [NTILE, 128, 4, TT], BF16)
    yb_s = dscr("yb_s", [NTILE, 128, 4, TT], BF16)
    qg_s = dscr("qg_s", [T, 536])
    win_s = dscr("win_s", [T, 256])
    xmid = dscr("xmid", [T, D])
    phik1_d = din("nsa_phi_k1", [DEPTH, 2048, 128])
    phik2_d = din("nsa_phi_k2", [DEPTH, 128, 64])
    phiv1_d = din("nsa_phi_v1", [DEPTH, 2048, 128])
    phiv2_d = din("nsa_phi_v2", [DEPTH, 128, 64])
    peTk_d = din("peTk", [DEPTH, 64, 32])
    peTv_d = din("peTv", [DEPTH, 64, 32])
    ropeCc_d = din("ropeCc", [32, 8, 8])
    ropeCs_d = din("ropeCs", [32, 8, 8])
    mapc_d = din("mapc", [32, 8, 64])
    wmask_d = din("wmask", [128, 6 * TT])
    cmask_d = din("cmask", [32, 5 * TT])
    ebig_d = din("ebig", [64, T])
    fbias_d = din("fbias", [T, 64])
    W['pa'] = din("w_proj_a", [DEPTH, 512, D])
    W['pb'] = din("w_proj_b", [DEPTH, 512, D])
    W['pc'] = din("w_proj_c", [DEPTH, 512, D])
    W['wo'] = din("w_o", [DEPTH, D, D])
    NPOOL = 5120
    xs_d = din("xs", [4, D])
    ccmp_d = din("cache_cmp", [DEPTH, NPOOL, 128, 256])
    csel_d = din("cache_sel", [DEPTH, NPOOL, 128, 256])
    cwin_d = din("cwin", [DEPTH, 4, 512, 256])
    cconv_d = din("cconv", [DEPTH, 4, 2, 512])
    s5re0_d = din("s5re0", [DEPTH, 4, 16, 128])
    s5im0_d = din("s5im0", [DEPTH, 4, 16, 128])
    ptab_d = din("ptab", [4, 128], I32)
    ropeS_d = din("ropeS", [4, 2, 8])
    lohi_d = din("lohi", [1, 256])
    fbs_d = din("fbias_s", [1, 264])
    maploc_d = din("maploc", [32, 16])
    pcol_d = din("pcol", [128, DEPTH])
    ropeCc2_d = din("ropeCc2", [32, 32, 8])
    ropeCs2_d = din("ropeCs2", [32, 32, 8])
    xmid_s = dscr("xmid_s", [4, D])
    x1s_s = dscr("x1s_s", [128, 8, 4])
    qgs_s = dscr("qgs_s", [4, 536])
    yas_s = dscr("yas_s", [128, 4, 4], BF16)
    ybs_s = dscr("ybs_s", [128, 4, 4], BF16)
    ycs_s = dscr("ycs_s", [128, 4, 4], BF16)
    ybr_s = dscr("ybr_s", [4, 512])
    y_sample = dout("y_sample", [4, D])
    cmp_rows_s = dout("cmp_rows_s", [DEPTH, 4, 256])
    sel_rows_s = dout("sel_rows_s", [DEPTH, 4, 256])
    win_so = dout("win_so", [DEPTH, 4, 512, 256])
    conv_so = dout("conv_so", [DEPTH, 4, 2, 512])
    s5re_so = dout("s5re_so", [DEPTH, 4, 16, 128])
    s5im_so = dout("s5im_so", [DEPTH, 4, 16, 128])
    y_prompt = dout("y_prompt", [T, D])
    cmp_rows_p = dout("cmp_rows_p", [DEPTH, T, 256])
    sel_rows_p = dout("sel_rows_p", [DEPTH, T, 256])
    win_p = dout("win_p", [DEPTH, 512, 256])
    conv_p = dout("conv_p", [DEPTH, 2, 512])
    s5re_p = dout("s5re_p", [DEPTH, 16, 128])
    s5im_p = dout("s5im_p", [DEPTH, 16, 128])
    dbg_out = {}
    for name, shape in dbg.items():
        dbg_out[name] = dout("dbg_" + name, shape)

    with ExitStack() as es:
        cur = [es]

        def sb(name, shape, dt=F32):
            return cur[0].enter_context(nc.sbuf_tensor("sb_" + name, list(shape), dt))

        def pst(name, shape, dt=F32):
            return es.enter_context(nc.psum_tensor("ps_" + name, list(shape), dt))

        ident = sb("ident", [128, 128])
        identb = sb("identb", [128, 128], BF16)
        onesm = sb("onesm", [128, 128])
        lng = sb("lng", [128, 48])
        lnb = sb("lnb", [128, 48])
        ropec = sb("ropec", [128, 32, 8])
        ropes = sb("ropes", [128, 32, 8])
        mmask = sb("mmask", [128, 16, 8])
        iota = sb("iota", [128, LCH])
        banks = [pst(f"bank{i}", [128, 512]) for i in range(8)]
        NSLOT = FLAGS.get('nslot', 3)
        NBANK = FLAGS.get('nbank', 7)
        xtok = xA = xAb = rS = hb = sg = lnm = lnv = lnr = wslot = big1 = st_big = big2 = bigi = sq = None
        tokq = rt = s5p = BTr = BTi = CTr = CTi = cosT = sinT = sLneg = mag = s5t = cre = cim = ctmp = None
        u32 = ub = bpr = bpi = rr = ri = st1 = st2 = srb = sib = yag = yagb = ya2b = cb32 = cc32 = vbuf = ctm = ycb = smallo = None
        uid = [0]

        SA = K()

        def alloc_ffn(stk):
            nonlocal xtok, xA, xAb, rS, hb, sg, lnm, lnv, lnr, wslot, big1, st_big, big2, bigi, sq
            uid[0] += 1
            u_ = f"_{uid[0]}"
            cur[0] = stk
            xtok = sb("xtok" + u_, [128, NSUB, D])
            xA = sb("xA" + u_, [128, 8, TT])
            xAb = sb("xAb" + u_, [128, 8, TT], BF16)
            rS = sb("rS" + u_, [128, 8, TT])
            hb = sb("hb" + u_, [128, 22, TT], BF16)
            sg = [sb(f"sg{i}" + u_, [128, TT]) for i in range(2)]
            lnm = sb("lnm" + u_, [128, TT])
            lnv = sb("lnv" + u_, [128, TT])
            lnr = sb("lnr" + u_, [128, TT])
            wslot = [sb(f"wslot{i}" + u_, [128, 8192], BF16) for i in range(NSLOT)]
            big1 = wslot[0][:, :].bitcast(F32)[:, 0:2048]
            st_big = wslot[0][:, :].bitcast(F32)[:, 2048:4096]
            big2 = wslot[1][:, :].bitcast(F32)[:, 0:2048]
            bigi = wslot[2][:, :].bitcast(I32)[:, 0:2048]
            sq = hb[:, 0:16, :].rearrange("p a b -> p (a b)").bitcast(F32).rearrange("p (k n) -> p k n", k=8)

        def alloc_A(stk):
            nonlocal tokq, rt, s5p, BTr, BTi, CTr, CTi, cosT, sinT, sLneg, mag, s5t, cre, cim, ctmp
            nonlocal u32, ub, bpr, bpi, rr, ri, st1, st2, srb, sib, yag, yagb, ya2b, cb32, cc32, vbuf, ctm, ycb, smallo
            uid[0] += 1
            u_ = f"_{uid[0]}"
            cur[0] = stk
            tokq = sb("tokq" + u_, [128, NSUB, 1304])
            rt = [sb(f"rt{i}" + u_, [128, NSUB, 8, 8]) for i in range(4)]
            s5p = sb("s5p" + u_, [128, S5X])
            BTr = sb("BTr" + u_, [128, 16, 128], BF16)
            BTi = sb("BTi" + u_, [128, 16, 128], BF16)
            CTr = sb("CTr" + u_, [128, 16, 128], BF16)
            CTi = sb("CTi" + u_, [128, 16, 128], BF16)
            cosT = sb("cosT" + u_, [128, 16, LCH])
            sinT = sb("sinT" + u_, [128, 16, LCH])
            sLneg = sb("sLneg" + u_, [128, 16])
            mag = sb("mag" + u_, [128, 16])
            s5t = [sb(f"s5t{i}" + u_, [128, 16]) for i in range(8)]
            cre = sb("cre" + u_, [128, 16])
            cim = sb("cim" + u_, [128, 16])
            ctmp = sb("ctmp" + u_, [128, 2])
            u32 = sb("u32" + u_, [128, 4, TT])
            ub = sb("ub" + u_, [128, 4, TT], BF16)
            bpr = sb("bpr" + u_, [128, TT])
            bpi = sb("bpi" + u_, [128, TT])
            rr = sb("rr" + u_, [128, TT])
            ri = sb("ri" + u_, [128, TT])
            st1 = sb("st1" + u_, [128, TT])
            st2 = sb("st2" + u_, [128, TT])
            srb = sb("srb" + u_, [128, TT], BF16)
            sib = sb("sib" + u_, [128, TT], BF16)
            yag = sb("yag" + u_, [128, 4, TT])
            yagb = sb("yagb" + u_, [128, 4, TT], BF16)
            ya2b = sb("ya2b" + u_, [128, 4, TT], BF16)
            cb32 = sb("cb32" + u_, [128, 4, TT])
            cc32 = sb("cc32" + u_, [128, 4, TT])
            vbuf = sb("vbuf" + u_, [128, 4, TT + 2])
            ctm = sb("ctm" + u_, [128, TT])
            ycb = sb("ycb" + u_, [128, 4, TT], BF16)
            smallo = sb("smallo" + u_, [16, 512])
            SA.ropeS = sb("ropeS" + u_, [4, 2, 8])
            SA.s0t = sb("s0t" + u_, [16, 2, 4, 128])
            SA.s0 = sb("s0" + u_, [128, 2, 16, 4])
            SA.sn = sb("sn" + u_, [128, 2, 16, 4])
            SA.snb = sb("snb" + u_, [128, 2, 16, 4], BF16)
            SA.t = [sb(f"sat{i}" + u_, [128, 16, 4]) for i in range(2)]
            SA.cvt = sb("cvt" + u_, [8, 512])
            SA.cbuf = sb("cbuf" + u_, [128, 4, 4, 2])
            SA.vn = sb("vn" + u_, [128, 4, 4])

        es.enter_context(nc.Block())
        S = Sched(nc, es)
        g.S = S
        bank_i = [0]

        def bank():
            i = bank_i[0] % NBANK
            bank_i[0] += 1
            return banks[i], f"bank{i}"

        YB = banks[7]
        YBK = 'bank7'

        S.dma('sp', 'c0', ident[:], ident_d[:, :], writes=['ident'])
        S.dma('sp', 'c1', onesm[:], onesm_d[:, :], writes=['onesm'])
        S.dma('sp', 'c2', lng[:], lng_d[:, :], writes=['lng'])
        S.dma('sp', 'c3', lnb[:], lnb_d[:, :], writes=['lnb'])
        S.op('dve', lambda e: e.tensor_copy(out=identb[:], in_=ident[:]), reads=['ident'], writes=['identb'])
        if FLAGS.get('win', True):
            S.dma('sp', 'c4', ropec[:], ropec_d[:, :, :], writes=['ropec'])
            S.dma('sp', 'c5', ropes[:], ropes_d[:, :, :], writes=['ropes'])
            S.dma('sp', 'c6', mmask[:], mmask_d[:, :, :], writes=['mmask'])
            S.dma('sp', 'c7', iota[:], iota_d[:, :], writes=['iota'])

        wq = []
        wstate = {'issued': 0, 'used': 0}

        def w_issue(spec):
            Wap, k0, nk, c0, ncols = spec
            s = wstate['issued'] % NSLOT
            wstate['issued'] += 1
            view = wslot[s][:, 0:nk * ncols].rearrange("p (k c) -> p k c", k=nk)
            src = Wap[k0 * 128:(k0 + nk) * 128, c0:c0 + ncols].rearrange("(k p) c -> p k c", p=128)
            S.dma('pool', f'w{s}', view, src, writes=[f'wslot{s}'])

        def run_tasks(tasks):
            norm = []
            for (spec, fn) in tasks:
                if spec is None:
                    norm.append(([], fn, 0))
                elif isinstance(spec, list):
                    norm.append((spec, fn, 2))
                else:
                    norm.append(([spec], fn, 1))
            wl = [sp for (specs, _, _) in norm for sp in specs]
            first = 0
            issued = 0
            for (specs, fn, mode) in norm:
                if mode == 0:
                    fn()
                    continue
                assert len(specs) <= NSLOT
                while issued < len(wl) and issued < first + NSLOT:
                    w_issue(wl[issued])
                    issued += 1
                views, keys = [], []
                for sp in specs:
                    s = wstate['used'] % NSLOT
                    wstate['used'] += 1
                    Wap, k0, nk, c0, ncols = sp
                    views.append(wslot[s][:, 0:nk * ncols].rearrange("p (k c) -> p k c", k=nk))
                    keys.append(f'wslot{s}')
                if mode == 1:
                    fn(views[0], keys[0])
                else:
                    fn(views, keys)
                first += len(specs)
            assert wstate['used'] == wstate['issued']

        def load_x_tile(tasks, src_ap, N, subs, rkeys=()):
            def fn():
                for si, (r0, nr) in enumerate(subs):
                    S.dma('sp', f'xin{si}', xtok[0:nr, si, :], src_ap[r0:r0 + nr, :], reads=list(rkeys), writes=[f'xtok{si}'])
                for k in range(8):
                    bk, bkey = bank()
                    for si, (r0, nr) in enumerate(subs):
                        S.op('pe', lambda e, si=si, nr=nr, k=k, bk=bk: e.transpose(
                            out=bk[:, si * 128:si * 128 + nr], in_=xtok[0:nr, si, k * 128:(k + 1) * 128], identity=ident[0:nr, 0:nr]),
                            reads=[f'xtok{si}', 'ident'], writes=[bkey])
                    S.op('act', lambda e, k=k, bk=bk: e.copy(out=xA[:, k, 0:N], in_=bk[:, 0:N]), reads=[bkey], writes=[f'xA{k}'])
                    S.op('dve', lambda e, k=k, bk=bk: e.tensor_copy(out=xAb[:, k, 0:N], in_=bk[:, 0:N]), reads=[bkey], writes=[f'xAb{k}'])
            tasks.append((None, fn))

        def store_x_tile(tasks, dst_ap, N, subs, oname):
            def fn():
                for si, (r0, nr) in enumerate(subs):
                    for k in range(8):
                        if k % 4 == 0:
                            bk, bkey = bank()
                        S.op('pe', lambda e, si=si, nr=nr, k=k, bk=bk: e.transpose(
                            out=bk[0:nr, (k % 4) * 128:(k % 4 + 1) * 128], in_=xA[:, k, si * 128:si * 128 + nr], identity=ident[:, :]),
                            reads=[f'xA{k}', 'ident'], writes=[bkey])
                        if k % 4 == 3:
                            h0 = (k // 4) * 512
                            eng = 'act' if (k // 4) == 0 else 'dve'
                            if eng == 'act':
                                S.op('act', lambda e, si=si, nr=nr, h0=h0, bk=bk: e.copy(out=xtok[0:nr, si, h0:h0 + 512], in_=bk[0:nr, :]),
                                     reads=[bkey], writes=[f'xtok{si}'])
                            else:
                                S.op('dve', lambda e, si=si, nr=nr, h0=h0, bk=bk: e.tensor_copy(out=xtok[0:nr, si, h0:h0 + 512], in_=bk[0:nr, :]),
                                     reads=[bkey], writes=[f'xtok{si}'])
                    S.dma('sp', f'xout{si}', dst_ap[r0:r0 + nr, :], xtok[0:nr, si, :],
                          reads=[f'xtok{si}'], writes=['OUT_' + oname])
            tasks.append((None, fn))

        def layernorm(tasks, N, l, i):
            def fn():
                col = (l * 3 + i) * 8
                S.op('act', lambda e: e.activation(out=sq[:, :, 0:N], in_=rS[:, :, 0:N], func=AF.Square),
                     reads=[f'rS{k}' for k in range(8)], writes=['sq'])
                b1, b1k = bank()
                for k in range(8):
                    S.op('pe', lambda e, k=k: e.matmul(b1[:, 0:N], lhsT=onesm[:], rhs=rS[:, k, 0:N], start=(k == 0), stop=(k == 7)),
                         reads=[f'rS{k}', 'onesm'], writes=[b1k])
                b2, b2k = bank()
                for k in range(8):
                    S.op('pe', lambda e, k=k: e.matmul(b2[:, 0:N], lhsT=onesm[:], rhs=sq[:, k, 0:N], start=(k == 0), stop=(k == 7)),
                         reads=['sq', 'onesm'], writes=[b2k])
                S.op('act', lambda e: e.copy(out=lnm[:, 0:N], in_=b1[:, 0:N]), reads=[b1k], writes=['lnm'])
                S.op('dve', lambda e: e.tensor_tensor(out=lnv[:, 0:N], in0=lnm[:, 0:N], in1=lnm[:, 0:N], op=ALU.mult), reads=['lnm'], writes=['lnv'])
                S.op('dve', lambda e: e.tensor_tensor(out=lnv[:, 0:N], in0=b2[:, 0:N], in1=lnv[:, 0:N], op=ALU.subtract), reads=[b2k, 'lnv'], writes=['lnv'])
                S.op('dve', lambda e: e.tensor_scalar(out=lnv[:, 0:N], in0=lnv[:, 0:N], scalar1=0.0, scalar2=EPS, op0=ALU.max, op1=ALU.add),
                     reads=['lnv'], writes=['lnv'])
                S.op('act', lambda e: e.activation(out=lnr[:, 0:N], in_=lnv[:, 0:N], func=AF.Sqrt), reads=['lnv'], writes=['lnr'])
                S.op('dve', lambda e: e.reciprocal(out=lnr[:, 0:N], in_=lnr[:, 0:N]), reads=['lnr'], writes=['lnr'])
                for k in range(8):
                    S.op('dve', lambda e, k=k: e.tensor_tensor(out=rS[:, k, 0:N], in0=rS[:, k, 0:N], in1=lnm[:, 0:N], op=ALU.subtract),
                         reads=[f'rS{k}', 'lnm'], writes=[f'rS{k}'])
                    S.op('pool', lambda e, k=k: e.tensor_tensor(out=rS[:, k, 0:N], in0=rS[:, k, 0:N], in1=lnr[:, 0:N], op=ALU.mult),
                         reads=[f'rS{k}', 'lnr'], writes=[f'rS{k}'])
                    S.op('act', lambda e, k=k: e.activation(out=xA[:, k, 0:N], in_=rS[:, k, 0:N], func=AF.Identity,
                                                            scale=lng[:, col + k:col + k + 1], bias=lnb[:, col + k:col + k + 1]),
                         reads=[f'rS{k}', 'lng', 'lnb'], writes=[f'xA{k}'])
                    S.op('dve', lambda e, k=k: e.tensor_copy(out=xAb[:, k, 0:N], in_=xA[:, k, 0:N]), reads=[f'xA{k}'], writes=[f'xAb{k}'])
            tasks.append((None, fn))

        def ffn(tasks, N, Wgu, Wdn):
            blocks = [(0, 1024), (1024, 1024), (2048, 768)]
            for (c0, ncols) in blocks:
                def fgu(views, keys, c0=c0, ncols=ncols):
                    gv, view = views
                    gk, key = keys
                    for m in range(ncols // 128):
                        j = c0 // 128 + m
                        bg, bgk = bank()
                        for k in range(8):
                            S.op('pe', lambda e, k=k, m=m: e.matmul(bg[:, 0:N], lhsT=gv[:, k, m * 128:(m + 1) * 128], rhs=xAb[:, k, 0:N],
                                                                     start=(k == 0), stop=(k == 7)), reads=[gk, f'xAb{k}'], writes=[bgk])
                        bu, buk = bank()
                        for k in range(8):
                            S.op('pe', lambda e, k=k, m=m: e.matmul(bu[:, 0:N], lhsT=view[:, k, m * 128:(m + 1) * 128], rhs=xAb[:, k, 0:N],
                                                                     start=(k == 0), stop=(k == 7)), reads=[key, f'xAb{k}'], writes=[buk])
                        sgi = j % 2
                        S.op('act', lambda e, sgi=sgi: e.activation(out=sg[sgi][:, 0:N], in_=bg[:, 0:N], func=AF.Silu), reads=[bgk], writes=[f'sg{sgi}'])
                        S.op('dve', lambda e, sgi=sgi, j=j: e.tensor_tensor(out=hb[:, j, 0:N], in0=bu[:, 0:N], in1=sg[sgi][:, 0:N], op=ALU.mult),
                             reads=[buk, f'sg{sgi}'], writes=[f'hb{j}'])
                tasks.append(([(Wgu, 0, 8, c0, ncols), (Wgu, 0, 8, DFF + c0, ncols)], fgu))
            for cb in range(4):
                def fd(view, key, cb=cb):
                    for m in range(2):
                        kk = cb * 2 + m
                        bd, bdk = bank()
                        for k in range(22):
                            S.op('pe', lambda e, k=k, m=m: e.matmul(bd[:, 0:N], lhsT=view[:, k, m * 128:(m + 1) * 128], rhs=hb[:, k, 0:N],
                                                                     start=(k == 0), stop=(k == 21)), reads=[key, f'hb{k}'], writes=[bdk])
                        S.op('act', lambda e, kk=kk: e.activation(out=rS[:, kk, 0:N], in_=xA[:, kk, 0:N], func=AF.Copy, scale=ALPHA),
                             reads=[f'xA{kk}'], writes=[f'rS{kk}'])
                        S.op('dve', lambda e, kk=kk: e.scalar_tensor_tensor(out=rS[:, kk, 0:N], in0=bd[:, 0:N], scalar=0.5, in1=rS[:, kk, 0:N],
                                                                              op0=ALU.mult, op1=ALU.add), reads=[bdk, f'rS{kk}'], writes=[f'rS{kk}'])
                tasks.append(((Wdn, 0, 22, cb * 256, 256), fd))

        def win_tokmajor(tasks, l, j):
            Win = W['w_in'][l]
            tkeys = [f'tokq{s_}' for s_ in range(NSUB)]

            def proj(view, key, c_lo, pieces):
                for sub in range(NSUB):
                    for (p0, pn) in pieces:
                        bk, bkey = bank()
                        for k in range(8):
                            S.op('pe', lambda e, k=k, sub=sub, p0=p0, pn=pn, bk=bk: e.matmul(
                                bk[:, 0:pn], lhsT=xAb[:, k, sub * 128:(sub + 1) * 128], rhs=view[:, k, p0:p0 + pn],
                                start=(k == 0), stop=(k == 7)), reads=[key, f'xAb{k}'], writes=[bkey])
                        S.op('act', lambda e, sub=sub, p0=p0, pn=pn, bk=bk: e.copy(out=tokq[:, sub, c_lo + p0:c_lo + p0 + pn], in_=bk[:, 0:pn]),
                             reads=[bkey], writes=[f'tokq{sub}'])
            tasks.append(((Win, 0, 8, 512, 1024), lambda view, key: proj(view, key, 0, [(0, 512), (512, 512)])))
            tasks.append(((Win, 0, 8, 1536, 280), lambda view, key: proj(view, key, 1024, [(0, 280)])))

            def rope_and_store():
                t0 = j * TT
                S.dma('sp', 'ocmp', cmp_rows_p[l, t0:t0 + TT, :].rearrange("(s p) c -> p s c", p=128), tokq[:, :, 512:768],
                      reads=tkeys, writes=['OUT_cmp_p'])
                cosb = lambda nh: ropec[:, j * NSUB:(j + 1) * NSUB, :].unsqueeze(2).broadcast_to([128, NSUB, nh, 8])
                sinb = lambda nh: ropes[:, j * NSUB:(j + 1) * NSUB, :].unsqueeze(2).broadcast_to([128, NSUB, nh, 8])
                for (c0, nh) in [(0, 8), (768, 2), (1024, 2)]:
                    V = tokq[:, :, c0:c0 + nh * 64].rearrange("p s (h d) -> p s h d", h=nh)
                    x1 = V[:, :, :, 0:8]
                    x2 = V[:, :, :, 8:16]
                    tm = [rt[i][:, :, 0:nh, :] for i in range(4)]
                    S.op('dve', lambda e: e.tensor_tensor(out=tm[0], in0=x1, in1=cosb(nh), op=ALU.mult), reads=tkeys + ['ropec'], writes=['rt0'])
                    S.op('dve', lambda e: e.tensor_tensor(out=tm[1], in0=x2, in1=sinb(nh), op=ALU.mult), reads=tkeys + ['ropes'], writes=['rt1'])
                    S.op('dve', lambda e: e.tensor_tensor(out=tm[2], in0=x2, in1=cosb(nh), op=ALU.mult), reads=tkeys + ['ropec'], writes=['rt2'])
                    S.op('dve', lambda e: e.tensor_tensor(out=tm[3], in0=x1, in1=sinb(nh), op=ALU.mult), reads=tkeys + ['ropes'], writes=['rt3'])
                    S.op('dve', lambda e: e.tensor_tensor(out=x1, in0=tm[0], in1=tm[1], op=ALU.subtract), reads=['rt0', 'rt1'], writes=tkeys)
                    S.op('dve', lambda e: e.tensor_tensor(out=x2, in0=tm[2], in1=tm[3], op=ALU.add), reads=['rt2', 'rt3'], writes=tkeys)
                S.dma('sp', 'osel', sel_rows_p[l, t0:t0 + TT, :].rearrange("(s p) c -> p s c", p=128), tokq[:, :, 768:1024],
                      reads=tkeys, writes=['OUT_sel_p'])
                nwt = 512 // TT
                if j >= NTILE - nwt:
                    w0 = (j - (NTILE - nwt)) * TT
                    S.dma('sp', 'owin', win_p[l, w0:w0 + TT, :].rearrange("(s p) c -> p s c", p=128), tokq[:, :, 1024:1280],
                          reads=tkeys, writes=['OUT_win_p'])
            tasks.append((None, rope_and_store))

        PI = 3.14159265358979

        def sincos(out_ap, th_ap, shift, F, keys_in, key_out):
            t1 = big1[:, 0:F]
            t2 = big2[:, 0:F]
            ti = bigi[:, 0:F]
            S.op('dve', lambda e: e.tensor_scalar(out=t1, in0=th_ap, scalar1=1.0 / (2 * PI), scalar2=0.5 + shift / (2 * PI), op0=ALU.mult, op1=ALU.add),
                 reads=keys_in, writes=['wslot0'])
            S.op('dve', lambda e: e.tensor_copy(out=ti, in_=t1), reads=['wslot0'], writes=['wslot2'])
            S.op('dve', lambda e: e.tensor_copy(out=t1, in_=ti), reads=['wslot2'], writes=['wslot0'])
            S.op('dve', lambda e: e.scalar_tensor_tensor(out=t1, in0=t1, scalar=-2 * PI, in1=th_ap, op0=ALU.mult, op1=ALU.add),
                 reads=['wslot0'] + keys_in, writes=['wslot0'])
            if shift != 0.0:
                S.op('dve', lambda e: e.tensor_scalar(out=t1, in0=t1, scalar1=shift, scalar2=None, op0=ALU.add), reads=['wslot0'], writes=['wslot0'])
            S.op('dve', lambda e: e.tensor_scalar(out=t2, in0=t1, scalar1=-PI, scalar2=2 * PI, op0=ALU.is_lt, op1=ALU.mult), reads=['wslot0'], writes=['wslot1'])
            S.op('dve', lambda e: e.tensor_tensor(out=t1, in0=t1, in1=t2, op=ALU.add), reads=['wslot0', 'wslot1'], writes=['wslot0'])
            S.op('dve', lambda e: e.tensor_scalar(out=t2, in0=t1, scalar1=PI, scalar2=-2 * PI, op0=ALU.is_gt, op1=ALU.mult), reads=['wslot0'], writes=['wslot1'])
            S.op('dve', lambda e: e.tensor_tensor(out=t1, in0=t1, in1=t2, op=ALU.add), reads=['wslot0', 'wslot1'], writes=['wslot0'])
            S.op('dve', lambda e: e.tensor_scalar(out=t1, in0=t1, scalar1=-3.1415925, scalar2=3.1415925, op0=ALU.max, op1=ALU.min), reads=['wslot0'], writes=['wslot0'])
            S.op('act', lambda e: e.activation(out=out_ap, in_=t1, func=AF.Sin), reads=['wslot0'], writes=[key_out])

        def s5_setup(l):
            S.dma('sp', 'c8', s5p[:], s5p_d[l, :, :], writes=['s5p'])
            lr = s5p[:, 0:16]
            li = s5p[:, 16:32]
            ldt = s5p[:, 32:48]
            bre = s5p[:, 48:304].rearrange("p (s i) -> p s i", s=16)
            bim = s5p[:, 304:560].rearrange("p (s i) -> p s i", s=16)
            cre_ = s5p[:, 560:816].rearrange("p (s i) -> p s i", s=16)
            cim_ = s5p[:, 816:1072].rearrange("p (s i) -> p s i", s=16)
            dt, th, sn, cs, are, aim, t6, t7 = [t[:, :] for t in s5t]
            tk = [f's5t{i}' for i in range(8)]
            S.op('act', lambda e: e.activation(out=dt, in_=ldt, func=AF.Exp), reads=['s5p'], writes=[tk[0]])
            S.op('dve', lambda e: e.tensor_tensor(out=th, in0=li, in1=dt, op=ALU.mult), reads=['s5p', tk[0]], writes=[tk[1]])
            S.op('dve', lambda e: e.tensor_tensor(out=t6, in0=lr, in1=dt, op=ALU.mult), reads=['s5p', tk[0]], writes=[tk[6]])
            S.op('act', lambda e: e.activation(out=mag[:, :], in_=t6, func=AF.Exp), reads=[tk[6]], writes=['mag'])
            sincos(sn, th, 0.0, 16, [tk[1]], tk[2])
            sincos(cs, th, PI / 2, 16, [tk[1]], tk[3])
            S.op('dve', lambda e: e.tensor_tensor(out=are, in0=mag[:, :], in1=cs, op=ALU.mult), reads=['mag', tk[3]], writes=[tk[4]])
            S.op('dve', lambda e: e.tensor_tensor(out=aim, in0=mag[:, :], in1=sn, op=ALU.mult), reads=['mag', tk[2]], writes=[tk[5]])
            S.op('dve', lambda e: e.tensor_tensor(out=t6, in0=lr, in1=lr, op=ALU.mult), reads=['s5p'], writes=[tk[6]])
            S.op('dve', lambda e: e.tensor_tensor(out=t7, in0=li, in1=li, op=ALU.mult), reads=['s5p'], writes=[tk[7]])
            S.op('dve', lambda e: e.tensor_tensor(out=t6, in0=t6, in1=t7, op=ALU.add), reads=[tk[6], tk[7]], writes=[tk[6]])
            S.op('dve', lambda e: e.reciprocal(out=dt, in_=t6), reads=[tk[6]], writes=[tk[0]])
            S.op('dve', lambda e: e.tensor_scalar(out=sn, in0=are, scalar1=-1.0, scalar2=None, op0=ALU.add), reads=[tk[4]], writes=[tk[2]])
            S.op('dve', lambda e: e.tensor_tensor(out=t6, in0=sn, in1=lr, op=ALU.mult), reads=[tk[2], 's5p'], writes=[tk[6]])
            S.op('dve', lambda e: e.tensor_tensor(out=cs, in0=aim, in1=li, op=ALU.mult), reads=[tk[5], 's5p'], writes=[tk[3]])
            S.op('dve', lambda e: e.tensor_tensor(out=t6, in0=t6, in1=cs, op=ALU.add), reads=[tk[6], tk[3]], writes=[tk[6]])
            S.op('dve', lambda e: e.tensor_tensor(out=t6, in0=t6, in1=dt, op=ALU.mult), reads=[tk[6], tk[0]], writes=[tk[6]])
            S.op('dve', lambda e: e.tensor_tensor(out=t7, in0=aim, in1=lr, op=ALU.mult), reads=[tk[5], 's5p'], writes=[tk[7]])
            S.op('dve', lambda e: e.tensor_tensor(out=cs, in0=sn, in1=li, op=ALU.mult), reads=[tk[2], 's5p'], writes=[tk[3]])
            S.op('dve', lambda e: e.tensor_tensor(out=t7, in0=t7, in1=cs, op=ALU.subtract), reads=[tk[7], tk[3]], writes=[tk[7]])
            S.op('dve', lambda e: e.tensor_tensor(out=t7, in0=t7, in1=dt, op=ALU.mult), reads=[tk[7], tk[0]], writes=[tk[7]])
            rre_b = t6.unsqueeze(2).broadcast_to([128, 16, 16])
            rim_b = t7.unsqueeze(2).broadcast_to([128, 16, 16])
            bbr = bpr[:, 0:256].rearrange("p (s i) -> p s i", s=16)
            bbi = bpi[:, 0:256].rearrange("p (s i) -> p s i", s=16)
            tA = rr[:, 0:256].rearrange("p (s i) -> p s i", s=16)
            S.op('dve', lambda e: e.tensor_tensor(out=bbr, in0=bre, in1=rre_b, op=ALU.mult), reads=['s5p', tk[6]], writes=['bpr'])
            S.op('dve', lambda e: e.tensor_tensor(out=tA, in0=bim, in1=rim_b, op=ALU.mult), reads=['s5p', tk[7]], writes=['rr'])
            S.op('dve', lambda e: e.tensor_tensor(out=bbr, in0=bbr, in1=tA, op=ALU.subtract), reads=['bpr', 'rr'], writes=['bpr'])
            S.op('dve', lambda e: e.tensor_tensor(out=bbi, in0=bim, in1=rre_b, op=ALU.mult), reads=['s5p', tk[6]], writes=['bpi'])
            S.op('dve', lambda e: e.tensor_tensor(out=tA, in0=bre, in1=rim_b, op=ALU.mult), reads=['s5p', tk[7], 'bpr'], writes=['rr'])
            S.op('dve', lambda e: e.tensor_tensor(out=bbi, in0=bbi, in1=tA, op=ALU.add), reads=['bpi', 'rr'], writes=['bpi'])
            mk4 = mmask[:, :, :].unsqueeze(3).broadcast_to([128, 16, 8, 16])
            for (bb, BT, nm) in [(bbr, BTr, 'BTr'), (bbi, BTi, 'BTi')]:
                Z = big1[:, :].rearrange("p (s g i) -> p s g i", s=16, g=8)
                S.op('dve', lambda e, bb=bb: e.tensor_tensor(out=Z, in0=bb.unsqueeze(2).broadcast_to([128, 16, 8, 16]), in1=mk4, op=ALU.mult),
                     reads=['bpr', 'bpi', 'mmask'], writes=['wslot0'])
                for q4 in range(4):
                    bk, bkey = bank()
                    for a in range(4):
                        sbi = q4 * 4 + a
                        S.op('pe', lambda e, sbi=sbi, a=a, bk=bk: e.transpose(out=bk[:, a * 128:(a + 1) * 128], in_=big1[:, sbi * 128:(sbi + 1) * 128], identity=ident[:, :]),
                             reads=['wslot0', 'ident'], writes=[bkey])
                    S.op('act', lambda e, q4=q4, bk=bk, BT=BT: e.copy(out=BT[:, q4 * 4:(q4 + 1) * 4, :], in_=bk[:, :].rearrange("p (a c) -> p a c", a=4)),
                         reads=[bkey], writes=[nm])
            CZr = CTr[:, :, :].rearrange("p s (g i) -> p s g i", g=8)
            CZi = CTi[:, :, :].rearrange("p s (g i) -> p s g i", g=8)
            S.op('dve', lambda e: e.tensor_tensor(out=CZr, in0=cre_.unsqueeze(2).broadcast_to([128, 16, 8, 16]), in1=mk4, op=ALU.mult),
                 reads=['s5p', 'mmask'], writes=['CTr'])
            S.op('dve', lambda e: e.tensor_tensor(out=CZi, in0=cim_.unsqueeze(2).broadcast_to([128, 16, 8, 16]), in1=mk4, op=ALU.mult),
                 reads=['s5p', 'mmask'], writes=['CTi'])
            CTi2 = CTi[:, :, :].rearrange("p s c -> p (s c)")
            S.op('dve', lambda e: e.tensor_scalar(out=CTi2, in0=CTi2, scalar1=-1.0, scalar2=None, op0=ALU.mult), reads=['CTi'], writes=['CTi'])
            ph = st_big[:, :].rearrange("p (s t) -> p s t", s=16)
            S.op('dve', lambda e: e.tensor_tensor(out=ph, in0=th.unsqueeze(2).broadcast_to([128, 16, LCH]), in1=iota[:, :].unsqueeze(1).broadcast_to([128, 16, LCH]), op=ALU.mult),
                 reads=[tk[1], 'iota'], writes=['wslot0'])
            sincos(sinT[:, :, :].rearrange("p s t -> p (s t)"), st_big[:, :], 0.0, 16 * LCH, ['wslot0'], 'sinT')
            sincos(cosT[:, :, :].rearrange("p s t -> p (s t)"), st_big[:, :], PI / 2, 16 * LCH, ['wslot0'], 'cosT')
            S.op('act', lambda e: e.activation(out=sLneg[:, :], in_=sinT[:, :, LCH - 1], func=AF.Copy, scale=-1.0), reads=['sinT'], writes=['sLneg'])
            S.op('dve', lambda e: e.memset(cre[:, :], 0.0), writes=['cre'])
            S.op('dve', lambda e: e.memset(cim[:, :], 0.0), writes=['cim'])
            S.op('dve', lambda e: e.memset(vbuf[:, :, 0:2], 0.0), writes=['vbuf'])

        def s5_tile(tasks, l, j, N):
            nch = N // LCH
            Win = W['w_in'][l]
            dcol = lambda c: s5p[:, 1072 + c:1073 + c]
            bglu = lambda c: s5p[:, 1076 + c:1077 + c]

            def fu(view, key):
                for c in range(4):
                    bk, bkey = bank()
                    for k in range(8):
                        S.op('pe', lambda e, k=k, c=c, bk=bk: e.matmul(bk[:, 0:N], lhsT=view[:, k, c * 128:(c + 1) * 128], rhs=xAb[:, k, 0:N],
                                                                        start=(k == 0), stop=(k == 7)), reads=[key, f'xAb{k}'], writes=[bkey])
                    S.op('act', lambda e, c=c, bk=bk: e.copy(out=u32[:, c, 0:N], in_=bk[:, 0:N]), reads=[bkey], writes=[f'u32_{c}'])
                    S.op('dve', lambda e, c=c: e.tensor_copy(out=ub[:, c, 0:N], in_=u32[:, c, 0:N]), reads=[f'u32_{c}'], writes=[f'ub{c}'])
            tasks.append(((Win, 0, 8, 0, 512), fu))

            def scan_all():
                cosB = lambda sbi: cosT[:, sbi, :].unsqueeze(1).broadcast_to([128, nch, LCH])
                sinB = lambda sbi: sinT[:, sbi, :].unsqueeze(1).broadcast_to([128, nch, LCH])
                v3 = lambda t_: t_[:, 0:N].rearrange("p (c t) -> p c t", c=nch)
                for c in range(4):
                    for a in range(4):
                        sbi = c * 4 + a
                        bA, bAk = bank()
                        S.op('pe', lambda e, sbi=sbi, c=c, bA=bA: e.matmul(bA[:, 0:N], lhsT=BTr[:, sbi, :], rhs=ub[:, c, 0:N], start=True, stop=True),
                             reads=['BTr', f'ub{c}'], writes=[bAk])
                        bB, bBk = bank()
                        S.op('pe', lambda e, sbi=sbi, c=c, bB=bB: e.matmul(bB[:, 0:N], lhsT=BTi[:, sbi, :], rhs=ub[:, c, 0:N], start=True, stop=True),
                             reads=['BTi', f'ub{c}'], writes=[bBk])
                        A4 = bA[:, 0:N].rearrange("p (c t) -> p c t", c=nch)
                        B4 = bB[:, 0:N].rearrange("p (c t) -> p c t", c=nch)
                        S.op('dve', lambda e, sbi=sbi: e.tensor_tensor(out=v3(st1), in0=A4, in1=cosB(sbi), op=ALU.mult), reads=[bAk, 'cosT'], writes=['st1'])
                        S.op('dve', lambda e, sbi=sbi: e.tensor_tensor(out=v3(st2), in0=B4, in1=sinB(sbi), op=ALU.mult), reads=[bBk, 'sinT'], writes=['st2'])
                        S.op('dve', lambda e: e.tensor_tensor(out=bpr[:, 0:N], in0=st1[:, 0:N], in1=st2[:, 0:N], op=ALU.add), reads=['st1', 'st2'], writes=['bpr'])
                        S.op('dve', lambda e, sbi=sbi: e.tensor_tensor(out=v3(st1), in0=B4, in1=cosB(sbi), op=ALU.mult), reads=[bBk, 'cosT'], writes=['st1'])
                        S.op('dve', lambda e, sbi=sbi: e.tensor_tensor(out=v3(st2), in0=A4, in1=sinB(sbi), op=ALU.mult), reads=[bAk, 'sinT'], writes=['st2'])
                        S.op('dve', lambda e: e.tensor_tensor(out=bpi[:, 0:N], in0=st1[:, 0:N], in1=st2[:, 0:N], op=ALU.subtract), reads=['st1', 'st2'], writes=['bpi'])
                        for ch in range(nch):
                            sl = slice(ch * LCH, (ch + 1) * LCH)
                            S.op('dve', lambda e, sbi=sbi, sl=sl: e.tensor_tensor_scan(out=rr[:, sl], data0=mag[:, sbi:sbi + 1].broadcast_to([128, LCH]), data1=bpr[:, sl],
                                                                                      initial=cre[:, sbi:sbi + 1], op0=ALU.mult, op1=ALU.add),
                                 reads=['mag', 'bpr', 'cre'], writes=['rr'])
                            S.op('dve', lambda e, sbi=sbi, sl=sl: e.tensor_tensor_scan(out=ri[:, sl], data0=mag[:, sbi:sbi + 1].broadcast_to([128, LCH]), data1=bpi[:, sl],
                                                                                      initial=cim[:, sbi:sbi + 1], op0=ALU.mult, op1=ALU.add),
                                 reads=['mag', 'bpi', 'cim'], writes=['ri'])
                            e0 = (ch + 1) * LCH - 1
                            cL = cosT[:, sbi, LCH - 1:LCH]
                            sL = sinT[:, sbi, LCH - 1:LCH]
                            S.op('dve', lambda e, e0=e0, cL=cL: e.tensor_scalar(out=ctmp[:, 0:1], in0=rr[:, e0:e0 + 1], scalar1=cL, scalar2=None, op0=ALU.mult),
                                 reads=['rr', 'cosT'], writes=['ctmp'])
                            S.op('dve', lambda e, e0=e0, cL=cL: e.tensor_scalar(out=ctmp[:, 1:2], in0=ri[:, e0:e0 + 1], scalar1=cL, scalar2=None, op0=ALU.mult),
                                 reads=['ri', 'cosT'], writes=['ctmp'])
                            S.op('dve', lambda e, e0=e0, sbi=sbi: e.scalar_tensor_tensor(out=cre[:, sbi:sbi + 1], in0=ri[:, e0:e0 + 1], scalar=sLneg[:, sbi:sbi + 1], in1=ctmp[:, 0:1],
                                                                                          op0=ALU.mult, op1=ALU.add), reads=['ri', 'sLneg', 'ctmp'], writes=['cre'])
                            S.op('dve', lambda e, e0=e0, sbi=sbi, sL=sL: e.scalar_tensor_tensor(out=cim[:, sbi:sbi + 1], in0=rr[:, e0:e0 + 1], scalar=sL, in1=ctmp[:, 1:2],
                                                                                                 op0=ALU.mult, op1=ALU.add), reads=['rr', 'sinT', 'ctmp'], writes=['cim'])
                        S.op('pool', lambda e, sbi=sbi: e.tensor_tensor(out=v3(st1), in0=v3(rr), in1=cosB(sbi), op=ALU.mult), reads=['rr', 'cosT'], writes=['st1'])
                        S.op('pool', lambda e, sbi=sbi: e.tensor_tensor(out=v3(st2), in0=v3(ri), in1=sinB(sbi), op=ALU.mult), reads=['ri', 'sinT'], writes=['st2'])
                        S.op('pool', lambda e: e.tensor_tensor(out=srb[:, 0:N], in0=st1[:, 0:N], in1=st2[:, 0:N], op=ALU.subtract), reads=['st1', 'st2'], writes=['srb'])
                        S.op('pool', lambda e, sbi=sbi: e.tensor_tensor(out=v3(st1), in0=v3(rr), in1=sinB(sbi), op=ALU.mult), reads=['rr', 'sinT'], writes=['st1'])
                        S.op('pool', lambda e, sbi=sbi: e.tensor_tensor(out=v3(st2), in0=v3(ri), in1=cosB(sbi), op=ALU.mult), reads=['ri', 'cosT'], writes=['st2'])
                        S.op('pool', lambda e: e.tensor_tensor(out=sib[:, 0:N], in0=st1[:, 0:N], in1=st2[:, 0:N], op=ALU.add), reads=['st1', 'st2'], writes=['sib'])
                        S.op('pe', lambda e, sbi=sbi, a=a: e.matmul(YB[:, 0:N], lhsT=CTr[:, sbi, :], rhs=srb[:, 0:N], start=(a == 0), stop=False),
                             reads=['CTr', 'srb'], writes=[YBK])
                        S.op('pe', lambda e, sbi=sbi, a=a: e.matmul(YB[:, 0:N], lhsT=CTi[:, sbi, :], rhs=sib[:, 0:N], start=False, stop=(a == 3)),
                             reads=['CTi', 'sib'], writes=[YBK])
                    S.op('dve', lambda e, c=c: e.scalar_tensor_tensor(out=yag[:, c, 0:N], in0=u32[:, c, 0:N], scalar=dcol(c), in1=YB[:, 0:N], op0=ALU.mult, op1=ALU.add),
                         reads=[f'u32_{c}', 's5p', YBK], writes=[f'yag{c}'])
                    S.op('act', lambda e, c=c: e.activation(out=yag[:, c, 0:N], in_=yag[:, c, 0:N], func=AF.Gelu), reads=[f'yag{c}'], writes=[f'yag{c}'])
                    S.op('dve', lambda e, c=c: e.tensor_copy(out=yagb[:, c, 0:N], in_=yag[:, c, 0:N]), reads=[f'yag{c}'], writes=[f'yagb{c}'])
            tasks.append((None, scan_all))

            def fglu(view, key):
                for m in range(4):
                    bk, bkey = bank()
                    for k in range(4):
                        S.op('pe', lambda e, k=k, m=m, bk=bk: e.matmul(bk[:, 0:N], lhsT=view[:, k, m * 128:(m + 1) * 128], rhs=yagb[:, k, 0:N],
                                                                        start=(k == 0), stop=(k == 3)), reads=[key, f'yagb{k}'], writes=[bkey])
                    S.op('act', lambda e, m=m, bk=bk: e.activation(out=st1[:, 0:N], in_=bk[:, 0:N], func=AF.Sigmoid, bias=bglu(m)), reads=[bkey, 's5p'], writes=['st1'])
                    S.op('dve', lambda e, m=m: e.tensor_tensor(out=ya2b[:, m, 0:N], in0=yag[:, m, 0:N], in1=st1[:, 0:N], op=ALU.mult), reads=[f'yag{m}', 'st1'], writes=[f'ya2b{m}'])
            tasks.append(((W['glu'][l], 0, 4, 0, 512), fglu))

            if j == NTILE - 1:
                def state_out():
                    for (src, dst, nm) in [(cre, s5re_p, 'OUT_s5re'), (cim, s5im_p, 'OUT_s5im')]:
                        bk, bkey = bank()
                        S.op('pe', lambda e, src=src, bk=bk: e.transpose(out=bk[0:16, 0:128], in_=src[:, :], identity=ident[:, :]), reads=['cre', 'cim', 'ident'], writes=[bkey])
                        S.op('act', lambda e, bk=bk: e.copy(out=smallo[0:16, 0:128], in_=bk[0:16, 0:128]), reads=[bkey], writes=['smallo'])
                        S.dma('sp', 'os5', dst[l, :, :], smallo[0:16, 0:128], reads=['smallo'], writes=[nm])
                tasks.append((None, state_out))

        def conv_tile(tasks, l, j, N):
            Win = W['w_in'][l]
            wcol = lambda k_, c: s5p[:, 1080 + k_ * 4 + c:1081 + k_ * 4 + c]

            def fbc(view, key):
                for c8 in range(8):
                    bk, bkey = bank()
                    for k in range(8):
                        S.op('pe', lambda e, k=k, c8=c8, bk=bk: e.matmul(bk[:, 0:N], lhsT=view[:, k, c8 * 128:(c8 + 1) * 128], rhs=xAb[:, k, 0:N],
                                                                          start=(k == 0), stop=(k == 7)), reads=[key, f'xAb{k}'], writes=[bkey])
                    dst = cb32 if c8 < 4 else cc32
                    nm = ('cb' if c8 < 4 else 'cc') + str(c8 % 4)
                    S.op('act', lambda e, c8=c8, bk=bk, dst=dst: e.copy(out=dst[:, c8 % 4, 0:N], in_=bk[:, 0:N]), reads=[bkey], writes=[nm])
            tasks.append(((Win, 0, 8, 1816, 1024), fbc))

            def fh(view, key):
                for c in range(4):
                    bk, bkey = bank()
                    for k in range(8):
                        S.op('pe', lambda e, k=k, c=c, bk=bk: e.matmul(bk[:, 0:N], lhsT=view[:, k, c * 128:(c + 1) * 128], rhs=xAb[:, k, 0:N],
                                                                        start=(k == 0), stop=(k == 7)), reads=[key, f'xAb{k}'], writes=[bkey])
                    S.op('dve', lambda e, c=c, bk=bk: e.tensor_tensor(out=vbuf[:, c, 2:2 + N], in0=bk[:, 0:N], in1=cc32[:, c, 0:N], op=ALU.mult),
                         reads=[bkey, f'cc{c}', 'vbuf'], writes=['vbuf'])
                    S.op('act', lambda e, c=c: e.activation(out=ctm[:, 0:N], in_=vbuf[:, c, 2:2 + N], func=AF.Copy, scale=wcol(2, c)), reads=['vbuf', 's5p'], writes=['ctm'])
                    S.op('dve', lambda e, c=c: e.scalar_tensor_tensor(out=ctm[:, 0:N], in0=vbuf[:, c, 1:1 + N], scalar=wcol(1, c), in1=ctm[:, 0:N], op0=ALU.mult, op1=ALU.add),
                         reads=['vbuf', 's5p', 'ctm'], writes=['ctm'])
                    S.op('dve', lambda e, c=c: e.scalar_tensor_tensor(out=ctm[:, 0:N], in0=vbuf[:, c, 0:N], scalar=wcol(0, c), in1=ctm[:, 0:N], op0=ALU.mult, op1=ALU.add),
                         reads=['vbuf', 's5p', 'ctm'], writes=['ctm'])
                    S.op('dve', lambda e, c=c: e.tensor_tensor(out=ycb[:, c, 0:N], in0=ctm[:, 0:N], in1=cb32[:, c, 0:N], op=ALU.mult), reads=['ctm', f'cb{c}'], writes=[f'ycb{c}'])
                if j == NTILE - 1:
                    bk, bkey = bank()
                    for c in range(4):
                        S.op('pe', lambda e, c=c, bk=bk: e.transpose(out=bk[0:2, c * 128:(c + 1) * 128], in_=vbuf[:, c, N:N + 2], identity=ident[:, :]),
                             reads=['vbuf', 'ident'], writes=[bkey])
                    S.op('act', lambda e, bk=bk: e.copy(out=smallo[0:2, 0:512], in_=bk[0:2, 0:512]), reads=[bkey], writes=['smallo'])
                    S.dma('sp', 'oconv', conv_p[l, :, :], smallo[0:2, 0:512], reads=['smallo'], writes=['OUT_conv'])
                else:
                    S.op('dve', lambda e: e.tensor_copy(out=vbuf[:, :, 0:2], in_=vbuf[:, :, N:N + 2]), reads=['vbuf'], writes=['vbuf'])
            tasks.append(((Win, 0, 8, 2840, 512), fh))

        def spill_A(tasks, j):
            def fn():
                t0 = j * TT
                tkeys = [f'tokq{s_}' for s_ in range(NSUB)]
                S.dma('sp', 'sp0', x1_s[j], xA[:, :, :], reads=[f'xA{k}' for k in range(8)], writes=['OUT_x1s'])
                S.dma('sp', 'sp1', ya_s[j], ya2b[:, :, :], reads=[f'ya2b{c}' for c in range(4)], writes=['OUT_yas'])
                S.dma('sp', 'sp2', yc_s[j], ycb[:, :, :], reads=[f'ycb{c}' for c in range(4)], writes=['OUT_ycs'])
                S.dma('sp', 'sp3', qg_s[t0:t0 + TT, 0:512].rearrange("(s p) c -> p s c", p=128), tokq[:, :, 0:512], reads=tkeys, writes=['OUT_qgs'])
                S.dma('sp', 'sp4', qg_s[t0:t0 + TT, 512:536].rearrange("(s p) c -> p s c", p=128), tokq[:, :, 1280:1304], reads=tkeys, writes=['OUT_qgs2'])
                S.dma('sp', 'sp5', win_s[t0:t0 + TT, :].rearrange("(s p) c -> p s c", p=128), tokq[:, :, 1024:1280], reads=tkeys, writes=['OUT_wins'])
            tasks.append((None, fn))

        B = K()

        def alloc_B(stk):
            uid[0] += 1
            u_ = f"_{uid[0]}"
            cur[0] = stk
            B.ksT = sb("ksT" + u_, [128, T], BF16)
            B.kwT = sb("kwT" + u_, [128, T], BF16)
            B.vs = sb("vs" + u_, [128, 32, 2, 65], BF16)
            B.vw = sb("vw" + u_, [128, 32, 2, 65], BF16)
            B.kcT = sb("kcT" + u_, [128, 256], BF16)
            B.vcm = sb("vcm" + u_, [32, 8, 2, 129], BF16)
            B.w1 = [sb(f"w1{i}" + u_, [128, 32, 128], BF16) for i in range(2)]
            B.w2 = [sb(f"w2{i}" + u_, [128, 64], BF16) for i in range(2)]
            B.peT = [sb(f"peT{i}" + u_, [128, 32], BF16) for i in range(2)]
            B.hpe = [sb(f"hpe{i}" + u_, [128, 1]) for i in range(2)]
            B.XrT = [sb(f"XrT{i}" + u_, [128, 16 + 512], BF16) for i in range(2)]
            B.rows = [sb(f"rows{i}" + u_, [128, 4, 256]) for i in range(3)]
            B.cosC = sb("cosC" + u_, [32, 8, 8])
            B.sinC = sb("sinC" + u_, [32, 8, 8])
            B.mapc = sb("mapc" + u_, [32, 8, 64])
            B.kct = sb("kct" + u_, [32, 2, 64])
            B.kt = [sb(f"kt{i}" + u_, [32, 2, 8]) for i in range(4)]
            B.hact = sb("hact" + u_, [128, 32], BF16)
            B.qtok = sb("qtok" + u_, [128, NSUB, 536])
            B.qT = sb("qT" + u_, [128, 4, TT], BF16)
            B.gates = sb("gates" + u_, [128, NSUB, 24])
            B.wmaskb = sb("wmaskb" + u_, [128, 6, TT], BF16)
            B.cmaskb = sb("cmaskb" + u_, [32, 5, TT], BF16)
            B.ebigb = sb("ebigb" + u_, [64, T], BF16)
            B.fb = sb("fb" + u_, [128, NSUB, 64])
            B.selbT = sb("selbT" + u_, [64, 2, TT], BF16)
            B.pT = [sb(f"pT{i}" + u_, [128, TT], BF16) for i in range(2)]
            B.ybt = sb("ybt" + u_, [128, NSUB, 512])
            B.imp = sb("imp" + u_, [128, NSUB, 2, 64])
            B.sc = sb("sc" + u_, [128, 8])
            B.score = sb("score" + u_, [128, 64])
            B.scr2 = sb("scr2" + u_, [128, 64])
            B.m8 = sb("m8" + u_, [128, 16])
            B.tmp64 = sb("tmp64" + u_, [128, 64])
            B.ybT = sb("ybT" + u_, [128, 4, TT], BF16)
            B.stage = sb("stage" + u_, [128, 2048])
            B.kcT_s = sb("kcT_s" + u_, [128, 1024], BF16)
            B.vcm_s = sb("vcm_s" + u_, [32, 32, 2, 65], BF16)
            B.cosC2 = sb("cosC2" + u_, [32, 32, 8])
            B.sinC2 = sb("sinC2" + u_, [32, 32, 8])
            B.maploc = sb("maploc" + u_, [32, 16], BF16)
            B.lohi = sb("lohi" + u_, [1, 256], BF16)
            B.fbs = sb("fbs" + u_, [1, 264])
            B.pti = sb("pti" + u_, [128, 128], I32)
            B.idf = sb("idf" + u_, [128, 128])
            B.idx = sb("idx" + u_, [128, 128], I32)
            B.pcol = sb("pcol" + u_, [128, DEPTH])
            B.qtok_s = sb("qtok_s" + u_, [4, 536])
            B.qT_s = sb("qT_s" + u_, [128, 4, 4], BF16)
            B.gcol = sb("gcol" + u_, [4, 2, 3])
            B.yacc = sb("yacc" + u_, [4, 2, 64])
            B.ybt_s = sb("ybt_s" + u_, [4, 512])
            B.ybT_s = sb("ybT_s" + u_, [128, 4, 4], BF16)
            B.impsb = sb("impsb" + u_, [4, 264])
            B.imprw = sb("imprw" + u_, [1, 264])
            B.scs = sb("scs" + u_, [1, 264])
            B.scs2 = sb("scs2" + u_, [1, 264])
            B.m8s = sb("m8s" + u_, [1, 16])
            B.selbb = sb("selbb" + u_, [1, 2, 264], BF16)
            B.rd = sb("rd" + u_, [4, 2, 4])
            B.pgT = sb("pgT" + u_, [128, 512], BF16)
            B.vpg = sb("vpg" + u_, [128, 4, 2, 65], BF16)
            B.pTs = [sb(f"pTs{i}" + u_, [128, 4], BF16) for i in range(2)]
            B.zl = sb("zl" + u_, [32, 4], BF16)
            B.zr = sb("zr" + u_, [32, 272], BF16)

        HP = [0, 4, 1, 5, 2, 6, 3, 7]

        def phaseB_build(l):
            ph1 = [phik1_d, phiv1_d]
            ph2 = [phik2_d, phiv2_d]
            pe = [peTk_d, peTv_d]
            for x in range(2):
                src = ph1[x][l].rearrange("(s d) h -> d s h", d=64)
                S.dma('pool', f'bw{x}a', B.w1[x][0:64, :, :], src, writes=[f'w1_{x}'])
                S.dma('pool', f'bw{x}b', B.w1[x][64:128, :, :], src, writes=[f'w1_{x}'])
                S.dma('pool', f'bw{x}c', B.w2[x][:, :], ph2[x][l], writes=[f'w2_{x}'])
                S.dma('pool', f'bw{x}d', B.peT[x][0:64, :], pe[x][l], writes=[f'peT{x}'])
                bk, bkey = bank()
                for s_ in range(32):
                    S.op('pe', lambda e, s_=s_, x=x, bk=bk: e.matmul(bk[:, 0:1], lhsT=B.w1[x][0:64, s_, :], rhs=B.peT[x][0:64, s_:s_ + 1], start=(s_ == 0), stop=(s_ == 31)),
                         reads=[f'w1_{x}', f'peT{x}'], writes=[bkey])
                S.op('act', lambda e, x=x, bk=bk: e.copy(out=B.hpe[x][:, :], in_=bk[:, 0:1]), reads=[bkey], writes=[f'hpe{x}'])
                S.op('dve', lambda e, x=x: e.memset(B.XrT[x][:, 0:16], 0.0), writes=[f'XrT{x}'])
            S.dma('sp', 'bc0', B.cosC[:], ropeCc_d[:, :, :], writes=['cosC'])
            S.dma('sp', 'bc1', B.sinC[:], ropeCs_d[:, :, :], writes=['sinC'])
            S.dma('sp', 'bc2', B.mapc[:], mapc_d[:, :, :], writes=['mapc'])
            S.op('dve', lambda e: e.memset(B.vcm[:, :, :, 64:65], 1.0), writes=['vcm'])
            for h in range(2):
                S.op('dve', lambda e, h=h: e.tensor_copy(out=B.vcm[:, :, h, 65:129], in_=B.mapc[:, :, :]), reads=['mapc', 'vcm'], writes=['vcm'])
            S.op('dve', lambda e: e.memset(B.vs[:, :, :, 64:65], 1.0), writes=['vs'])
            S.op('dve', lambda e: e.memset(B.vw[:, :, :, 64:65], 1.0), writes=['vw'])
            S.dma('sp', 'bc3', B.stage[:, 0:6 * TT], wmask_d[:, :], writes=['stage'])
            S.op('dve', lambda e: e.tensor_copy(out=B.wmaskb[:, :, :].rearrange("p a b -> p (a b)"), in_=B.stage[:, 0:6 * TT]), reads=['stage'], writes=['wmaskb'])
            S.dma('sp', 'bc3', B.stage[0:32, 0:5 * TT], cmask_d[:, :], reads=['wmaskb'], writes=['stage'])
            S.op('dve', lambda e: e.tensor_copy(out=B.cmaskb[:, :, :].rearrange("p a b -> p (a b)"), in_=B.stage[0:32, 0:5 * TT]), reads=['stage'], writes=['cmaskb'])
            for hf in range(2):
                S.dma('sp', 'bc3', B.stage[0:64, 0:2048], ebig_d[:, hf * 2048:(hf + 1) * 2048], reads=['cmaskb', 'ebigb'], writes=['stage'])
                S.op('dve', lambda e, hf=hf: e.tensor_copy(out=B.ebigb[:, hf * 2048:(hf + 1) * 2048], in_=B.stage[0:64, 0:2048]), reads=['stage'], writes=['ebigb'])
            srcs = [cmp_rows_p, sel_rows_p, win_s]
            for jg in range(max(1, (ntile * TT) // 512)):
                t0 = jg * 512
                for r_ in range(3):
                    S.dma('sp', f'br{r_}', B.rows[r_][:, :, :], srcs[r_][l, t0:t0 + 512, :].rearrange("(s p) c -> p s c", p=128) if r_ < 2
                          else win_s[t0:t0 + 512, :].rearrange("(s p) c -> p s c", p=128), writes=[f'rows{r_}'])
                for (r_, dst, dkey, col0) in [(0, B.XrT[0][:, 16:528], 'XrT0', 0), (0, B.XrT[1][:, 16:528], 'XrT1', 128),
                                              (1, B.ksT[:, t0:t0 + 512], 'ksT', 0), (2, B.kwT[:, t0:t0 + 512], 'kwT', 0)]:
                    bk, bkey = bank()
                    for sub in range(4):
                        S.op('pe', lambda e, r_=r_, sub=sub, col0=col0, bk=bk: e.transpose(out=bk[:, sub * 128:(sub + 1) * 128], in_=B.rows[r_][:, sub, col0:col0 + 128], identity=ident[:, :]),
                             reads=[f'rows{r_}', 'ident'], writes=[bkey])
                    S.op('act', lambda e, dst=dst, bk=bk: e.copy(out=dst, in_=bk[:, :]), reads=[bkey], writes=[dkey])
                S.op('dve', lambda e, jg=jg: e.tensor_copy(out=B.vs[:, jg * 4:(jg + 1) * 4, :, 0:64], in_=B.rows[1][:, :, 128:256].rearrange("p s (h d) -> p s h d", h=2)),
                     reads=['rows1'], writes=['vs'])
                S.op('dve', lambda e, jg=jg: e.tensor_copy(out=B.vw[:, jg * 4:(jg + 1) * 4, :, 0:64], in_=B.rows[2][:, :, 128:256].rearrange("p s (h d) -> p s h d", h=2)),
                     reads=['rows2'], writes=['vw'])
                for x in range(2):
                    for h in range(2):
                        bk, bkey = bank()
                        for s_ in range(32):
                            S.op('pe', lambda e, s_=s_, x=x, h=h, bk=bk: e.matmul(bk[:, 0:32], lhsT=B.w1[x][64 * h:64 * h + 64, s_, :],
                                                                                   rhs=B.XrT[x][64 * h:64 * h + 64, s_:s_ + 497:16], start=(s_ == 0), stop=(s_ == 31)),
                                 reads=[f'w1_{x}', f'XrT{x}'], writes=[bkey])
                        S.op('act', lambda e, x=x, bk=bk: e.activation(out=B.hact[:, :], in_=bk[:, 0:32], func=AF.Gelu, bias=B.hpe[x][:, 0:1]), reads=[bkey, f'hpe{x}'], writes=['hact'])
                        bk2, bk2key = bank()
                        S.op('pe', lambda e, x=x, bk2=bk2: e.matmul(bk2[0:32, 0:64], lhsT=B.hact[:, :], rhs=B.w2[x][:, :], start=True, stop=True),
                             reads=['hact', f'w2_{x}'], writes=[bk2key])
                        if x == 0:
                            S.op('act', lambda e, h=h, bk2=bk2: e.copy(out=B.kct[:, h, :], in_=bk2[0:32, 0:64]), reads=[bk2key], writes=['kct'])
                        else:
                            S.op('act', lambda e, h=h, jg=jg, bk2=bk2: e.copy(out=B.vcm[:, jg, h, 0:64], in_=bk2[0:32, 0:64]), reads=[bk2key], writes=['vcm'])
                    if x == 0:
                        x1 = B.kct[:, :, 0:8]
                        x2 = B.kct[:, :, 8:16]
                        cb_ = B.cosC[:, jg, :].unsqueeze(1).broadcast_to([32, 2, 8])
                        sb_ = B.sinC[:, jg, :].unsqueeze(1).broadcast_to([32, 2, 8])
                        tm = [t_[:, :, :] for t_ in B.kt]
                        S.op('dve', lambda e: e.tensor_tensor(out=tm[0], in0=x1, in1=cb_, op=ALU.mult), reads=['kct', 'cosC'], writes=['kt0'])
                        S.op('dve', lambda e: e.tensor_tensor(out=tm[1], in0=x2, in1=sb_, op=ALU.mult), reads=['kct', 'sinC'], writes=['kt1'])
                        S.op('dve', lambda e: e.tensor_tensor(out=tm[2], in0=x2, in1=cb_, op=ALU.mult), reads=['kct', 'cosC'], writes=['kt2'])
                        S.op('dve', lambda e: e.tensor_tensor(out=tm[3], in0=x1, in1=sb_, op=ALU.mult), reads=['kct', 'sinC'], writes=['kt3'])
                        S.op('dve', lambda e: e.tensor_tensor(out=x1, in0=tm[0], in1=tm[1], op=ALU.subtract), reads=['kt0', 'kt1'], writes=['kct'])
                        S.op('dve', lambda e: e.tensor_tensor(out=x2, in0=tm[2], in1=tm[3], op=ALU.add), reads=['kt2', 'kt3'], writes=['kct'])
                        bk, bkey = bank()
                        S.op('pe', lambda e, bk=bk: e.transpose(out=bk[:, 0:32], in_=B.kct[:, :, :].rearrange("p h d -> p (h d)"), identity=ident[0:32, 0:32]),
                             reads=['kct', 'ident'], writes=[bkey])
                        S.op('act', lambda e, jg=jg, bk=bk: e.copy(out=B.kcT[:, jg * 32:(jg + 1) * 32], in_=bk[:, 0:32]), reads=[bkey], writes=['kcT'])
                for x in range(2):
                    S.op('dve', lambda e, x=x: e.tensor_copy(out=B.XrT[x][:, 0:16], in_=B.XrT[x][:, 512:528]), reads=[f'XrT{x}'], writes=[f'XrT{x}'])

        ACC = [banks[5], banks[6]]
        ACCK = ['bank5', 'bank6']
        rb = [0]

        def bankB():
            i = rb[0] % 5
            rb[0] += 1
            return banks[i], f"bank{i}"

        def finish_branch(j, hq, br, ncols, first_branch):
            h = hq // 4
            pos = 2 * (hq % 4) + h
            for sub in range(NSUB):
                acc = ACC[sub]
                S.op('dve', lambda e, acc=acc: e.tensor_scalar(out=B.sc[:, 0:1], in0=acc[:, 64:65], scalar1=1e-30, scalar2=None, op0=ALU.max), reads=[ACCK[sub]], writes=['sc'])
                S.op('dve', lambda e: e.reciprocal(out=B.sc[:, 1:2], in_=B.sc[:, 0:1]), reads=['sc'], writes=['sc'])
                S.op('dve', lambda e, sub=sub: e.tensor_tensor(out=B.sc[:, 2:3], in0=B.sc[:, 1:2], in1=B.gates[:, sub, hq * 3 + br:hq * 3 + br + 1], op=ALU.mult),
                     reads=['sc', 'gates'], writes=['sc'])
                dst = B.ybt[:, sub, pos * 64:(pos + 1) * 64]
                if first_branch:
                    S.op('dve', lambda e, acc=acc, dst=dst: e.tensor_scalar(out=dst, in0=acc[:, 0:64], scalar1=B.sc[:, 2:3], scalar2=None, op0=ALU.mult),
                         reads=[ACCK[sub], 'sc'], writes=[f'ybt{sub}'])
                else:
                    S.op('dve', lambda e, acc=acc, dst=dst: e.scalar_tensor_tensor(out=dst, in0=acc[:, 0:64], scalar=B.sc[:, 2:3], in1=dst, op0=ALU.mult, op1=ALU.add),
                         reads=[ACCK[sub], 'sc', f'ybt{sub}'], writes=[f'ybt{sub}'])
                if ncols == 129:
                    di = B.imp[:, sub, h, :]
                    if hq % 4 == 0:
                        S.op('dve', lambda e, acc=acc, di=di: e.tensor_scalar(out=di, in0=acc[:, 65:129], scalar1=B.sc[:, 1:2], scalar2=None, op0=ALU.mult),
                             reads=[ACCK[sub], 'sc'], writes=['imp'])
                    else:
                        S.op('dve', lambda e, acc=acc, di=di: e.scalar_tensor_tensor(out=di, in0=acc[:, 65:129], scalar=B.sc[:, 1:2], in1=di, op0=ALU.mult, op1=ALU.add),
                             reads=[ACCK[sub], 'sc', 'imp'], writes=['imp'])

        def key_tile(qh, kT_ap, nk, extra, v_ap, ncols, subs_ok, first, last, pi, qtag=0):
            st, stk = bankB()
            n_e = len(extra)
            S.op('pe', lambda e: e.matmul(st[0:nk, 0:TT], lhsT=kT_ap, rhs=qh, start=True, stop=(n_e == 0)), reads=['qT', 'kv'], writes=[stk], tag=('qk', qtag))
            for i_, (lt, rh) in enumerate(extra):
                S.op('pe', lambda e, lt=lt, rh=rh, i_=i_: e.matmul(st[0:nk, 0:TT], lhsT=lt, rhs=rh, start=False, stop=(i_ == n_e - 1)), reads=['masks', 'selbT'], writes=[stk], tag=('ex', i_, nk))
            pT = B.pT[pi % 2]
            S.op('act', lambda e: e.activation(out=pT[0:nk, :], in_=st[0:nk, 0:TT], func=AF.Exp, scale=0.125), reads=[stk], writes=[f'pT{pi % 2}'])
            for sub in range(NSUB):
                if not subs_ok[sub]:
                    continue
                S.op('pe', lambda e, sub=sub: e.matmul(ACC[sub][:, 0:ncols], lhsT=pT[0:nk, sub * 128:(sub + 1) * 128], rhs=v_ap, start=first[sub], stop=last[sub]),
                     reads=[f'pT{pi % 2}', 'kv'], writes=[ACCK[sub]])

        def phaseB_tile(l, j):
            t0 = j * TT
            S.dma('sp', 'bq', B.qtok[:, :, :], qg_s[t0:t0 + TT, :].rearrange("(s p) c -> p s c", p=128), writes=['qtok'])
            S.dma('sp', 'bf', B.fb[:, :, :], fbias_d[t0:t0 + TT, :].rearrange("(s p) c -> p s c", p=128), writes=['fb'])
            for c in range(4):
                bk, bkey = bankB()
                for sub in range(NSUB):
                    S.op('pe', lambda e, c=c, sub=sub, bk=bk: e.transpose(out=bk[:, sub * 128:(sub + 1) * 128], in_=B.qtok[:, sub, c * 128:(c + 1) * 128], identity=ident[:, :]),
                         reads=['qtok', 'ident'], writes=[bkey])
                S.op('act', lambda e, c=c, bk=bk: e.copy(out=B.qT[:, c, :], in_=bk[:, 0:TT]), reads=[bkey], writes=['qT'])
            S.op('act', lambda e: e.activation(out=B.gates[:, :, :], in_=B.qtok[:, :, 512:536], func=AF.Sigmoid), reads=['qtok'], writes=['gates'])
            pi = [0]
            jgd = j // 2
            for hq in range(8):
                h = hq // 4
                qh = B.qT[64 * h:64 * h + 64, hq % 4, :]
                for jg in range(jgd + 1):
                    extra = []
                    if jg == jgd:
                        idx = (j % 2) + (2 if jg == 0 else 0)
                        extra.append((identb[0:32, 0:32], B.cmaskb[:, idx, :]))
                    elif jg == 0:
                        extra.append((identb[0:32, 0:32], B.cmaskb[:, 4, :]))
                    key_tile(qh, B.kcT[64 * h:64 * h + 64, jg * 32:(jg + 1) * 32], 32, extra, B.vcm[:, jg, h, :], 129,
                             [True] * NSUB, [jg == 0] * NSUB, [jg == jgd] * NSUB, pi[0], qtag=h)
                    pi[0] += 1
                finish_branch(j, hq, 0, 129, True)
            for sub in range(NSUB):
                for h in range(2):
                    S.op('dve', lambda e, sub=sub, h=h: e.tensor_tensor(out=B.score[:, :], in0=B.imp[:, sub, h, :], in1=B.fb[:, sub, :], op=ALU.add), reads=['imp', 'fb'], writes=['score'])
                    S.op('dve', lambda e: e.max(out=B.m8[:, 0:8], in_=B.score[:, :]), reads=['score'], writes=['m8'])
                    S.op('dve', lambda e: e.match_replace(out=B.scr2[:, :], in_to_replace=B.m8[:, 0:8], in_values=B.score[:, :], imm_value=-3.0e38), reads=['score', 'm8'], writes=['scr2'])
                    S.op('dve', lambda e: e.max(out=B.m8[:, 8:16], in_=B.scr2[:, :]), reads=['scr2'], writes=['m8'])
                    S.op('dve', lambda e: e.tensor_scalar(out=B.tmp64[:, :], in0=B.score[:, :], scalar1=B.m8[:, 15:16], scalar2=-30000.0, op0=ALU.is_lt, op1=ALU.mult),
                         reads=['score', 'm8'], writes=['tmp64'])
                    bk, bkey = bankB()
                    S.op('pe', lambda e, bk=bk: e.transpose(out=bk[0:64, 0:128], in_=B.tmp64[:, :], identity=ident[:, :]), reads=['tmp64', 'ident'], writes=[bkey])
                    S.op('act', lambda e, sub=sub, h=h, bk=bk: e.copy(out=B.selbT[:, h, sub * 128:(sub + 1) * 128], in_=bk[0:64, 0:128]), reads=[bkey], writes=['selbT'])
            for hq in range(8):
                h = hq // 4
                qh = B.qT[64 * h:64 * h + 64, hq % 4, :]
                nkt = NSUB * j + NSUB
                first = [True] * NSUB
                for kt in range(nkt):
                    d = kt - NSUB * j
                    extra = [(B.ebigb[:, kt * 128:(kt + 1) * 128], B.selbT[:, h, :])]
                    if d >= 0:
                        extra.append((identb[:, :], B.wmaskb[:, d + 4, :]))
                    ok = [(kt <= NSUB * j + sub) for sub in range(NSUB)]
                    last = [(kt == NSUB * j + sub) for sub in range(NSUB)]
                    key_tile(qh, B.ksT[64 * h:64 * h + 64, kt * 128:(kt + 1) * 128], 128, extra, B.vs[:, kt, h, :], 65, ok, list(first), last, pi[0], qtag=h)
                    first = [f and not o for f, o in zip(first, ok)]
                    pi[0] += 1
                finish_branch(j, hq, 1, 65, False)
                first = [True] * NSUB
                for kt in range(max(0, NSUB * j - 4), nkt):
                    d = kt - NSUB * j
                    extra = []
                    if d in (-4, -3, 0, 1):
                        extra.append((identb[:, :], B.wmaskb[:, d + 4, :]))
                    ok = [(kt <= NSUB * j + sub) and (kt >= NSUB * j + sub - 4) for sub in range(NSUB)]
                    last = [(kt == NSUB * j + sub) for sub in range(NSUB)]
                    key_tile(qh, B.kwT[64 * h:64 * h + 64, kt * 128:(kt + 1) * 128], 128, extra, B.vw[:, kt, h, :], 65, ok, list(first), last, pi[0], qtag=h)
                    first = [f and not o for f, o in zip(first, ok)]
                    pi[0] += 1
                finish_branch(j, hq, 2, 65, False)
            for c in range(4):
                bk, bkey = bankB()
                for sub in range(NSUB):
                    S.op('pe', lambda e, c=c, sub=sub, bk=bk: e.transpose(out=bk[:, sub * 128:(sub + 1) * 128], in_=B.ybt[:, sub, c * 128:(c + 1) * 128], identity=ident[:, :]),
                         reads=[f'ybt{sub}', 'ident'], writes=[bkey])
                S.op('act', lambda e, c=c, bk=bk: e.copy(out=B.ybT[:, c, :], in_=bk[:, 0:TT]), reads=[bkey], writes=['ybT'])
            S.dma('sp', 'byb', yb_s[j], B.ybT[:, :, :], reads=['ybT'], writes=['OUT_ybs'])


        def sample_B(l):
            rs = [0]

            def bankS():
                i = rs[0] % 4
                rs[0] += 1
                return banks[i], f"bank{i}"
            ACCs = [banks[5], banks[6]]
            ACCsK = ['bank5', 'bank6']
            IMP = [banks[7], banks[4]]
            IMPK = ['bank7', 'bank4']
            S.dma('sp', 'sq0', B.cosC2[:], ropeCc2_d[:, :, :], writes=['cosC2'])
            S.dma('sp', 'sq1', B.sinC2[:], ropeCs2_d[:, :, :], writes=['sinC2'])
            S.dma('pool', 'sq2', B.maploc[:, :], maploc_d[:, :], writes=['maploc'])
            S.dma('pool', 'sq3', B.lohi[:, :], lohi_d[:, :], writes=['lohi'])
            S.dma('sp', 'sq4', B.fbs[:, :], fbs_d[:, :], writes=['fbs'])
            S.dma('sp', 'sq4b', B.pcol[:, :], pcol_d[:, :], writes=['pcol'])
            S.op('dve', lambda e: e.memset(B.vcm_s[:, :, :, 64:65], 1.0), writes=['vcm_s'])
            S.op('dve', lambda e: e.memset(B.vpg[:, :, :, 64:65], 1.0), writes=['vpg'])
            S.op('dve', lambda e: e.memset(B.zl[:, :], 0.0), writes=['zl'])
            S.op('dve', lambda e: e.memset(B.zr[:, :], 0.0), writes=['zr'])
            S.dma('sp', 'sq5', B.qtok_s[:, :], qgs_s[:, :], writes=['qtok_s'])
            bk, bkey = bankS()
            for c in range(4):
                S.op('pe', lambda e, c=c, bk=bk: e.transpose(out=bk[:, c * 4:(c + 1) * 4], in_=B.qtok_s[0:4, c * 128:(c + 1) * 128], identity=ident[0:4, 0:4]), reads=['qtok_s', 'ident'], writes=[bkey])
            S.op('act', lambda e, bk=bk: e.copy(out=B.qT_s[:, :, :].rearrange("p c b -> p (c b)"), in_=bk[:, 0:16]), reads=[bkey], writes=['qT_s'])
            S.barrier()

            def gather(pool_ap, n, dst, key, slot):
                S.idma(slot, dst, pool_ap.rearrange("l n s c -> (l n s) c"), B.idx[:, n:n + 1], reads=['idx'], writes=[key])

            def ktile(b, h, kT_ap, nk, extra, v_ap, first, last, qtag, kk=('pgT', 'vpg')):
                qh = B.qT_s[64 * h:64 * h + 64, :, b]
                st, stk = bankS()
                n_e = len(extra)
                S.op('pe', lambda e: e.matmul(st[0:nk, 0:4], lhsT=kT_ap, rhs=qh, start=True, stop=(n_e == 0)), reads=['qT_s', kk[0]], writes=[stk], tag=('qk', qtag))
                for i_, (lt, rh) in enumerate(extra):
                    S.op('pe', lambda e, lt=lt, rh=rh, i_=i_: e.matmul(st[0:nk, 0:4], lhsT=lt, rhs=rh, start=False, stop=(i_ == n_e - 1)), reads=['selbb'], writes=[stk], tag=('ex', i_, nk))
                pi_ = rs[0] % 2
                pT = B.pTs[pi_]
                S.op('act', lambda e: e.activation(out=pT[0:nk, :], in_=st[0:nk, 0:4], func=AF.Exp, scale=0.125), reads=[stk], writes=[f'pTs{pi_}'])
                S.op('pe', lambda e: e.matmul(ACCs[h][0:4, 0:65], lhsT=pT[0:nk, 0:4], rhs=v_ap, start=first, stop=last), reads=[f'pTs{pi_}', kk[1]], writes=[ACCsK[h]], tag=('pv', nk))
                return pT, f'pTs{pi_}'

            def finish(b, h, br, first_branch):
                S.op('dve', lambda e: e.tensor_scalar(out=B.rd[:, h, 0:1], in0=ACCs[h][0:4, 64:65], scalar1=1e-30, scalar2=None, op0=ALU.max), reads=[ACCsK[h]], writes=['rd'])
                S.op('dve', lambda e: e.reciprocal(out=B.rd[:, h, 1:2], in_=B.rd[:, h, 0:1]), reads=['rd'], writes=['rd'])
                S.op('dve', lambda e: e.tensor_tensor(out=B.rd[:, h, 2:3], in0=B.rd[:, h, 1:2], in1=B.gcol[:, h, br:br + 1], op=ALU.mult), reads=['rd', 'gcol'], writes=['rd'])
                if first_branch:
                    S.op('dve', lambda e: e.tensor_scalar(out=B.yacc[:, h, :], in0=ACCs[h][0:4, 0:64], scalar1=B.rd[:, h, 2:3], scalar2=None, op0=ALU.mult), reads=[ACCsK[h], 'rd'], writes=['yacc'])
                else:
                    S.op('dve', lambda e: e.scalar_tensor_tensor(out=B.yacc[:, h, :], in0=ACCs[h][0:4, 0:64], scalar=B.rd[:, h, 2:3], in1=B.yacc[:, h, :], op0=ALU.mult, op1=ALU.add),
                         reads=[ACCsK[h], 'rd', 'yacc'], writes=['yacc'])

            for b in range(4):
                S.dma('sp', 'sq6', B.pti[:, :], ptab_d[b:b + 1, :].partition_broadcast(128), writes=['pti'])
                S.op('dve', lambda e: e.tensor_copy(out=B.idf[:, :], in_=B.pti[:, :]), reads=['pti'], writes=['idf'])
                S.op('dve', lambda e: e.tensor_scalar(out=B.idx[:, :], in0=B.idf[:, :], scalar1=128.0, scalar2=B.pcol[:, l:l + 1], op0=ALU.mult, op1=ALU.add),
                     reads=['idf', 'pcol'], writes=['idx'])
                for h in range(2):
                    S.dma('sp', f'sq7{h}', B.gcol[:, h, :], qgs_s[b, 512 + 12 * h:512 + 12 * h + 12].rearrange("(c r) -> c r", r=3), writes=['gcol'])
                S.op('act', lambda e: e.activation(out=B.gcol[:, :, :], in_=B.gcol[:, :, :], func=AF.Sigmoid), reads=['gcol'], writes=['gcol'])
                for x in range(2):
                    S.op('dve', lambda e, x=x: e.memset(B.XrT[x][:, 0:16], 0.0), writes=[f'XrT{x}'])
                for jg in range(32):
                    for s4 in range(4):
                        gather(ccmp_d, jg * 4 + s4, B.rows[0][:, s4, :], 'rows0', f'pg{s4}')
                    for (dst, dkey, col0) in [(B.XrT[0][:, 16:528], 'XrT0', 0), (B.XrT[1][:, 16:528], 'XrT1', 128)]:
                        bk, bkey = bankS()
                        for sub in range(4):
                            S.op('pe', lambda e, sub=sub, col0=col0, bk=bk: e.transpose(out=bk[:, sub * 128:(sub + 1) * 128], in_=B.rows[0][:, sub, col0:col0 + 128], identity=ident[:, :]),
                                 reads=['rows0', 'ident'], writes=[bkey])
                        S.op('act', lambda e, dst=dst, bk=bk: e.copy(out=dst, in_=bk[:, :]), reads=[bkey], writes=[dkey])
                    for x in range(2):
                        for h in range(2):
                            bk, bkey = bankS()
                            for s_ in range(32):
                                S.op('pe', lambda e, s_=s_, x=x, h=h, bk=bk: e.matmul(bk[:, 0:32], lhsT=B.w1[x][64 * h:64 * h + 64, s_, :],
                                                                                       rhs=B.XrT[x][64 * h:64 * h + 64, s_:s_ + 497:16], start=(s_ == 0), stop=(s_ == 31)),
                                     reads=[f'w1_{x}', f'XrT{x}'], writes=[bkey], tag=('cp', h))
                            S.op('act', lambda e, x=x, bk=bk: e.activation(out=B.hact[:, :], in_=bk[:, 0:32], func=AF.Gelu, bias=B.hpe[x][:, 0:1]), reads=[bkey, f'hpe{x}'], writes=['hact'])
                            bk2, bk2key = bankS()
                            S.op('pe', lambda e, x=x, bk2=bk2: e.matmul(bk2[0:32, 0:64], lhsT=B.hact[:, :], rhs=B.w2[x][:, :], start=True, stop=True), reads=['hact', f'w2_{x}'], writes=[bk2key])
                            if x == 0:
                                S.op('act', lambda e, h=h, bk2=bk2: e.copy(out=B.kct[:, h, :], in_=bk2[0:32, 0:64]), reads=[bk2key], writes=['kct'])
                            else:
                                S.op('act', lambda e, h=h, jg=jg, bk2=bk2: e.copy(out=B.vcm_s[:, jg, h, 0:64], in_=bk2[0:32, 0:64]), reads=[bk2key], writes=['vcm_s'])
                        if x == 0:
                            x1 = B.kct[:, :, 0:8]
                            x2 = B.kct[:, :, 8:16]
                            cb_ = B.cosC2[:, jg, :].unsqueeze(1).broadcast_to([32, 2, 8])
                            sb_ = B.sinC2[:, jg, :].unsqueeze(1).broadcast_to([32, 2, 8])
                            tm = [t_[:, :, :] for t_ in B.kt]
                            S.op('dve', lambda e: e.tensor_tensor(out=tm[0], in0=x1, in1=cb_, op=ALU.mult), reads=['kct', 'cosC2'], writes=['kt0'])
                            S.op('dve', lambda e: e.tensor_tensor(out=tm[1], in0=x2, in1=sb_, op=ALU.mult), reads=['kct', 'sinC2'], writes=['kt1'])
                            S.op('dve', lambda e: e.tensor_tensor(out=tm[2], in0=x2, in1=cb_, op=ALU.mult), reads=['kct', 'cosC2'], writes=['kt2'])
                            S.op('dve', lambda e: e.tensor_tensor(out=tm[3], in0=x1, in1=sb_, op=ALU.mult), reads=['kct', 'sinC2'], writes=['kt3'])
                            S.op('dve', lambda e: e.tensor_tensor(out=x1, in0=tm[0], in1=tm[1], op=ALU.subtract), reads=['kt0', 'kt1'], writes=['kct'])
                            S.op('dve', lambda e: e.tensor_tensor(out=x2, in0=tm[2], in1=tm[3], op=ALU.add), reads=['kt2', 'kt3'], writes=['kct'])
                            bk, bkey = bankS()
                            S.op('pe', lambda e, bk=bk: e.transpose(out=bk[:, 0:32], in_=B.kct[:, :, :].rearrange("p h d -> p (h d)"), identity=ident[0:32, 0:32]), reads=['kct', 'ident'], writes=[bkey])
                            S.op('act', lambda e, jg=jg, bk=bk: e.copy(out=B.kcT_s[:, jg * 32:(jg + 1) * 32], in_=bk[:, 0:32]), reads=[bkey], writes=['kcT_s'])
                    for x in range(2):
                        S.op('dve', lambda e, x=x: e.tensor_copy(out=B.XrT[x][:, 0:16], in_=B.XrT[x][:, 512:528]), reads=[f'XrT{x}'], writes=[f'XrT{x}'])
                for h in range(2):
                    S.op('pe', lambda e, h=h: e.matmul(IMP[h][0:4, 0:272], lhsT=B.zl[:, :], rhs=B.zr[:, :], start=True, stop=True), reads=['zl', 'zr'], writes=[IMPK[h]], tag=('z',))
                    for jg in range(32):
                        extra = [(identb[0:32, 0:32], B.cmaskb[:, 4, 0:4])] if jg == 0 else []
                        pT, pk = ktile(b, h, B.kcT_s[64 * h:64 * h + 64, jg * 32:(jg + 1) * 32], 32, extra, B.vcm_s[:, jg, h, :], jg == 0, jg == 31, h, kk=('kcT_s', 'vcm_s'))
                        S.op('pe', lambda e, jg=jg, h=h, pT=pT: e.matmul(IMP[h][0:4, 8 * jg:8 * jg + 9], lhsT=pT[0:32, 0:4], rhs=B.maploc[:, 0:9], start=False, stop=True, skip_group_check=True),
                             reads=[pk, 'maploc'], writes=[IMPK[h]], tag=('z',))
                    finish(b, h, 0, True)
                    S.op('act', lambda e, h=h: e.copy(out=B.impsb[:, :], in_=IMP[h][0:4, 0:264]), reads=[IMPK[h]], writes=['impsb'])
                    rw, rwk = bankS()
                    S.op('pe', lambda e, h=h, rw=rw: e.matmul(rw[0:1, 0:264], lhsT=B.rd[:, h, 1:2], rhs=B.impsb[:, :], start=True, stop=True), reads=['rd', 'impsb'], writes=[rwk])
                    S.op('dve', lambda e, rw=rw: e.tensor_tensor(out=B.scs[:, 0:257], in0=rw[0:1, 1:258], in1=B.fbs[:, 0:257], op=ALU.add), reads=[rwk, 'fbs'], writes=['scs'])
                    S.op('dve', lambda e: e.max(out=B.m8s[:, 0:8], in_=B.scs[:, 0:257]), reads=['scs'], writes=['m8s'])
                    S.op('dve', lambda e: e.match_replace(out=B.scs2[:, 0:257], in_to_replace=B.m8s[:, 0:8], in_values=B.scs[:, 0:257], imm_value=-3.0e38), reads=['scs', 'm8s'], writes=['scs2'])
                    S.op('dve', lambda e: e.max(out=B.m8s[:, 8:16], in_=B.scs2[:, 0:257]), reads=['scs2'], writes=['m8s'])
                    S.op('dve', lambda e, h=h: e.tensor_scalar(out=B.selbb[:, h, 0:257], in0=B.scs[:, 0:257], scalar1=B.m8s[:, 15:16], scalar2=-30000.0, op0=ALU.is_lt, op1=ALU.mult),
                         reads=['scs', 'm8s'], writes=['selbb'])
                for n4 in range(32):
                    for s4 in range(4):
                        gather(csel_d, n4 * 4 + s4, B.rows[1][:, s4, :], 'rows1', f'pg{s4}')
                    bk, bkey = bankS()
                    for s4 in range(4):
                        S.op('pe', lambda e, s4=s4, bk=bk: e.transpose(out=bk[:, s4 * 128:(s4 + 1) * 128], in_=B.rows[1][:, s4, 0:128], identity=ident[:, :]), reads=['rows1', 'ident'], writes=[bkey])
                    S.op('act', lambda e, bk=bk: e.copy(out=B.pgT[:, :], in_=bk[:, :]), reads=[bkey], writes=['pgT'])
                    S.op('dve', lambda e: e.tensor_copy(out=B.vpg[:, :, :, 0:64], in_=B.rows[1][:, :, 128:256].rearrange("p s (h d) -> p s h d", h=2)), reads=['rows1'], writes=['vpg'])
                    for s4 in range(4):
                        n = n4 * 4 + s4
                        for h in range(2):
                            extra = [(B.lohi[0:1, 0:128], B.selbb[0:1, h, 2 * n:2 * n + 1].broadcast_to([1, 4])),
                                     (B.lohi[0:1, 128:256], B.selbb[0:1, h, 2 * n + 1:2 * n + 2].broadcast_to([1, 4]))]
                            ktile(b, h, B.pgT[64 * h:64 * h + 64, s4 * 128:(s4 + 1) * 128], 128, extra, B.vpg[:, s4, h, :], n == 0, False, h)
                S.dma('sp', 'sq8', B.rows[1][0:1, 0, :], sel_rows_s[l, b:b + 1, :], writes=['rows1'])
                bk, bkey = bankS()
                S.op('pe', lambda e, bk=bk: e.transpose(out=bk[:, 0:1], in_=B.rows[1][0:1, 0, 0:128], identity=ident[0:1, 0:1]), reads=['rows1', 'ident'], writes=[bkey])
                S.op('act', lambda e, bk=bk: e.copy(out=B.pgT[:, 0:1], in_=bk[:, 0:1]), reads=[bkey], writes=['pgT'])
                S.op('dve', lambda e: e.tensor_copy(out=B.vpg[0:1, 0, :, 0:64], in_=B.rows[1][0:1, 0, 128:256].rearrange("p (h d) -> p h d", h=2)), reads=['rows1'], writes=['vpg'])
                for h in range(2):
                    ktile(b, h, B.pgT[64 * h:64 * h + 64, 0:1], 1, [], B.vpg[0:1, 0, h, :], False, True, h)
                    finish(b, h, 1, False)
                S.dma('sp', 'sq9', B.rows[2][:, :, :], win_so[l, b, :, :].rearrange("(s p) c -> p s c", p=128), reads=['OUT_win_s', 'OUT_win_s2'], writes=['rows2'])
                bk, bkey = bankS()
                for s4 in range(4):
                    S.op('pe', lambda e, s4=s4, bk=bk: e.transpose(out=bk[:, s4 * 128:(s4 + 1) * 128], in_=B.rows[2][:, s4, 0:128], identity=ident[:, :]), reads=['rows2', 'ident'], writes=[bkey])
                S.op('act', lambda e, bk=bk: e.copy(out=B.pgT[:, :], in_=bk[:, :]), reads=[bkey], writes=['pgT'])
                S.op('dve', lambda e: e.tensor_copy(out=B.vpg[:, :, :, 0:64], in_=B.rows[2][:, :, 128:256].rearrange("p s (h d) -> p s h d", h=2)), reads=['rows2'], writes=['vpg'])
                for h in range(2):
                    for s4 in range(4):
                        ktile(b, h, B.pgT[64 * h:64 * h + 64, s4 * 128:(s4 + 1) * 128], 128, [], B.vpg[:, s4, h, :], s4 == 0, s4 == 3, h)
                    finish(b, h, 2, False)
                    S.dma('sp', f'sqa{h}', ybr_s[b, :].rearrange("(c two d) -> c two d", two=2, d=64)[:, h, :], B.yacc[:, h, :], reads=['yacc'], writes=['OUT_ybr_s'])
            S.barrier()
            S.dma('sp', 'sqb', B.ybt_s[:, :], ybr_s[:, :], writes=['ybt_s'])
            bk, bkey = bankS()
            for c in range(4):
                S.op('pe', lambda e, c=c, bk=bk: e.transpose(out=bk[:, c * 4:(c + 1) * 4], in_=B.ybt_s[0:4, c * 128:(c + 1) * 128], identity=ident[0:4, 0:4]), reads=['ybt_s', 'ident'], writes=[bkey])
            S.op('act', lambda e, bk=bk: e.copy(out=B.ybT_s[:, :, :].rearrange("p c b -> p (c b)"), in_=bk[:, 0:16]), reads=[bkey], writes=['ybT_s'])
            S.dma('sp', 'sqc', ybs_s[:, :, :], B.ybT_s[:, :, :], reads=['ybT_s'], writes=['OUT_ybs_s'])

        C_ = K()

        def alloc_C(stk):
            uid[0] += 1
            u_ = f"_{uid[0]}"
            cur[0] = stk
            C_.gm = sb("gm" + u_, [128, 3, 2, TT], BF16)
            C_.m32 = sb("m32" + u_, [128, 2, TT])
            C_.mt = sb("mt" + u_, [128, TT])
            C_.mb = sb("mb" + u_, [128, 8, TT], BF16)
            C_.yin = sb("yin" + u_, [128, 3, 4, TT], BF16)

        def phaseC_tile(tasks, l, j, N=TT, srcs=None):
            x1src, ysrcs = srcs if srcs is not None else (x1_s[j], [ya_s[j], yb_s[j], yc_s[j]])
            Win = W['w_in'][l]

            def ld():
                S.dma('sp', 'cx', xA[:, :, 0:N], x1src, writes=[f'xA{k}' for k in range(8)])
                for k in range(8):
                    S.op('dve', lambda e, k=k: e.tensor_copy(out=xAb[:, k, 0:N], in_=xA[:, k, 0:N]), reads=[f'xA{k}'], writes=[f'xAb{k}'])
                for i_, src in enumerate(ysrcs):
                    S.dma('sp', f'cy{i_}', C_.yin[:, i_, :, 0:N], src, writes=[f'yin{i_}'])
            tasks.append((None, ld))
            Wp = [W['pa'][l], W['pb'][l], W['pc'][l]]
            for b_ in range(4):
                for br in range(3):
                    def fg(view, key, br=br):
                        for ml in range(2):
                            bk, bkey = bank()
                            for k in range(8):
                                S.op('pe', lambda e, k=k, ml=ml, bk=bk: e.matmul(bk[:, 0:N], lhsT=view[:, k, ml * 128:(ml + 1) * 128], rhs=xAb[:, k, 0:N], start=(k == 0), stop=(k == 7)),
                                     reads=[key, f'xAb{k}'], writes=[bkey])
                            S.op('act', lambda e, ml=ml, bk=bk, br=br: e.activation(out=C_.gm[:, br, ml, 0:N], in_=bk[:, 0:N], func=AF.Sigmoid), reads=[bkey], writes=[f'gm{br}{ml}'])
                    tasks.append(((Win, 0, 8, 3352 + br * 1024 + 256 * b_, 256), fg))
                for br in range(3):
                    def fp(view, key, br=br, b_=b_):
                        for ml in range(2):
                            bk, bkey = bank()
                            for k in range(4):
                                S.op('pe', lambda e, k=k, ml=ml, bk=bk: e.matmul(bk[:, 0:N], lhsT=view[:, k, ml * 128:(ml + 1) * 128], rhs=C_.yin[:, br, k, 0:N], start=(k == 0), stop=(k == 3)),
                                     reads=[key, f'yin{br}'], writes=[bkey])
                            if br == 0:
                                S.op('dve', lambda e, ml=ml, bk=bk: e.tensor_tensor(out=C_.m32[:, ml, 0:N], in0=bk[:, 0:N], in1=C_.gm[:, 0, ml, 0:N], op=ALU.mult),
                                     reads=[bkey, f'gm0{ml}'], writes=[f'm32{ml}'])
                            else:
                                S.op('dve', lambda e, ml=ml, bk=bk, br=br: e.tensor_tensor(out=C_.mt[:, 0:N], in0=bk[:, 0:N], in1=C_.gm[:, br, ml, 0:N], op=ALU.mult),
                                     reads=[bkey, f'gm{br}{ml}'], writes=['mt'])
                                if br == 1:
                                    S.op('dve', lambda e, ml=ml: e.tensor_tensor(out=C_.m32[:, ml, 0:N], in0=C_.m32[:, ml, 0:N], in1=C_.mt[:, 0:N], op=ALU.add),
                                         reads=[f'm32{ml}', 'mt'], writes=[f'm32{ml}'])
                                else:
                                    S.op('dve', lambda e, ml=ml, b_=b_: e.tensor_tensor(out=C_.mb[:, 2 * b_ + ml, 0:N], in0=C_.m32[:, ml, 0:N], in1=C_.mt[:, 0:N], op=ALU.add),
                                         reads=[f'm32{ml}', 'mt'], writes=[f'mb{2 * b_ + ml}'])
                    tasks.append(((Wp[br], 0, 4, 256 * b_, 256), fp))

            def fo(view, key):
                for m in range(8):
                    bk, bkey = bank()
                    for k in range(8):
                        S.op('pe', lambda e, k=k, m=m, bk=bk: e.matmul(bk[:, 0:N], lhsT=view[:, k, m * 128:(m + 1) * 128], rhs=C_.mb[:, k, 0:N], start=(k == 0), stop=(k == 7)),
                             reads=[key, f'mb{k}'], writes=[bkey])
                    S.op('act', lambda e, m=m: e.activation(out=rS[:, m, 0:N], in_=xA[:, m, 0:N], func=AF.Copy, scale=ALPHA), reads=[f'xA{m}'], writes=[f'rS{m}'])
                    S.op('dve', lambda e, m=m, bk=bk: e.tensor_tensor(out=rS[:, m, 0:N], in0=bk[:, 0:N], in1=rS[:, m, 0:N], op=ALU.add), reads=[bkey, f'rS{m}'], writes=[f'rS{m}'])
            tasks.append(((W['wo'][l], 0, 8, 0, 1024), fo))

        def sample_A(l):
            N = 4
            tasks = []
            load_x_tile(tasks, xs_d if l == 0 else xmid_s, N, [(0, 4)], ['OUT_xmid_s'] if l else [])
            ffn(tasks, N, W['ffn1_gu'][l], W['ffn1_dn'][l])
            layernorm(tasks, N, l, 0)
            Win = W['w_in'][l]
            tk = ['tokq0']

            def proj(view, key, c_lo, pieces):
                for (p0, pn) in pieces:
                    bk, bkey = bank()
                    for k in range(8):
                        S.op('pe', lambda e, k=k, p0=p0, pn=pn, bk=bk: e.matmul(bk[0:4, 0:pn], lhsT=xAb[:, k, 0:4], rhs=view[:, k, p0:p0 + pn], start=(k == 0), stop=(k == 7)),
                             reads=[key, f'xAb{k}'], writes=[bkey])
                    S.op('act', lambda e, p0=p0, pn=pn, bk=bk: e.copy(out=tokq[0:4, 0, c_lo + p0:c_lo + p0 + pn], in_=bk[0:4, 0:pn]), reads=[bkey], writes=tk)
            tasks.append(((Win, 0, 8, 512, 1024), lambda view, key: proj(view, key, 0, [(0, 512), (512, 512)])))
            tasks.append(((Win, 0, 8, 1536, 280), lambda view, key: proj(view, key, 1024, [(0, 280)])))

            def rope_store():
                S.dma('sp', 'sa0', SA.ropeS[:], ropeS_d[:, :, :], writes=['ropeS'])
                S.dma('sp', 'sa1', cmp_rows_s[l], tokq[0:4, 0, 512:768], reads=tk, writes=['OUT_cmp_s'])
                for (c0, nh) in [(0, 8), (768, 2), (1024, 2)]:
                    V = tokq[0:4, 0, c0:c0 + nh * 64].rearrange("p (h d) -> p h d", h=nh)
                    x1 = V[:, :, 0:8]
                    x2 = V[:, :, 8:16]
                    cb_ = SA.ropeS[0:4, 0, :].unsqueeze(1).broadcast_to([4, nh, 8])
                    sb_ = SA.ropeS[0:4, 1, :].unsqueeze(1).broadcast_to([4, nh, 8])
                    tm = [rt[i][0:4, 0, 0:nh, :] for i in range(4)]
                    S.op('dve', lambda e: e.tensor_tensor(out=tm[0], in0=x1, in1=cb_, op=ALU.mult), reads=tk + ['ropeS'], writes=['rt0'])
                    S.op('dve', lambda e: e.tensor_tensor(out=tm[1], in0=x2, in1=sb_, op=ALU.mult), reads=tk + ['ropeS'], writes=['rt1'])
                    S.op('dve', lambda e: e.tensor_tensor(out=tm[2], in0=x2, in1=cb_, op=ALU.mult), reads=tk + ['ropeS'], writes=['rt2'])
                    S.op('dve', lambda e: e.tensor_tensor(out=tm[3], in0=x1, in1=sb_, op=ALU.mult), reads=tk + ['ropeS'], writes=['rt3'])
                    S.op('dve', lambda e: e.tensor_tensor(out=x1, in0=tm[0], in1=tm[1], op=ALU.subtract), reads=['rt0', 'rt1'], writes=tk)
                    S.op('dve', lambda e: e.tensor_tensor(out=x2, in0=tm[2], in1=tm[3], op=ALU.add), reads=['rt2', 'rt3'], writes=tk)
                S.dma('sp', 'sa2', sel_rows_s[l], tokq[0:4, 0, 768:1024], reads=tk, writes=['OUT_sel_s'])
                S.dma('sp', 'sa3', win_so[l, :, 511, :], tokq[0:4, 0, 1024:1280], reads=tk, writes=['OUT_win_s'])
                S.dma('sp', 'sa4', win_so[l, :, 0:511, :], cwin_d[l, :, 1:512, :], writes=['OUT_win_s2'])
                S.dma('sp', 'sa5', qgs_s[:, 0:512], tokq[0:4, 0, 0:512], reads=tk, writes=['OUT_qgs_s'])
                S.dma('sp', 'sa6', qgs_s[:, 512:536], tokq[0:4, 0, 1280:1304], reads=tk, writes=['OUT_qgs_s2'])
                S.dma('sp', 'sa7', x1s_s[:, :, :], xA[:, :, 0:4], reads=[f'xA{k}' for k in range(8)], writes=['OUT_x1s_s'])
            tasks.append((None, rope_store))
            dcol = lambda c: s5p[:, 1072 + c:1073 + c]
            bglu = lambda c: s5p[:, 1076 + c:1077 + c]

            def fu(view, key):
                for c in range(4):
                    bk, bkey = bank()
                    for k in range(8):
                        S.op('pe', lambda e, k=k, c=c, bk=bk: e.matmul(bk[:, 0:N], lhsT=view[:, k, c * 128:(c + 1) * 128], rhs=xAb[:, k, 0:N], start=(k == 0), stop=(k == 7)),
                             reads=[key, f'xAb{k}'], writes=[bkey])
                    S.op('act', lambda e, c=c, bk=bk: e.copy(out=u32[:, c, 0:N], in_=bk[:, 0:N]), reads=[bkey], writes=[f'u32_{c}'])
                    S.op('dve', lambda e, c=c: e.tensor_copy(out=ub[:, c, 0:N], in_=u32[:, c, 0:N]), reads=[f'u32_{c}'], writes=[f'ub{c}'])
            tasks.append(((Win, 0, 8, 0, 512), fu))

            def step():
                S.dma('sp', 'sb0', SA.s0t[:, 0, :, :], s5re0_d[l].rearrange("b s c -> s b c"), writes=['s0t'])
                S.dma('sp', 'sb1', SA.s0t[:, 1, :, :], s5im0_d[l].rearrange("b s c -> s b c"), writes=['s0t'])
                bk, bkey = bank()
                for ri_ in range(2):
                    for b_ in range(4):
                        o_ = (ri_ * 4 + b_) * 16
                        S.op('pe', lambda e, ri_=ri_, b_=b_, o_=o_, bk=bk: e.transpose(out=bk[:, o_:o_ + 16], in_=SA.s0t[0:16, ri_, b_, :], identity=ident[0:16, 0:16]),
                             reads=['s0t', 'ident'], writes=[bkey])
                S.op('act', lambda e, bk=bk: e.copy(out=SA.s0[:, :, :, :].rearrange("p r s b -> p r b s"), in_=bk[:, 0:128].rearrange("p (r b s) -> p r b s", r=2, b=4)),
                     reads=[bkey], writes=['s0'])
                bA, bAk = bank()
                bB, bBk = bank()
                for sbi in range(16):
                    S.op('pe', lambda e, sbi=sbi: e.matmul(bA[:, sbi * 4:(sbi + 1) * 4], lhsT=BTr[:, sbi, :], rhs=ub[:, sbi // 4, 0:4], start=True, stop=True),
                         reads=['BTr', f'ub{sbi // 4}'], writes=[bAk])
                    S.op('pe', lambda e, sbi=sbi: e.matmul(bB[:, sbi * 4:(sbi + 1) * 4], lhsT=BTi[:, sbi, :], rhs=ub[:, sbi // 4, 0:4], start=True, stop=True),
                         reads=['BTi', f'ub{sbi // 4}'], writes=[bBk])
                are_b = s5t[4][:, :].unsqueeze(2).broadcast_to([128, 16, 4])
                aim_b = s5t[5][:, :].unsqueeze(2).broadcast_to([128, 16, 4])
                A3 = bA[:, 0:64].rearrange("p (s b) -> p s b", b=4)
                B3 = bB[:, 0:64].rearrange("p (s b) -> p s b", b=4)
                s0r = SA.s0[:, 0, :, :]
                s0i = SA.s0[:, 1, :, :]
                t1 = SA.t[0][:, :, :]
                t2 = SA.t[1][:, :, :]
                S.op('dve', lambda e: e.tensor_tensor(out=t1, in0=s0r, in1=are_b, op=ALU.mult), reads=['s0', 's5t4'], writes=['sat0'])
                S.op('dve', lambda e: e.tensor_tensor(out=t2, in0=s0i, in1=aim_b, op=ALU.mult), reads=['s0', 's5t5'], writes=['sat1'])
                S.op('dve', lambda e: e.tensor_tensor(out=t1, in0=t1, in1=t2, op=ALU.subtract), reads=['sat0', 'sat1'], writes=['sat0'])
                S.op('dve', lambda e: e.tensor_tensor(out=SA.sn[:, 0, :, :], in0=A3, in1=t1, op=ALU.add), reads=[bAk, 'sat0'], writes=['sn'])
                S.op('dve', lambda e: e.tensor_tensor(out=t1, in0=s0i, in1=are_b, op=ALU.mult), reads=['s0', 's5t4'], writes=['sat0'])
                S.op('dve', lambda e: e.tensor_tensor(out=t2, in0=s0r, in1=aim_b, op=ALU.mult), reads=['s0', 's5t5'], writes=['sat1'])
                S.op('dve', lambda e: e.tensor_tensor(out=t1, in0=t1, in1=t2, op=ALU.add), reads=['sat0', 'sat1'], writes=['sat0'])
                S.op('dve', lambda e: e.tensor_tensor(out=SA.sn[:, 1, :, :], in0=B3, in1=t1, op=ALU.add), reads=[bBk, 'sat0'], writes=['sn'])
                S.op('dve', lambda e: e.tensor_copy(out=SA.snb[:, :, :, :], in_=SA.sn[:, :, :, :]), reads=['sn'], writes=['snb'])
                for c in range(4):
                    for a in range(4):
                        sbi = c * 4 + a
                        S.op('pe', lambda e, sbi=sbi, a=a: e.matmul(YB[:, 0:4], lhsT=CTr[:, sbi, :], rhs=SA.snb[:, 0, sbi, :], start=(a == 0), stop=False), reads=['CTr', 'snb'], writes=[YBK])
                        S.op('pe', lambda e, sbi=sbi, a=a: e.matmul(YB[:, 0:4], lhsT=CTi[:, sbi, :], rhs=SA.snb[:, 1, sbi, :], start=False, stop=(a == 3)), reads=['CTi', 'snb'], writes=[YBK])
                    S.op('dve', lambda e, c=c: e.scalar_tensor_tensor(out=yag[:, c, 0:N], in0=u32[:, c, 0:N], scalar=dcol(c), in1=YB[:, 0:N], op0=ALU.mult, op1=ALU.add),
                         reads=[f'u32_{c}', 's5p', YBK], writes=[f'yag{c}'])
                    S.op('act', lambda e, c=c: e.activation(out=yag[:, c, 0:N], in_=yag[:, c, 0:N], func=AF.Gelu), reads=[f'yag{c}'], writes=[f'yag{c}'])
                    S.op('dve', lambda e, c=c: e.tensor_copy(out=yagb[:, c, 0:N], in_=yag[:, c, 0:N]), reads=[f'yag{c}'], writes=[f'yagb{c}'])
                for ri_, dst, nm in [(0, s5re_so, 'OUT_s5re_s'), (1, s5im_so, 'OUT_s5im_s')]:
                    bk, bkey = bank()
                    for b_ in range(4):
                        S.op('pe', lambda e, ri_=ri_, b_=b_, bk=bk: e.transpose(out=bk[0:16, b_ * 128:(b_ + 1) * 128], in_=SA.sn[:, ri_, :, b_], identity=ident[:, :]),
                             reads=['sn', 'ident'], writes=[bkey])
                    S.op('act', lambda e, ri_=ri_, bk=bk: e.copy(out=SA.s0t[0:16, ri_, :, :], in_=bk[0:16, 0:512].rearrange("p (b c) -> p b c", b=4)), reads=[bkey], writes=['s0t'])
                    S.dma('sp', f'sc{ri_}', dst[l].rearrange("b s c -> s b c"), SA.s0t[0:16, ri_, :, :], reads=['s0t'], writes=[nm])
            tasks.append((None, step))

            def fglu(view, key):
                for m in range(4):
                    bk, bkey = bank()
                    for k in range(4):
                        S.op('pe', lambda e, k=k, m=m, bk=bk: e.matmul(bk[:, 0:N], lhsT=view[:, k, m * 128:(m + 1) * 128], rhs=yagb[:, k, 0:N], start=(k == 0), stop=(k == 3)),
                             reads=[key, f'yagb{k}'], writes=[bkey])
                    S.op('act', lambda e, m=m, bk=bk: e.activation(out=st1[:, 0:N], in_=bk[:, 0:N], func=AF.Sigmoid, bias=bglu(m)), reads=[bkey, 's5p'], writes=['st1'])
                    S.op('dve', lambda e, m=m: e.tensor_tensor(out=ya2b[:, m, 0:N], in0=yag[:, m, 0:N], in1=st1[:, 0:N], op=ALU.mult), reads=[f'yag{m}', 'st1'], writes=[f'ya2b{m}'])
            tasks.append(((W['glu'][l], 0, 4, 0, 512), fglu))
            wcol = lambda k_, c: s5p[:, 1080 + k_ * 4 + c:1081 + k_ * 4 + c]

            def fbc(view, key):
                for c8 in range(8):
                    bk, bkey = bank()
                    for k in range(8):
                        S.op('pe', lambda e, k=k, c8=c8, bk=bk: e.matmul(bk[:, 0:N], lhsT=view[:, k, c8 * 128:(c8 + 1) * 128], rhs=xAb[:, k, 0:N], start=(k == 0), stop=(k == 7)),
                             reads=[key, f'xAb{k}'], writes=[bkey])
                    dst = cb32 if c8 < 4 else cc32
                    nm = ('cb' if c8 < 4 else 'cc') + str(c8 % 4)
                    S.op('act', lambda e, c8=c8, bk=bk, dst=dst: e.copy(out=dst[:, c8 % 4, 0:N], in_=bk[:, 0:N]), reads=[bkey], writes=[nm])
            tasks.append(((Win, 0, 8, 1816, 1024), fbc))

            def fh(view, key):
                S.dma('sp', 'sd0', SA.cvt[:, :], cconv_d[l].rearrange("b t c -> (b t) c"), writes=['cvt'])
                S.dma('sp', 'sd1', conv_so[l, :, 0, :], cconv_d[l, :, 1, :], writes=['OUT_conv_s0'])
                bq, bqk = bank()
                for c in range(4):
                    S.op('pe', lambda e, c=c: e.transpose(out=bq[:, c * 8:(c + 1) * 8], in_=SA.cvt[0:8, c * 128:(c + 1) * 128], identity=ident[0:8, 0:8]), reads=['cvt', 'ident'], writes=[bqk])
                S.op('act', lambda e: e.copy(out=SA.cbuf[:, :, :, :].rearrange("p c b t -> p (c b t)"), in_=bq[:, 0:32]), reads=[bqk], writes=['cbuf'])
                for c in range(4):
                    bk, bkey = bank()
                    for k in range(8):
                        S.op('pe', lambda e, k=k, c=c, bk=bk: e.matmul(bk[:, 0:N], lhsT=view[:, k, c * 128:(c + 1) * 128], rhs=xAb[:, k, 0:N], start=(k == 0), stop=(k == 7)),
                             reads=[key, f'xAb{k}'], writes=[bkey])
                    S.op('dve', lambda e, c=c, bk=bk: e.tensor_tensor(out=SA.vn[:, c, :], in0=bk[:, 0:N], in1=cc32[:, c, 0:N], op=ALU.mult), reads=[bkey, f'cc{c}'], writes=['vn'])
                    S.op('act', lambda e, c=c: e.activation(out=ctm[:, 0:N], in_=SA.vn[:, c, :], func=AF.Copy, scale=wcol(2, c)), reads=['vn', 's5p'], writes=['ctm'])
                    S.op('dve', lambda e, c=c: e.scalar_tensor_tensor(out=ctm[:, 0:N], in0=SA.cbuf[:, c, :, 1], scalar=wcol(1, c), in1=ctm[:, 0:N], op0=ALU.mult, op1=ALU.add),
                         reads=['cbuf', 's5p', 'ctm'], writes=['ctm'])
                    S.op('dve', lambda e, c=c: e.scalar_tensor_tensor(out=ctm[:, 0:N], in0=SA.cbuf[:, c, :, 0], scalar=wcol(0, c), in1=ctm[:, 0:N], op0=ALU.mult, op1=ALU.add),
                         reads=['cbuf', 's5p', 'ctm'], writes=['ctm'])
                    S.op('dve', lambda e, c=c: e.tensor_tensor(out=ycb[:, c, 0:N], in0=ctm[:, 0:N], in1=cb32[:, c, 0:N], op=ALU.mult), reads=['ctm', f'cb{c}'], writes=[f'ycb{c}'])
                bk, bkey = bank()
                for c in range(4):
                    S.op('pe', lambda e, c=c, bk=bk: e.transpose(out=bk[0:4, c * 128:(c + 1) * 128], in_=SA.vn[:, c, :], identity=ident[:, :]), reads=['vn', 'ident'], writes=[bkey])
                S.op('act', lambda e, bk=bk: e.copy(out=smallo[0:4, 0:512], in_=bk[0:4, 0:512]), reads=[bkey], writes=['smallo'])
                S.dma('sp', 'sd2', conv_so[l, :, 1, :], smallo[0:4, 0:512], reads=['smallo'], writes=['OUT_conv_s1'])
                S.dma('sp', 'sd3', yas_s[:, :, :], ya2b[:, :, 0:4], reads=[f'ya2b{c}' for c in range(4)], writes=['OUT_yas_s'])
                S.dma('sp', 'sd4', ycs_s[:, :, :], ycb[:, :, 0:4], reads=[f'ycb{c}' for c in range(4)], writes=['OUT_ycs_s'])
            tasks.append(((Win, 0, 8, 2840, 512), fh))
            run_tasks(tasks)

        def dbg_dump(tasks, name, tile_ap, keys):
            if name not in dbg_out:
                return

            def fn():
                S.dma('sp', 'dbg_' + name, dbg_out[name], tile_ap, reads=keys, writes=['OUT_dbg_' + name])
            tasks.append((None, fn))

        subs512 = [(a * 128, 128) for a in range(NSUB)]
        nlayer = FLAGS.get('nlayer', DEPTH)
        for l in range(nlayer):
            with ExitStack() as pa:
                alloc_ffn(pa)
                alloc_A(pa)
                s5_setup(l)
                for j in range(ntile):
                    tasks = []
                    load_x_tile(tasks, (xp if l == 0 else xmid)[j * TT:(j + 1) * TT, :], TT, subs512, ['OUT_xmid'] if l else [])
                    ffn(tasks, TT, W['ffn1_gu'][l], W['ffn1_dn'][l])
                    layernorm(tasks, TT, l, 0)
                    if j == 0 and l == 0:
                        dbg_dump(tasks, 'x1', xA[:, :, :], [f'xA{k}' for k in range(8)])
                    win_tokmajor(tasks, l, j)
                    s5_tile(tasks, l, j, TT)
                    conv_tile(tasks, l, j, TT)
                    if j == 0 and l == 0:
                        dbg_dump(tasks, 'ya', yag[:, :, :], [f'yag{c}' for c in range(4)])
                    spill_A(tasks, j)
                    run_tasks(tasks)
                if FLAGS.get('sample', True):
                    sample_A(l)
                S.barrier()
            if not FLAGS.get('phaseB', True) or (l == nlayer - 1 and FLAGS.get('skip_last_BC', False)):
                continue
            with ExitStack() as pb:
                alloc_B(pb)
                phaseB_build(l)
                S.barrier()
                for j in range(ntile):
                    phaseB_tile(l, j)
                S.barrier()
                if FLAGS.get('sample', True):
                    sample_B(l)
                    S.barrier()
            with ExitStack() as pc:
                alloc_ffn(pc)
                alloc_C(pc)
                for j in range(ntile):
                    tasks = []
                    phaseC_tile(tasks, l, j)
                    layernorm(tasks, TT, l, 1)
                    if j == 0 and l == 0:
                        dbg_dump(tasks, 'x2', xA[:, :, :], [f'xA{k}' for k in range(8)])
                    ffn(tasks, TT, W['ffn2_gu'][l], W['ffn2_dn'][l])
                    layernorm(tasks, TT, l, 2)
                    if l == DEPTH - 1:
                        store_x_tile(tasks, y_prompt[j * TT:(j + 1) * TT, :], TT, subs512, 'y_prompt')
                    else:
                        store_x_tile(tasks, xmid[j * TT:(j + 1) * TT, :], TT, subs512, 'xmid')
                    run_tasks(tasks)
                if FLAGS.get('sample', True):
                    tasks = []
                    phaseC_tile(tasks, l, 0, 4, (x1s_s[:, :, :], [yas_s[:, :, :], ybs_s[:, :, :], ycs_s[:, :, :]]))
                    layernorm(tasks, 4, l, 1)
                    ffn(tasks, 4, W['ffn2_gu'][l], W['ffn2_dn'][l])
                    layernorm(tasks, 4, l, 2)
                    store_x_tile(tasks, y_sample if l == DEPTH - 1 else xmid_s, 4, [(0, 4)], 'y_sample' if l == DEPTH - 1 else 'xmid_s')
                    run_tasks(tasks)
                S.barrier()
        outs = [k for k in S.bufs if k.startswith('OUT_')]
        S.wait_all('sp', outs)
        print(f"[build] ops={S.nops} waits={S.nwaits} sems={S.nsem}")
    return nc


def host_consts():
    c = {}
    c['ident'] = np.eye(128, dtype=np.float32)
    c['onesm'] = np.full((128, 128), 1.0 / D, dtype=np.float32)
    mm = np.zeros((128, 16, 8), np.float32)
    for g2 in range(2):
        for sbi in range(16):
            mm[g2 * 64:(g2 + 1) * 64, sbi, (2 * sbi + g2) % 8] = 1.0
    c['mmask'] = mm
    c['iota128'] = np.tile(np.arange(1, LCH + 1, dtype=np.float32)[None, :], (128, 1))
    return c


def pack_s5(inputs):
    out = np.zeros((DEPTH, 128, S5X), np.float32)
    for l in range(DEPTH):
        def gp(a):
            return a.reshape(16, 2, 64).transpose(1, 2, 0).reshape(128, 16)
        out[l, :, 0:16] = gp(inputs['s5_lambda_re'][l])
        out[l, :, 16:32] = gp(inputs['s5_lambda_im'][l])
        out[l, :, 32:48] = gp(np.repeat(inputs['s5_log_dt'][l][:, None], 64, axis=1))
        out[l, :, 48:304] = inputs['s5_b_re'][l].reshape(16, 2, 64, 16).transpose(1, 2, 0, 3).reshape(128, 256)
        out[l, :, 304:560] = inputs['s5_b_im'][l].reshape(16, 2, 64, 16).transpose(1, 2, 0, 3).reshape(128, 256)
        out[l, :, 560:816] = inputs['s5_c_re'][l].reshape(16, 2, 16, 64).transpose(1, 3, 0, 2).reshape(128, 256)
        out[l, :, 816:1072] = inputs['s5_c_im'][l].reshape(16, 2, 16, 64).transpose(1, 3, 0, 2).reshape(128, 256)
        out[l, :, 1072:1076] = inputs['s5_d'][l].reshape(4, 128).T
        out[l, :, 1076:1080] = inputs['s5_b_glu'][l].reshape(4, 128).T
        out[l, :, 1080:1092] = inputs['conv_w'][l].reshape(3, 4, 128).transpose(2, 0, 1).reshape(128, 12)
    return out


HP_ = [0, 4, 1, 5, 2, 6, 3, 7]


def nsa_consts():
    c = {}
    half = 8
    inv_freq = (np.float32(500000.0) ** (-np.arange(half, dtype=np.float32) / np.float32(half))).astype(np.float32)
    npr = np.arange(256)
    pos = (16 * npr + 15).astype(np.float32)
    ang = (pos[:, None] * inv_freq[None, :]).astype(np.float32)
    c['ropeCc'] = np.ascontiguousarray(np.cos(ang).astype(np.float32).reshape(8, 32, 8).transpose(1, 0, 2))
    c['ropeCs'] = np.ascontiguousarray(np.sin(ang).astype(np.float32).reshape(8, 32, 8).transpose(1, 0, 2))
    jb = np.arange(64)
    mp = ((npr[:, None] >= 4 * jb[None, :]) & (npr[:, None] <= 4 * jb[None, :] + 4)).astype(np.float32)
    c['mapc'] = np.ascontiguousarray(mp.reshape(8, 32, 64).transpose(1, 0, 2))
    NEG = np.float32(-30000.0)
    r = np.arange(128)[:, None]
    cc = np.arange(TT)[None, :]
    wm = np.zeros((128, 6, TT), np.float32)
    for di, d in enumerate(range(-4, 2)):
        dp = cc - (128 * d + r)
        wm[:, di, :] = np.where((dp >= 0) & (dp < 512), 0.0, NEG)
    c['wmask'] = wm.reshape(128, 6 * TT)
    r32 = np.arange(32)[:, None]
    cm = np.zeros((32, 5, TT), np.float32)
    cm[:, 0, :] = np.where(16 * r32 + 15 <= cc, 0.0, NEG)
    cm[:, 1, :] = np.where(16 * r32 + 15 <= cc + 256, 0.0, NEG)
    cm[:, 2, :] = cm[:, 0, :]
    cm[0, 2, :] = NEG
    cm[:, 3, :] = cm[:, 1, :]
    cm[0, 3, :] = NEG
    cm[0, 4, :] = NEG
    c['cmask'] = cm.reshape(32, 5 * TT)
    k = np.arange(T)[None, :]
    c['ebig'] = (k // 64 == np.arange(64)[:, None]).astype(np.float32)
    q = np.arange(T)[:, None]
    cur = q // 64
    blk = np.arange(64)[None, :]
    valid = blk <= cur
    forced = valid & ((blk == 0) | (blk >= cur - 1))
    c['fbias'] = np.where(forced, np.float32(1e9), np.where(valid, np.float32(0.0), np.float32(-1e30))).astype(np.float32)
    return c


def rope_tables():
    half = 8
    inv_freq = (np.float32(500000.0) ** (-np.arange(half, dtype=np.float32) / np.float32(half))).astype(np.float32)
    pos = np.arange(T, dtype=np.float32)
    ang = (pos[:, None] * inv_freq[None, :]).astype(np.float32)
    c = np.cos(ang).astype(np.float32).reshape(32, 128, 8).transpose(1, 0, 2)
    s_ = np.sin(ang).astype(np.float32).reshape(32, 128, 8).transpose(1, 0, 2)
    return np.ascontiguousarray(c), np.ascontiguousarray(s_)


def sample_consts():
    c = {}
    half = 8
    inv_freq = (np.float32(500000.0) ** (-np.arange(half, dtype=np.float32) / np.float32(half))).astype(np.float32)
    ang = (np.float32(16384.0) * inv_freq).astype(np.float32)
    rs = np.zeros((4, 2, 8), np.float32)
    rs[:, 0, :] = np.cos(ang).astype(np.float32)[None, :]
    rs[:, 1, :] = np.sin(ang).astype(np.float32)[None, :]
    c['ropeS'] = rs
    lohi = np.zeros((1, 256), np.float32)
    lohi[0, 0:64] = 1.0
    lohi[0, 128 + 64:256] = 1.0
    c['lohi'] = lohi
    fb = np.zeros((1, 264), np.float32)
    fb[0, [0, 255, 256]] = 1e9
    c['fbias_s'] = fb
    ml = np.zeros((32, 16), np.float32)
    r = np.arange(32)[:, None]
    i = np.arange(9)[None, :]
    ml[:, 0:9] = ((r >= 4 * (i - 1)) & (r <= 4 * (i - 1) + 4)).astype(np.float32)
    c['maploc'] = ml
    c['pcol'] = (np.arange(128, dtype=np.float32)[:, None] + np.arange(DEPTH, dtype=np.float32)[None, :] * np.float32(5120 * 128)).astype(np.float32)
    npr = np.arange(1024)
    pos = (16 * npr + 15).astype(np.float32)
    ang2 = (pos[:, None] * inv_freq[None, :]).astype(np.float32)
    c['ropeCc2'] = np.ascontiguousarray(np.cos(ang2).astype(np.float32).reshape(32, 32, 8).transpose(1, 0, 2))
    c['ropeCs2'] = np.ascontiguousarray(np.sin(ang2).astype(np.float32).reshape(32, 32, 8).transpose(1, 0, 2))
    return c


def kernel(**inputs):
    dbg = inputs.pop('_dbg', None)
    ntile = inputs.pop('_ntile', NTILE)
    nc = build_program(dbg, ntile)
    consts = host_consts()
    lng = np.ascontiguousarray(inputs['ln_g'].reshape(DEPTH * 3, 8, 128).transpose(2, 0, 1).reshape(128, 48))
    lnb = np.ascontiguousarray(inputs['ln_b'].reshape(DEPTH * 3, 8, 128).transpose(2, 0, 1).reshape(128, 48))
    rc, rs_ = rope_tables()
    s5pk = pack_s5(inputs)
    nsac = nsa_consts()
    smc = sample_consts()
    qperm = np.concatenate([np.arange(512 + h * 64, 512 + (h + 1) * 64) for h in HP_])
    w_in_p = np.array(inputs['w_in'], copy=True)
    w_in_p[:, :, 512:1024] = inputs['w_in'][:, :, qperm]
    bperm = np.concatenate([np.arange(h * 64, (h + 1) * 64) for h in HP_])
    w_pb_p = np.ascontiguousarray(inputs['w_proj_b'][:, bperm, :])
    peTk = np.ascontiguousarray(inputs['nsa_pe_k'].transpose(0, 2, 1))
    peTv = np.ascontiguousarray(inputs['nsa_pe_v'].transpose(0, 2, 1))
    use_sample = FLAGS.get('sample', True)
    npool = inputs['cache_cmp_kv'].shape[1]
    ccmp = inputs['cache_cmp_kv'].reshape(DEPTH, npool, 128, 256)
    csel = inputs['cache_sel_kv'].reshape(DEPTH, npool, 128, 256)
    in_maps = []
    for c in range(NCORES):
        m = dict(consts)
        m['xp'] = np.ascontiguousarray(inputs['x_prompt'][c])
        m['lng'] = lng
        m['lnb'] = lnb
        for k in ['ffn1_w_gu', 'ffn1_w_down', 'ffn2_w_gu', 'ffn2_w_down', 's5_w_glu',
                  'nsa_phi_k1', 'nsa_phi_k2', 'nsa_phi_v1', 'nsa_phi_v2', 'w_proj_a', 'w_proj_c', 'w_o']:
            m[k] = inputs[k]
        m['w_in'] = w_in_p
        m['w_proj_b'] = w_pb_p
        m['ropec'], m['ropes'] = rc, rs_
        m['s5p'] = s5pk
        m.update(nsac)
        m['peTk'] = peTk
        m['peTv'] = peTv
        m.update(smc)
        sl = slice(4 * c, 4 * c + 4)
        m['xs'] = np.ascontiguousarray(inputs['x_sample'][sl, 0, :])
        m['cache_cmp'] = ccmp
        m['cache_sel'] = csel
        m['cwin'] = np.ascontiguousarray(inputs['cache_win_kv'][:, sl].reshape(DEPTH, 4, 512, 256))
        m['cconv'] = np.ascontiguousarray(inputs['cache_conv'][:, sl])
        m['s5re0'] = np.ascontiguousarray(inputs['state_s5_re'][:, sl].reshape(DEPTH, 4, 16, 128))
        m['s5im0'] = np.ascontiguousarray(inputs['state_s5_im'][:, sl].reshape(DEPTH, 4, 16, 128))
        m['ptab'] = np.ascontiguousarray(inputs['page_table'][sl].astype(np.int32))
        in_maps.append(m)
    res = run_bass_kernel_spmd(nc, in_maps, core_ids=list(range(NCORES)))
    R = res.results
    DEBUG['res'] = R

    def st(name, axis):
        return np.stack([R[c][name] for c in range(NCORES)], axis=axis)

    def cat(name, axis):
        return np.concatenate([R[c][name] for c in range(NCORES)], axis=axis)
    y_prompt = st('y_prompt', 0)
    y_sample = cat('y_sample', 0).reshape(32, 1, D)
    cmp_p = st('cmp_rows_p', 1).reshape(DEPTH, NCORES, T, 2, 2, 64)
    cmp_s = cat('cmp_rows_s', 1).reshape(DEPTH, 32, 1, 2, 2, 64)
    sel_p = st('sel_rows_p', 1).reshape(DEPTH, NCORES, T, 2, 2, 64)
    sel_s = cat('sel_rows_s', 1).reshape(DEPTH, 32, 1, 2, 2, 64)
    win_p = st('win_p', 1).reshape(DEPTH, NCORES, 512, 2, 2, 64)
    win_s = cat('win_so', 1).reshape(DEPTH, 32, 512, 2, 2, 64)
    conv_p = st('conv_p', 1)
    conv_s = cat('conv_so', 1)
    re_p = st('s5re_p', 1).reshape(DEPTH, NCORES, 32, 64)
    re_s = cat('s5re_so', 1).reshape(DEPTH, 32, 32, 64)
    im_p = st('s5im_p', 1).reshape(DEPTH, NCORES, 32, 64)
    im_s = cat('s5im_so', 1).reshape(DEPTH, 32, 32, 64)
    outs = (y_prompt, y_sample, cmp_p, cmp_s, sel_p, sel_s, win_p, win_s, conv_p, conv_s, re_p, re_s, im_p, im_s)
    return tuple(np.ascontiguousarray(o, dtype=np.float32) for o in outs)
```

```python
import numpy as np
from contextlib import ExitStack
import concourse.bass as bass
import concourse.mybir as mybir
from concourse.bass_utils import run_bass_kernel_spmd

F32 = mybir.dt.float32
BF16 = mybir.dt.bfloat16
I32 = mybir.dt.int32
AF = mybir.ActivationFunctionType
ALU = mybir.AluOpType

NCORES = 8
D = 1024
T = 4096
DFF = 2816
DIN = 6424
DEPTH = 2
TT = 256
NSUB = TT // 128
LCH = 128
S5X = 1092
NTILE = T // TT
ALPHA = (2.0 * DEPTH) ** 0.25
EPS = 1e-5

DEBUG = {}
FLAGS = {'ffn': True, 'ln': True}


class Sched:
    SEM_ROLL = 30000

    def __init__(self, nc, es):
        self.nc = nc
        self.es = es
        self.engs = {'pe': nc.tensor, 'dve': nc.vector, 'act': nc.scalar, 'pool': nc.gpsimd, 'sp': nc.sync}
        self.cur = {}
        self.known = {e: {} for e in self.engs}
        self.bufs = {}
        self.nsem = 0
        self.nops = 0
        self.nwaits = 0

    def _newsem(self, name):
        self.nsem += 1
        return self.es.enter_context(self.nc.semaphore(f"s{self.nsem}_{name}"))

    def _tick(self, stream, inc):
        c = self.cur.get(stream)
        if c is None or c[1] + inc > self.SEM_ROLL:
            c = [self._newsem(stream), 0]
            self.cur[stream] = c
        c[1] += inc
        return (c[0], c[1])

    def _deps(self, eng, reads, writes, tag=None):
        need = {}

        def add(tok, kind):
            sem, val, teng = tok[0], tok[1], tok[2]
            if teng == eng and eng == 'pe':
                if not (kind == 'waw' and len(tok) > 3 and tok[3] != tag):
                    return
            if val > need.get(id(sem), (None, 0))[1]:
                need[id(sem)] = (sem, val)

        for k in reads:
            b = self.bufs.get(k)
            if b is not None and b[0] is not None:
                add(b[0], 'raw')
            if b is not None and k.startswith('bank'):
                for r in b[1]:
                    add(r, 'war')
        for k in writes:
            b = self.bufs.get(k)
            if b is not None:
                if b[0] is not None:
                    add(b[0], 'waw')
                for r in b[1]:
                    add(r, 'war')
        kn = self.known[eng]
        e = self.engs[eng]
        for sid, (sem, val) in need.items():
            if kn.get(sid, 0) < val:
                e.wait_ge(sem, val)
                kn[sid] = val
                self.nwaits += 1

    def _post(self, tok, reads, writes):
        for k in reads:
            b = self.bufs.setdefault(k, [None, []])
            b[1].append(tok)
            if len(b[1]) > 64:
                b[1] = self._compact(b[1])
        for k in writes:
            self.bufs[k] = [tok, []]
        self.nops += 1

    @staticmethod
    def _compact(toks):
        best = {}
        for t in toks:
            k = (id(t[0]), t[2])
            if k not in best or best[k][1] < t[1]:
                best[k] = t
        return list(best.values())

    def op(self, eng, fn, reads=(), writes=(), tag=None):
        self._deps(eng, reads, writes, tag)
        inst = fn(self.engs[eng])
        sem, val = self._tick(eng, 1)
        inst.then_inc(sem, 1)
        self._post((sem, val, eng, tag), reads, writes)

    def dma(self, q, slot, out, in_, reads=(), writes=(), **kw):
        self._deps(q, reads, writes)
        self._own(q, 'dma_' + slot)
        inst = self.engs[q].dma_start(out=out, in_=in_, **kw)
        sem, val = self._tick('dma_' + slot, 16)
        inst.then_inc(sem, 16)
        self._post((sem, val, 'dma_' + slot), reads, writes)

    def _own(self, q, stream):
        c = self.cur.get(stream)
        if c is not None and c[1] > 0 and self.known[q].get(id(c[0]), 0) < c[1]:
            self.engs[q].wait_ge(c[0], c[1])
            self.known[q][id(c[0])] = c[1]

    def idma(self, slot, out, in_, idx_ap, reads=(), writes=()):
        self._deps('pool', reads, writes)
        self._own('pool', 'dma_' + slot)
        inst = self.nc.gpsimd.indirect_dma_start(out=out, out_offset=None, in_=in_, in_offset=bass.IndirectOffsetOnAxis(ap=idx_ap, axis=0))
        sem, val = self._tick('dma_' + slot, 16)
        inst.then_inc(sem, 16)
        self._post((sem, val, 'dma_' + slot), reads, writes)

    def barrier(self):
        for eng, e in self.engs.items():
            kn = self.known[eng]
            for stream, (sem, cnt) in self.cur.items():
                if cnt > 0 and kn.get(id(sem), 0) < cnt:
                    e.wait_ge(sem, cnt)
                    kn[id(sem)] = cnt
        self.bufs = {k: v for k, v in self.bufs.items() if k.startswith('OUT_')}

    def wait_all(self, eng, keys):
        self._deps(eng, keys, ())


class K:
    pass


def build_program(dbg=None, ntile=NTILE):
    dbg = dbg or {}
    nc = bass.Bass("TRN2", target_bir_lowering=False)
    g = K()
    g.nc = nc

    def din(name, shape, dt=F32):
        return nc.dram_tensor(name, list(shape), dt, kind="ExternalInput").ap()

    def dout(name, shape, dt=F32):
        return nc.dram_tensor(name, list(shape), dt, kind="ExternalOutput").ap()

    xp = din("xp", [T, D])
    ident_d = din("ident", [128, 128])
    onesm_d = din("onesm", [128, 128])
    lng_d = din("lng", [128, 48])
    lnb_d = din("lnb", [128, 48])
    W = {}
    if FLAGS['ffn']:
        W['ffn1_gu'] = din("ffn1_w_gu", [DEPTH, D, 2 * DFF])
        W['ffn1_dn'] = din("ffn1_w_down", [DEPTH, DFF, D])
        W['ffn2_gu'] = din("ffn2_w_gu", [DEPTH, D, 2 * DFF])
        W['ffn2_dn'] = din("ffn2_w_down", [DEPTH, DFF, D])
    if FLAGS.get('win', True):
        W['w_in'] = din("w_in", [DEPTH, D, DIN])
        ropec_d = din("ropec", [128, 32, 8])
        ropes_d = din("ropes", [128, 32, 8])
        s5p_d = din("s5p", [DEPTH, 128, S5X])
        mmask_d = din("mmask", [128, 16, 8])
        iota_d = din("iota128", [128, LCH])
        W['glu'] = din("s5_w_glu", [DEPTH, 512, 512])
    def dscr(name, shape, dt=F32):
        return nc.dram_tensor(name, list(shape), dt, kind="Internal").ap()
    x1_s = dscr("x1_s", [NTILE, 128, 8, TT])
    ya_s = dscr("ya_s", [NTILE, 128, 4, TT], BF16)
    yc_s = dscr("yc_s", [NTILE, 128, 4, TT], BF16)
    yb_s = dscr("yb_s", [NTILE, 128, 4, TT], BF16)
    qg_s = dscr("qg_s", [T, 536])
    win_s = dscr("win_s", [T, 256])
    xmid = dscr("xmid", [T, D])
    phik1_d = din("nsa_phi_k1", [DEPTH, 2048, 128])
    phik2_d = din("nsa_phi_k2", [DEPTH, 128, 64])
    phiv1_d = din("nsa_phi_v1", [DEPTH, 2048, 128])
    phiv2_d = din("nsa_phi_v2", [DEPTH, 128, 64])
    peTk_d = din("peTk", [DEPTH, 64, 32])
    peTv_d = din("peTv", [DEPTH, 64, 32])
    ropeCc_d = din("ropeCc", [32, 8, 8])
    ropeCs_d = din("ropeCs", [32, 8, 8])
    mapc_d = din("mapc", [32, 8, 64])
    wmask_d = din("wmask", [128, 6 * TT])
    cmask_d = din("cmask", [32, 5 * TT])
    ebig_d = din("ebig", [64, T])
    fbias_d = din("fbias", [T, 64])
    W['pa'] = din("w_proj_a", [DEPTH, 512, D])
    W['pb'] = din("w_proj_b", [DEPTH, 512, D])
    W['pc'] = din("w_proj_c", [DEPTH, 512, D])
    W['wo'] = din("w_o", [DEPTH, D, D])
    NPOOL = 5120
    xs_d = din("xs", [4, D])
    ccmp_d = din("cache_cmp", [DEPTH, NPOOL, 128, 256])
    csel_d = din("cache_sel", [DEPTH, NPOOL, 128, 256])
    cwin_d = din("cwin", [DEPTH, 4, 512, 256])
    cconv_d = din("cconv", [DEPTH, 4, 2, 512])
    s5re0_d = din("s5re0", [DEPTH, 4, 16, 128])
    s5im0_d = din("s5im0", [DEPTH, 4, 16, 128])
    ptab_d = din("ptab", [4, 128], I32)
    ropeS_d = din("ropeS", [4, 2, 8])
    lohi_d = din("lohi", [1, 256])
    fbs_d = din("fbias_s", [1, 264])
    maploc_d = din("maploc", [32, 16])
    pcol_d = din("pcol", [128, DEPTH])
    ropeCc2_d = din("ropeCc2", [32, 32, 8])
    ropeCs2_d = din("ropeCs2", [32, 32, 8])
    xmid_s = dscr("xmid_s", [4, D])
    x1s_s = dscr("x1s_s", [128, 8, 4])
    qgs_s = dscr("qgs_s", [4, 536])
    yas_s = dscr("yas_s", [128, 4, 4], BF16)
    ybs_s = dscr("ybs_s", [128, 4, 4], BF16)
    ycs_s = dscr("ycs_s", [128, 4, 4], BF16)
    ybr_s = dscr("ybr_s", [4, 512])
    y_sample = dout("y_sample", [4, D])
    cmp_rows_s = dout("cmp_rows_s", [DEPTH, 4, 256])
    sel_rows_s = dout("sel_rows_s", [DEPTH, 4, 256])
    win_so = dout("win_so", [DEPTH, 4, 512, 256])
    conv_so = dout("conv_so", [DEPTH, 4, 2, 512])
    s5re_so = dout("s5re_so", [DEPTH, 4, 16, 128])
    s5im_so = dout("s5im_so", [DEPTH, 4, 16, 128])
    y_prompt = dout("y_prompt", [T, D])
    cmp_rows_p = dout("cmp_rows_p", [DEPTH, T, 256])
    sel_rows_p = dout("sel_rows_p", [DEPTH, T, 256])
    win_p = dout("win_p", [DEPTH, 512, 256])
    conv_p = dout("conv_p", [DEPTH, 2, 512])
    s5re_p = dout("s5re_p", [DEPTH, 16, 128])
    s5im_p = dout("s5im_p", [DEPTH, 16, 128])
    dbg_out = {}
    for name, shape in dbg.items():
        dbg_out[name] = dout("dbg_" + name, shape)

    with ExitStack() as es:
        cur = [es]

        def sb(name, shape, dt=F32):
            return cur[0].enter_context(nc.sbuf_tensor("sb_" + name, list(shape), dt))

        def pst(name, shape, dt=F32):
            return es.enter_context(nc.psum_tensor("ps_" + name, list(shape), dt))

        ident = sb("ident", [128, 128])
        identb = sb("identb", [128, 128], BF16)
        onesm = sb("onesm", [128, 128])
        lng = sb("lng", [128, 48])
        lnb = sb("lnb", [128, 48])
        ropec = sb("ropec", [128, 32, 8])
        ropes = sb("ropes", [128, 32, 8])
        mmask = sb("mmask", [128, 16, 8])
        iota = sb("iota", [128, LCH])
        banks = [pst(f"bank{i}", [128, 512]) for i in range(8)]
        NSLOT = FLAGS.get('nslot', 3)
        NBANK = FLAGS.get('nbank', 7)
        xtok = xA = xAb = rS = hb = sg = lnm = lnv = lnr = wslot = big1 = st_big = big2 = bigi = sq = None
        tokq = rt = s5p = BTr = BTi = CTr = CTi = cosT = sinT = sLneg = mag = s5t = cre = cim = ctmp = None
        u32 = ub = bpr = bpi = rr = ri = st1 = st2 = srb = sib = yag = yagb = ya2b = cb32 = cc32 = vbuf = ctm = ycb = smallo = None
        uid = [0]

        SA = K()

        def alloc_ffn(stk):
            nonlocal xtok, xA, xAb, rS, hb, sg, lnm, lnv, lnr, wslot, big1, st_big, big2, bigi, sq
            uid[0] += 1
            u_ = f"_{uid[0]}"
            cur[0] = stk
            xtok = sb("xtok" + u_, [128, NSUB, D])
            xA = sb("xA" + u_, [128, 8, TT])
            xAb = sb("xAb" + u_, [128, 8, TT], BF16)
            rS = sb("rS" + u_, [128, 8, TT])
            hb = sb("hb" + u_, [128, 22, TT], BF16)
            sg = [sb(f"sg{i}" + u_, [128, TT]) for i in range(2)]
            lnm = sb("lnm" + u_, [128, TT])
            lnv = sb("lnv" + u_, [128, TT])
            lnr = sb("lnr" + u_, [128, TT])
            wslot = [sb(f"wslot{i}" + u_, [128, 8192], BF16) for i in range(NSLOT)]
            big1 = wslot[0][:, :].bitcast(F32)[:, 0:2048]
            st_big = wslot[0][:, :].bitcast(F32)[:, 2048:4096]
            big2 = wslot[1][:, :].bitcast(F32)[:, 0:2048]
            bigi = wslot[2][:, :].bitcast(I32)[:, 0:2048]
            sq = hb[:, 0:16, :].rearrange("p a b -> p (a b)").bitcast(F32).rearrange("p (k n) -> p k n", k=8)

        def alloc_A(stk):
            nonlocal tokq, rt, s5p, BTr, BTi, CTr, CTi, cosT, sinT, sLneg, mag, s5t, cre, cim, ctmp
            nonlocal u32, ub, bpr, bpi, rr, ri, st1, st2, srb, sib, yag, yagb, ya2b, cb32, cc32, vbuf, ctm, ycb, smallo
            uid[0] += 1
            u_ = f"_{uid[0]}"
            cur[0] = stk
            tokq = sb("tokq" + u_, [128, NSUB, 1304])
            rt = [sb(f"rt{i}" + u_, [128, NSUB, 8, 8]) for i in range(4)]
            s5p = sb("s5p" + u_, [128, S5X])
            BTr = sb("BTr" + u_, [128, 16, 128], BF16)
            BTi = sb("BTi" + u_, [128, 16, 128], BF16)
            CTr = sb("CTr" + u_, [128, 16, 128], BF16)
            CTi = sb("CTi" + u_, [128, 16, 128], BF16)
            cosT = sb("cosT" + u_, [128, 16, LCH])
            sinT = sb("sinT" + u_, [128, 16, LCH])
            sLneg = sb("sLneg" + u_, [128, 16])
            mag = sb("mag" + u_, [128, 16])
            s5t = [sb(f"s5t{i}" + u_, [128, 16]) for i in range(8)]
            cre = sb("cre" + u_, [128, 16])
            cim = sb("cim" + u_, [128, 16])
            ctmp = sb("ctmp" + u_, [128, 2])
            u32 = sb("u32" + u_, [128, 4, TT])
            ub = sb("ub" + u_, [128, 4, TT], BF16)
            bpr = sb("bpr" + u_, [128, TT])
            bpi = sb("bpi" + u_, [128, TT])
            rr = sb("rr" + u_, [128, TT])
            ri = sb("ri" + u_, [128, TT])
            st1 = sb("st1" + u_, [128, TT])
            st2 = sb("st2" + u_, [128, TT])
            srb = sb("srb" + u_, [128, TT], BF16)
            sib = sb("sib" + u_, [128, TT], BF16)
            yag = sb("yag" + u_, [128, 4, TT])
            yagb = sb("yagb" + u_, [128, 4, TT], BF16)
            ya2b = sb("ya2b" + u_, [128, 4, TT], BF16)
            cb32 = sb("cb32" + u_, [128, 4, TT])
            cc32 = sb("cc32" + u_, [128, 4, TT])
            vbuf = sb("vbuf" + u_, [128, 4, TT + 2])
            ctm = sb("ctm" + u_, [128, TT])
            ycb = sb("ycb" + u_, [128, 4, TT], BF16)
            smallo = sb("smallo" + u_, [16, 512])
            SA.ropeS = sb("ropeS" + u_, [4, 2, 8])
            SA.s0t = sb("s0t" + u_, [16, 2, 4, 128])
            SA.s0 = sb("s0" + u_, [128, 2, 16, 4])
            SA.sn = sb("sn" + u_, [128, 2, 16, 4])
            SA.snb = sb("snb" + u_, [128, 2, 16, 4], BF16)
            SA.t = [sb(f"sat{i}" + u_, [128, 16, 4]) for i in range(2)]
            SA.cvt = sb("cvt" + u_, [8, 512])
            SA.cbuf = sb("cbuf" + u_, [128, 4, 4, 2])
            SA.vn = sb("vn" + u_, [128, 4, 4])

        es.enter_context(nc.Block())
        S = Sched(nc, es)
        g.S = S
        bank_i = [0]

        def bank():
            i = bank_i[0] % NBANK
            bank_i[0] += 1
            return banks[i], f"bank{i}"

        YB = banks[7]
        YBK = 'bank7'

        S.dma('sp', 'c0', ident[:], ident_d[:, :], writes=['ident'])
        S.dma('sp', 'c1', onesm[:], onesm_d[:, :], writes=['onesm'])
        S.dma('sp', 'c2', lng[:], lng_d[:, :], writes=['lng'])
        S.dma('sp', 'c3', lnb[:], lnb_d[:, :], writes=['lnb'])
        S.op('dve', lambda e: e.tensor_copy(out=identb[:], in_=ident[:]), reads=['ident'], writes=['identb'])
        if FLAGS.get('win', True):
            S.dma('sp', 'c4', ropec[:], ropec_d[:, :, :], writes=['ropec'])
            S.dma('sp', 'c5', ropes[:], ropes_d[:, :, :], writes=['ropes'])
            S.dma('sp', 'c6', mmask[:], mmask_d[:, :, :], writes=['mmask'])
            S.dma('sp', 'c7', iota[:], iota_d[:, :], writes=['iota'])

        wq = []
        wstate = {'issued': 0, 'used': 0}

        def w_issue(spec):
            Wap, k0, nk, c0, ncols = spec
            s = wstate['issued'] % NSLOT
            wstate['issued'] += 1
            view = wslot[s][:, 0:nk * ncols].rearrange("p (k c) -> p k c", k=nk)
            src = Wap[k0 * 128:(k0 + nk) * 128, c0:c0 + ncols].rearrange("(k p) c -> p k c", p=128)
            S.dma('pool', f'w{s}', view, src, writes=[f'wslot{s}'])

        def run_tasks(tasks):
            norm = []
            for (spec, fn) in tasks:
                if spec is None:
                    norm.append(([], fn, 0))
                elif isinstance(spec, list):
                    norm.append((spec, fn, 2))
                else:
                    norm.append(([spec], fn, 1))
            wl = [sp for (specs, _, _) in norm for sp in specs]
            first = 0
            issued = 0
            for (specs, fn, mode) in norm:
                if mode == 0:
                    fn()
                    continue
                assert len(specs) <= NSLOT
                while issued < len(wl) and issued < first + NSLOT:
                    w_issue(wl[issued])
                    issued += 1
                views, keys = [], []
                for sp in specs:
                    s = wstate['used'] % NSLOT
                    wstate['used'] += 1
                    Wap, k0, nk, c0, ncols = sp
                    views.append(wslot[s][:, 0:nk * ncols].rearrange("p (k c) -> p k c", k=nk))
                    keys.append(f'wslot{s}')
                if mode == 1:
                    fn(views[0], keys[0])
                else:
                    fn(views, keys)
                first += len(specs)
            assert wstate['used'] == wstate['issued']

        def load_x_tile(tasks, src_ap, N, subs, rkeys=()):
            def fn():
                for si, (r0, nr) in enumerate(subs):
                    S.dma('sp', f'xin{si}', xtok[0:nr, si, :], src_ap[r0:r0 + nr, :], reads=list(rkeys), writes=[f'xtok{si}'])
                for k in range(8):
                    bk, bkey = bank()
                    for si, (r0, nr) in enumerate(subs):
                        S.op('pe', lambda e, si=si, nr=nr, k=k, bk=bk: e.transpose(
                            out=bk[:, si * 128:si * 128 + nr], in_=xtok[0:nr, si, k * 128:(k + 1) * 128], identity=ident[0:nr, 0:nr]),
                            reads=[f'xtok{si}', 'ident'], writes=[bkey])
                    S.op('act', lambda e, k=k, bk=bk: e.copy(out=xA[:, k, 0:N], in_=bk[:, 0:N]), reads=[bkey], writes=[f'xA{k}'])
                    S.op('dve', lambda e, k=k, bk=bk: e.tensor_copy(out=xAb[:, k, 0:N], in_=bk[:, 0:N]), reads=[bkey], writes=[f'xAb{k}'])
            tasks.append((None, fn))

        def store_x_tile(tasks, dst_ap, N, subs, oname):
            def fn():
                for si, (r0, nr) in enumerate(subs):
                    for k in range(8):
                        if k % 4 == 0:
                            bk, bkey = bank()
                        S.op('pe', lambda e, si=si, nr=nr, k=k, bk=bk: e.transpose(
                            out=bk[0:nr, (k % 4) * 128:(k % 4 + 1) * 128], in_=xA[:, k, si * 128:si * 128 + nr], identity=ident[:, :]),
                            reads=[f'xA{k}', 'ident'], writes=[bkey])
                        if k % 4 == 3:
                            h0 = (k // 4) * 512
                            eng = 'act' if (k // 4) == 0 else 'dve'
                            if eng == 'act':
                                S.op('act', lambda e, si=si, nr=nr, h0=h0, bk=bk: e.copy(out=xtok[0:nr, si, h0:h0 + 512], in_=bk[0:nr, :]),
                                     reads=[bkey], writes=[f'xtok{si}'])
                            else:
                                S.op('dve', lambda e, si=si, nr=nr, h0=h0, bk=bk: e.tensor_copy(out=xtok[0:nr, si, h0:h0 + 512], in_=bk[0:nr, :]),
                                     reads=[bkey], writes=[f'xtok{si}'])
                    S.dma('sp', f'xout{si}', dst_ap[r0:r0 + nr, :], xtok[0:nr, si, :],
                          reads=[f'xtok{si}'], writes=['OUT_' + oname])
            tasks.append((None, fn))

        def layernorm(tasks, N, l, i):
            def fn():
                col = (l * 3 + i) * 8
                S.op('act', lambda e: e.activation(out=sq[:, :, 0:N], in_=rS[:, :, 0:N], func=AF.Square),
                     reads=[f'rS{k}' for k in range(8)], writes=['sq'])
                b1, b1k = bank()
                for k in range(8):
                    S.op('pe', lambda e, k=k: e.matmul(b1[:, 0:N], lhsT=onesm[:], rhs=rS[:, k, 0:N], start=(k == 0), stop=(k == 7)),
                         reads=[f'rS{k}', 'onesm'], writes=[b1k])
                b2, b2k = bank()
                for k in range(8):
                    S.op('pe', lambda e, k=k: e.matmul(b2[:, 0:N], lhsT=onesm[:], rhs=sq[:, k, 0:N], start=(k == 0), stop=(k == 7)),
                         reads=['sq', 'onesm'], writes=[b2k])
                S.op('act', lambda e: e.copy(out=lnm[:, 0:N], in_=b1[:, 0:N]), reads=[b1k], writes=['lnm'])
                S.op('dve', lambda e: e.tensor_tensor(out=lnv[:, 0:N], in0=lnm[:, 0:N], in1=lnm[:, 0:N], op=ALU.mult), reads=['lnm'], writes=['lnv'])
                S.op('dve', lambda e: e.tensor_tensor(out=lnv[:, 0:N], in0=b2[:, 0:N], in1=lnv[:, 0:N], op=ALU.subtract), reads=[b2k, 'lnv'], writes=['lnv'])
                S.op('dve', lambda e: e.tensor_scalar(out=lnv[:, 0:N], in0=lnv[:, 0:N], scalar1=0.0, scalar2=EPS, op0=ALU.max, op1=ALU.add),
                     reads=['lnv'], writes=['lnv'])
                S.op('act', lambda e: e.activation(out=lnr[:, 0:N], in_=lnv[:, 0:N], func=AF.Sqrt), reads=['lnv'], writes=['lnr'])
                S.op('dve', lambda e: e.reciprocal(out=lnr[:, 0:N], in_=lnr[:, 0:N]), reads=['lnr'], writes=['lnr'])
                for k in range(8):
                    S.op('dve', lambda e, k=k: e.tensor_tensor(out=rS[:, k, 0:N], in0=rS[:, k, 0:N], in1=lnm[:, 0:N], op=ALU.subtract),
                         reads=[f'rS{k}', 'lnm'], writes=[f'rS{k}'])
                    S.op('pool', lambda e, k=k: e.tensor_tensor(out=rS[:, k, 0:N], in0=rS[:, k, 0:N], in1=lnr[:, 0:N], op=ALU.mult),
                         reads=[f'rS{k}', 'lnr'], writes=[f'rS{k}'])
                    S.op('act', lambda e, k=k: e.activation(out=xA[:, k, 0:N], in_=rS[:, k, 0:N], func=AF.Identity,
                                                            scale=lng[:, col + k:col + k + 1], bias=lnb[:, col + k:col + k + 1]),
                         reads=[f'rS{k}', 'lng', 'lnb'], writes=[f'xA{k}'])
                    S.op('dve', lambda e, k=k: e.tensor_copy(out=xAb[:, k, 0:N], in_=xA[:, k, 0:N]), reads=[f'xA{k}'], writes=[f'xAb{k}'])
            tasks.append((None, fn))

        def ffn(tasks, N, Wgu, Wdn):
            blocks = [(0, 1024), (1024, 1024), (2048, 768)]
            for (c0, ncols) in blocks:
                def fgu(views, keys, c0=c0, ncols=ncols):
                    gv, view = views
                    gk, key = keys
                    for m in range(ncols // 128):
                        j = c0 // 128 + m
                        bg, bgk = bank()
                        for k in range(8):
                            S.op('pe', lambda e, k=k, m=m: e.matmul(bg[:, 0:N], lhsT=gv[:, k, m * 128:(m + 1) * 128], rhs=xAb[:, k, 0:N],
                                                                     start=(k == 0), stop=(k == 7)), reads=[gk, f'xAb{k}'], writes=[bgk])
                        bu, buk = bank()
                        for k in range(8):
                            S.op('pe', lambda e, k=k, m=m: e.matmul(bu[:, 0:N], lhsT=view[:, k, m * 128:(m + 1) * 128], rhs=xAb[:, k, 0:N],
                                                                     start=(k == 0), stop=(k == 7)), reads=[key, f'xAb{k}'], writes=[buk])
                        sgi = j % 2
                        S.op('act', lambda e, sgi=sgi: e.activation(out=sg[sgi][:, 0:N], in_=bg[:, 0:N], func=AF.Silu), reads=[bgk], writes=[f'sg{sgi}'])
                        S.op('dve', lambda e, sgi=sgi, j=j: e.tensor_tensor(out=hb[:, j, 0:N], in0=bu[:, 0:N], in1=sg[sgi][:, 0:N], op=ALU.mult),
                             reads=[buk, f'sg{sgi}'], writes=[f'hb{j}'])
                tasks.append(([(Wgu, 0, 8, c0, ncols), (Wgu, 0, 8, DFF + c0, ncols)], fgu))
            for cb in range(4):
                def fd(view, key, cb=cb):
                    for m in range(2):
                        kk = cb * 2 + m
                        bd, bdk = bank()
                        for k in range(22):
                            S.op('pe', lambda e, k=k, m=m: e.matmul(bd[:, 0:N], lhsT=view[:, k, m * 128:(m + 1) * 128], rhs=hb[:, k, 0:N],
                                                                     start=(k == 0), stop=(k == 21)), reads=[key, f'hb{k}'], writes=[bdk])
                        S.op('act', lambda e, kk=kk: e.activation(out=rS[:, kk, 0:N], in_=xA[:, kk, 0:N], func=AF.Copy, scale=ALPHA),
                             reads=[f'xA{kk}'], writes=[f'rS{kk}'])
                        S.op('dve', lambda e, kk=kk: e.scalar_tensor_tensor(out=rS[:, kk, 0:N], in0=bd[:, 0:N], scalar=0.5, in1=rS[:, kk, 0:N],
                                                                              op0=ALU.mult, op1=ALU.add), reads=[bdk, f'rS{kk}'], writes=[f'rS{kk}'])
                tasks.append(((Wdn, 0, 22, cb * 256, 256), fd))

        def win_tokmajor(tasks, l, j):
            Win = W['w_in'][l]
            tkeys = [f'tokq{s_}' for s_ in range(NSUB)]

            def proj(view, key, c_lo, pieces):
                for sub in range(NSUB):
                    for (p0, pn) in pieces:
                        bk, bkey = bank()
                        for k in range(8):
                            S.op('pe', lambda e, k=k, sub=sub, p0=p0, pn=pn, bk=bk: e.matmul(
                                bk[:, 0:pn], lhsT=xAb[:, k, sub * 128:(sub + 1) * 128], rhs=view[:, k, p0:p0 + pn],
                                start=(k == 0), stop=(k == 7)), reads=[key, f'xAb{k}'], writes=[bkey])
                        S.op('act', lambda e, sub=sub, p0=p0, pn=pn, bk=bk: e.copy(out=tokq[:, sub, c_lo + p0:c_lo + p0 + pn], in_=bk[:, 0:pn]),
                             reads=[bkey], writes=[f'tokq{sub}'])
            tasks.append(((Win, 0, 8, 512, 1024), lambda view, key: proj(view, key, 0, [(0, 512), (512, 512)])))
            tasks.append(((Win, 0, 8, 1536, 280), lambda view, key: proj(view, key, 1024, [(0, 280)])))

            def rope_and_store():
                t0 = j * TT
                S.dma('sp', 'ocmp', cmp_rows_p[l, t0:t0 + TT, :].rearrange("(s p) c -> p s c", p=128), tokq[:, :, 512:768],
                      reads=tkeys, writes=['OUT_cmp_p'])
                cosb = lambda nh: ropec[:, j * NSUB:(j + 1) * NSUB, :].unsqueeze(2).broadcast_to([128, NSUB, nh, 8])
                sinb = lambda nh: ropes[:, j * NSUB:(j + 1) * NSUB, :].unsqueeze(2).broadcast_to([128, NSUB, nh, 8])
                for (c0, nh) in [(0, 8), (768, 2), (1024, 2)]:
                    V = tokq[:, :, c0:c0 + nh * 64].rearrange("p s (h d) -> p s h d", h=nh)
                    x1 = V[:, :, :, 0:8]
                    x2 = V[:, :, :, 8:16]
                    tm = [rt[i][:, :, 0:nh, :] for i in range(4)]
                    S.op('dve', lambda e: e.tensor_tensor(out=tm[0], in0=x1, in1=cosb(nh), op=ALU.mult), reads=tkeys + ['ropec'], writes=['rt0'])
                    S.op('dve', lambda e: e.tensor_tensor(out=tm[1], in0=x2, in1=sinb(nh), op=ALU.mult), reads=tkeys + ['ropes'], writes=['rt1'])
                    S.op('dve', lambda e: e.tensor_tensor(out=tm[2], in0=x2, in1=cosb(nh), op=ALU.mult), reads=tkeys + ['ropec'], writes=['rt2'])
                    S.op('dve', lambda e: e.tensor_tensor(out=tm[3], in0=x1, in1=sinb(nh), op=ALU.mult), reads=tkeys + ['ropes'], writes=['rt3'])
                    S.op('dve', lambda e: e.tensor_tensor(out=x1, in0=tm[0], in1=tm[1], op=ALU.subtract), reads=['rt0', 'rt1'], writes=tkeys)
                    S.op('dve', lambda e: e.tensor_tensor(out=x2, in0=tm[2], in1=tm[3], op=ALU.add), reads=['rt2', 'rt3'], writes=tkeys)
                S.dma('sp', 'osel', sel_rows_p[l, t0:t0 + TT, :].rearrange("(s p) c -> p s c", p=128), tokq[:, :, 768:1024],
                      reads=tkeys, writes=['OUT_sel_p'])
                nwt = 512 // TT
                if j >= NTILE - nwt:
                    w0 = (j - (NTILE - nwt)) * TT
                    S.dma('sp', 'owin', win_p[l, w0:w0 + TT, :].rearrange("(s p) c -> p s c", p=128), tokq[:, :, 1024:1280],
                          reads=tkeys, writes=['OUT_win_p'])
            tasks.append((None, rope_and_store))

        PI = 3.14159265358979

        def sincos(out_ap, th_ap, shift, F, keys_in, key_out):
            t1 = big1[:, 0:F]
            t2 = big2[:, 0:F]
            ti = bigi[:, 0:F]
            S.op('dve', lambda e: e.tensor_scalar(out=t1, in0=th_ap, scalar1=1.0 / (2 * PI), scalar2=0.5 + shift / (2 * PI), op0=ALU.mult, op1=ALU.add),
                 reads=keys_in, writes=['wslot0'])
            S.op('dve', lambda e: e.tensor_copy(out=ti, in_=t1), reads=['wslot0'], writes=['wslot2'])
            S.op('dve', lambda e: e.tensor_copy(out=t1, in_=ti), reads=['wslot2'], writes=['wslot0'])
            S.op('dve', lambda e: e.scalar_tensor_tensor(out=t1, in0=t1, scalar=-2 * PI, in1=th_ap, op0=ALU.mult, op1=ALU.add),
                 reads=['wslot0'] + keys_in, writes=['wslot0'])
            if shift != 0.0:
                S.op('dve', lambda e: e.tensor_scalar(out=t1, in0=t1, scalar1=shift, scalar2=None, op0=ALU.add), reads=['wslot0'], writes=['wslot0'])
            S.op('dve', lambda e: e.tensor_scalar(out=t2, in0=t1, scalar1=-PI, scalar2=2 * PI, op0=ALU.is_lt, op1=ALU.mult), reads=['wslot0'], writes=['wslot1'])
            S.op('dve', lambda e: e.tensor_tensor(out=t1, in0=t1, in1=t2, op=ALU.add), reads=['wslot0', 'wslot1'], writes=['wslot0'])
            S.op('dve', lambda e: e.tensor_scalar(out=t2, in0=t1, scalar1=PI, scalar2=-2 * PI, op0=ALU.is_gt, op1=ALU.mult), reads=['wslot0'], writes=['wslot1'])
            S.op('dve', lambda e: e.tensor_tensor(out=t1, in0=t1, in1=t2, op=ALU.add), reads=['wslot0', 'wslot1'], writes=['wslot0'])
            S.op('dve', lambda e: e.tensor_scalar(out=t1, in0=t1, scalar1=-3.1415925, scalar2=3.1415925, op0=ALU.max, op1=ALU.min), reads=['wslot0'], writes=['wslot0'])
            S.op('act', lambda e: e.activation(out=out_ap, in_=t1, func=AF.Sin), reads=['wslot0'], writes=[key_out])

        def s5_setup(l):
            S.dma('sp', 'c8', s5p[:], s5p_d[l, :, :], writes=['s5p'])
            lr = s5p[:, 0:16]
            li = s5p[:, 16:32]
            ldt = s5p[:, 32:48]
            bre = s5p[:, 48:304].rearrange("p (s i) -> p s i", s=16)
            bim = s5p[:, 304:560].rearrange("p (s i) -> p s i", s=16)
            cre_ = s5p[:, 560:816].rearrange("p (s i) -> p s i", s=16)
            cim_ = s5p[:, 816:1072].rearrange("p (s i) -> p s i", s=16)
            dt, th, sn, cs, are, aim, t6, t7 = [t[:, :] for t in s5t]
            tk = [f's5t{i}' for i in range(8)]
            S.op('act', lambda e: e.activation(out=dt, in_=ldt, func=AF.Exp), reads=['s5p'], writes=[tk[0]])
            S.op('dve', lambda e: e.tensor_tensor(out=th, in0=li, in1=dt, op=ALU.mult), reads=['s5p', tk[0]], writes=[tk[1]])
            S.op('dve', lambda e: e.tensor_tensor(out=t6, in0=lr, in1=dt, op=ALU.mult), reads=['s5p', tk[0]], writes=[tk[6]])
            S.op('act', lambda e: e.activation(out=mag[:, :], in_=t6, func=AF.Exp), reads=[tk[6]], writes=['mag'])
            sincos(sn, th, 0.0, 16, [tk[1]], tk[2])
            sincos(cs, th, PI / 2, 16, [tk[1]], tk[3])
            S.op('dve', lambda e: e.tensor_tensor(out=are, in0=mag[:, :], in1=cs, op=ALU.mult), reads=['mag', tk[3]], writes=[tk[4]])
            S.op('dve', lambda e: e.tensor_tensor(out=aim, in0=mag[:, :], in1=sn, op=ALU.mult), reads=['mag', tk[2]], writes=[tk[5]])
            S.op('dve', lambda e: e.tensor_tensor(out=t6, in0=lr, in1=lr, op=ALU.mult), reads=['s5p'], writes=[tk[6]])
            S.op('dve', lambda e: e.tensor_tensor(out=t7, in0=li, in1=li, op=ALU.mult), reads=['s5p'], writes=[tk[7]])
            S.op('dve', lambda e: e.tensor_tensor(out=t6, in0=t6, in1=t7, op=ALU.add), reads=[tk[6], tk[7]], writes=[tk[6]])
            S.op('dve', lambda e: e.reciprocal(out=dt, in_=t6), reads=[tk[6]], writes=[tk[0]])
            S.op('dve', lambda e: e.tensor_scalar(out=sn, in0=are, scalar1=-1.0, scalar2=None, op0=ALU.add), reads=[tk[4]], writes=[tk[2]])
            S.op('dve', lambda e: e.tensor_tensor(out=t6, in0=sn, in1=lr, op=ALU.mult), reads=[tk[2], 's5p'], writes=[tk[6]])
            S.op('dve', lambda e: e.tensor_tensor(out=cs, in0=aim, in1=li, op=ALU.mult), reads=[tk[5], 's5p'], writes=[tk[3]])
            S.op('dve', lambda e: e.tensor_tensor(out=t6, in0=t6, in1=cs, op=ALU.add), reads=[tk[6], tk[3]], writes=[tk[6]])
            S.op('dve', lambda e: e.tensor_tensor(out=t6, in0=t6, in1=dt, op=ALU.mult), reads=[tk[6], tk[0]], writes=[tk[6]])
            S.op('dve', lambda e: e.tensor_tensor(out=t7, in0=aim, in1=lr, op=ALU.mult), reads=[tk[5], 's5p'], writes=[tk[7]])
            S.op('dve', lambda e: e.tensor_tensor(out=cs, in0=sn, in1=li, op=ALU.mult), reads=[tk[2], 's5p'], writes=[tk[3]])
            S.op('dve', lambda e: e.tensor_tensor(out=t7, in0=t7, in1=cs, op=ALU.subtract), reads=[tk[7], tk[3]], writes=[tk[7]])
            S.op('dve', lambda e: e.tensor_tensor(out=t7, in0=t7, in1=dt, op=ALU.mult), reads=[tk[7], tk[0]], writes=[tk[7]])
            rre_b = t6.unsqueeze(2).broadcast_to([128, 16, 16])
            rim_b = t7.unsqueeze(2).broadcast_to([128, 16, 16])
            bbr = bpr[:, 0:256].rearrange("p (s i) -> p s i", s=16)
            bbi = bpi[:, 0:256].rearrange("p (s i) -> p s i", s=16)
            tA = rr[:, 0:256].rearrange("p (s i) -> p s i", s=16)
            S.op('dve', lambda e: e.tensor_tensor(out=bbr, in0=bre, in1=rre_b, op=ALU.mult), reads=['s5p', tk[6]], writes=['bpr'])
            S.op('dve', lambda e: e.tensor_tensor(out=tA, in0=bim, in1=rim_b, op=ALU.mult), reads=['s5p', tk[7]], writes=['rr'])
            S.op('dve', lambda e: e.tensor_tensor(out=bbr, in0=bbr, in1=tA, op=ALU.subtract), reads=['bpr', 'rr'], writes=['bpr'])
            S.op('dve', lambda e: e.tensor_tensor(out=bbi, in0=bim, in1=rre_b, op=ALU.mult), reads=['s5p', tk[6]], writes=['bpi'])
            S.op('dve', lambda e: e.tensor_tensor(out=tA, in0=bre, in1=rim_b, op=ALU.mult), reads=['s5p', tk[7], 'bpr'], writes=['rr'])
            S.op('dve', lambda e: e.tensor_tensor(out=bbi, in0=bbi, in1=tA, op=ALU.add), reads=['bpi', 'rr'], writes=['bpi'])
            mk4 = mmask[:, :, :].unsqueeze(3).broadcast_to([128, 16, 8, 16])
            for (bb, BT, nm) in [(bbr, BTr, 'BTr'), (bbi, BTi, 'BTi')]:
                Z = big1[:, :].rearrange("p (s g i) -> p s g i", s=16, g=8)
                S.op('dve', lambda e, bb=bb: e.tensor_tensor(out=Z, in0=bb.unsqueeze(2).broadcast_to([128, 16, 8, 16]), in1=mk4, op=ALU.mult),
                     reads=['bpr', 'bpi', 'mmask'], writes=['wslot0'])
                for q4 in range(4):
                    bk, bkey = bank()
                    for a in range(4):
                        sbi = q4 * 4 + a
                        S.op('pe', lambda e, sbi=sbi, a=a, bk=bk: e.transpose(out=bk[:, a * 128:(a + 1) * 128], in_=big1[:, sbi * 128:(sbi + 1) * 128], identity=ident[:, :]),
                             reads=['wslot0', 'ident'], writes=[bkey])
                    S.op('act', lambda e, q4=q4, bk=bk, BT=BT: e.copy(out=BT[:, q4 * 4:(q4 + 1) * 4, :], in_=bk[:, :].rearrange("p (a c) -> p a c", a=4)),
                         reads=[bkey], writes=[nm])
            CZr = CTr[:, :, :].rearrange("p s (g i) -> p s g i", g=8)
            CZi = CTi[:, :, :].rearrange("p s (g i) -> p s g i", g=8)
            S.op('dve', lambda e: e.tensor_tensor(out=CZr, in0=cre_.unsqueeze(2).broadcast_to([128, 16, 8, 16]), in1=mk4, op=ALU.mult),
                 reads=['s5p', 'mmask'], writes=['CTr'])
            S.op('dve', lambda e: e.tensor_tensor(out=CZi, in0=cim_.unsqueeze(2).broadcast_to([128, 16, 8, 16]), in1=mk4, op=ALU.mult),
                 reads=['s5p', 'mmask'], writes=['CTi'])
            CTi2 = CTi[:, :, :].rearrange("p s c -> p (s c)")
            S.op('dve', lambda e: e.tensor_scalar(out=CTi2, in0=CTi2, scalar1=-1.0, scalar2=None, op0=ALU.mult), reads=['CTi'], writes=['CTi'])
            ph = st_big[:, :].rearrange("p (s t) -> p s t", s=16)
            S.op('dve', lambda e: e.tensor_tensor(out=ph, in0=th.unsqueeze(2).broadcast_to([128, 16, LCH]), in1=iota[:, :].unsqueeze(1).broadcast_to([128, 16, LCH]), op=ALU.mult),
                 reads=[tk[1], 'iota'], writes=['wslot0'])
            sincos(sinT[:, :, :].rearrange("p s t -> p (s t)"), st_big[:, :], 0.0, 16 * LCH, ['wslot0'], 'sinT')
            sincos(cosT[:, :, :].rearrange("p s t -> p (s t)"), st_big[:, :], PI / 2, 16 * LCH, ['wslot0'], 'cosT')
            S.op('act', lambda e: e.activation(out=sLneg[:, :], in_=sinT[:, :, LCH - 1], func=AF.Copy, scale=-1.0), reads=['sinT'], writes=['sLneg'])
            S.op('dve', lambda e: e.memset(cre[:, :], 0.0), writes=['cre'])
            S.op('dve', lambda e: e.memset(cim[:, :], 0.0), writes=['cim'])
            S.op('dve', lambda e: e.memset(vbuf[:, :, 0:2], 0.0), writes=['vbuf'])

        def s5_tile(tasks, l, j, N):
            nch = N // LCH
            Win = W['w_in'][l]
            dcol = lambda c: s5p[:, 1072 + c:1073 + c]
            bglu = lambda c: s5p[:, 1076 + c:1077 + c]

            def fu(view, key):
                for c in range(4):
                    bk, bkey = bank()
                    for k in range(8):
                        S.op('pe', lambda e, k=k, c=c, bk=bk: e.matmul(bk[:, 0:N], lhsT=view[:, k, c * 128:(c + 1) * 128], rhs=xAb[:, k, 0:N],
                                                                        start=(k == 0), stop=(k == 7)), reads=[key, f'xAb{k}'], writes=[bkey])
                    S.op('act', lambda e, c=c, bk=bk: e.copy(out=u32[:, c, 0:N], in_=bk[:, 0:N]), reads=[bkey], writes=[f'u32_{c}'])
                    S.op('dve', lambda e, c=c: e.tensor_copy(out=ub[:, c, 0:N], in_=u32[:, c, 0:N]), reads=[f'u32_{c}'], writes=[f'ub{c}'])
            tasks.append(((Win, 0, 8, 0, 512), fu))

            def scan_all():
                cosB = lambda sbi: cosT[:, sbi, :].unsqueeze(1).broadcast_to([128, nch, LCH])
                sinB = lambda sbi: sinT[:, sbi, :].unsqueeze(1).broadcast_to([128, nch, LCH])
                v3 = lambda t_: t_[:, 0:N].rearrange("p (c t) -> p c t", c=nch)
                for c in range(4):
                    for a in range(4):
                        sbi = c * 4 + a
                        bA, bAk = bank()
                        S.op('pe', lambda e, sbi=sbi, c=c, bA=bA: e.matmul(bA[:, 0:N], lhsT=BTr[:, sbi, :], rhs=ub[:, c, 0:N], start=True, stop=True),
                             reads=['BTr', f'ub{c}'], writes=[bAk])
                        bB, bBk = bank()
                        S.op('pe', lambda e, sbi=sbi, c=c, bB=bB: e.matmul(bB[:, 0:N], lhsT=BTi[:, sbi, :], rhs=ub[:, c, 0:N], start=True, stop=True),
                             reads=['BTi', f'ub{c}'], writes=[bBk])
                        A4 = bA[:, 0:N].rearrange("p (c t) -> p c t", c=nch)
                        B4 = bB[:, 0:N].rearrange("p (c t) -> p c t", c=nch)
                        S.op('dve', lambda e, sbi=sbi: e.tensor_tensor(out=v3(st1), in0=A4, in1=cosB(sbi), op=ALU.mult), reads=[bAk, 'cosT'], writes=['st1'])
                        S.op('dve', lambda e, sbi=sbi: e.tensor_tensor(out=v3(st2), in0=B4, in1=sinB(sbi), op=ALU.mult), reads=[bBk, 'sinT'], writes=['st2'])
                        S.op('dve', lambda e: e.tensor_tensor(out=bpr[:, 0:N], in0=st1[:, 0:N], in1=st2[:, 0:N], op=ALU.add), reads=['st1', 'st2'], writes=['bpr'])
                        S.op('dve', lambda e, sbi=sbi: e.tensor_tensor(out=v3(st1), in0=B4, in1=cosB(sbi), op=ALU.mult), reads=[bBk, 'cosT'], writes=['st1'])
                        S.op('dve', lambda e, sbi=sbi: e.tensor_tensor(out=v3(st2), in0=A4, in1=sinB(sbi), op=ALU.mult), reads=[bAk, 'sinT'], writes=['st2'])
                        S.op('dve', lambda e: e.tensor_tensor(out=bpi[:, 0:N], in0=st1[:, 0:N], in1=st2[:, 0:N], op=ALU.subtract), reads=['st1', 'st2'], writes=['bpi'])
                        for ch in range(nch):
                            sl = slice(ch * LCH, (ch + 1) * LCH)
                            S.op('dve', lambda e, sbi=sbi, sl=sl: e.tensor_tensor_scan(out=rr[:, sl], data0=mag[:, sbi:sbi + 1].broadcast_to([128, LCH]), data1=bpr[:, sl],
                                                                                      initial=cre[:, sbi:sbi + 1], op0=ALU.mult, op1=ALU.add),
                                 reads=['mag', 'bpr', 'cre'], writes=['rr'])
                            S.op('dve', lambda e, sbi=sbi, sl=sl: e.tensor_tensor_scan(out=ri[:, sl], data0=mag[:, sbi:sbi + 1].broadcast_to([128, LCH]), data1=bpi[:, sl],
                                                                                      initial=cim[:, sbi:sbi + 1], op0=ALU.mult, op1=ALU.add),
                                 reads=['mag', 'bpi', 'cim'], writes=['ri'])
                            e0 = (ch + 1) * LCH - 1
                            cL = cosT[:, sbi, LCH - 1:LCH]
                            sL = sinT[:, sbi, LCH - 1:LCH]
                            S.op('dve', lambda e, e0=e0, cL=cL: e.tensor_scalar(out=ctmp[:, 0:1], in0=rr[:, e0:e0 + 1], scalar1=cL, scalar2=None, op0=ALU.mult),
                                 reads=['rr', 'cosT'], writes=['ctmp'])
                            S.op('dve', lambda e, e0=e0, cL=cL: e.tensor_scalar(out=ctmp[:, 1:2], in0=ri[:, e0:e0 + 1], scalar1=cL, scalar2=None, op0=ALU.mult),
                                 reads=['ri', 'cosT'], writes=['ctmp'])
                            S.op('dve', lambda e, e0=e0, sbi=sbi: e.scalar_tensor_tensor(out=cre[:, sbi:sbi + 1], in0=ri[:, e0:e0 + 1], scalar=sLneg[:, sbi:sbi + 1], in1=ctmp[:, 0:1],
                                                                                          op0=ALU.mult, op1=ALU.add), reads=['ri', 'sLneg', 'ctmp'], writes=['cre'])
                            S.op('dve', lambda e, e0=e0, sbi=sbi, sL=sL: e.scalar_tensor_tensor(out=cim[:, sbi:sbi + 1], in0=rr[:, e0:e0 + 1], scalar=sL, in1=ctmp[:, 1:2],
                                                                                                 op0=ALU.mult, op1=ALU.add), reads=['rr', 'sinT', 'ctmp'], writes=['cim'])
                        S.op('pool', lambda e, sbi=sbi: e.tensor_tensor(out=v3(st1), in0=v3(rr), in1=cosB(sbi), op=ALU.mult), reads=['rr', 'cosT'], writes=['st1'])
                        S.op('pool', lambda e, sbi=sbi: e.tensor_tensor(out=v3(st2), in0=v3(ri), in1=sinB(sbi), op=ALU.mult), reads=['ri', 'sinT'], writes=['st2'])
                        S.op('pool', lambda e: e.tensor_tensor(out=srb[:, 0:N], in0=st1[:, 0:N], in1=st2[:, 0:N], op=ALU.subtract), reads=['st1', 'st2'], writes=['srb'])
                        S.op('pool', lambda e, sbi=sbi: e.tensor_tensor(out=v3(st1), in0=v3(rr), in1=sinB(sbi), op=ALU.mult), reads=['rr', 'sinT'], writes=['st1'])
                        S.op('pool', lambda e, sbi=sbi: e.tensor_tensor(out=v3(st2), in0=v3(ri), in1=cosB(sbi), op=ALU.mult), reads=['ri', 'cosT'], writes=['st2'])
                        S.op('pool', lambda e: e.tensor_tensor(out=sib[:, 0:N], in0=st1[:, 0:N], in1=st2[:, 0:N], op=ALU.add), reads=['st1', 'st2'], writes=['sib'])
                        S.op('pe', lambda e, sbi=sbi, a=a: e.matmul(YB[:, 0:N], lhsT=CTr[:, sbi, :], rhs=srb[:, 0:N], start=(a == 0), stop=False),
                             reads=['CTr', 'srb'], writes=[YBK])
                        S.op('pe', lambda e, sbi=sbi, a=a: e.matmul(YB[:, 0:N], lhsT=CTi[:, sbi, :], rhs=sib[:, 0:N], start=False, stop=(a == 3)),
                             reads=['CTi', 'sib'], writes=[YBK])
                    S.op('dve', lambda e, c=c: e.scalar_tensor_tensor(out=yag[:, c, 0:N], in0=u32[:, c, 0:N], scalar=dcol(c), in1=YB[:, 0:N], op0=ALU.mult, op1=ALU.add),
                         reads=[f'u32_{c}', 's5p', YBK], writes=[f'yag{c}'])
                    S.op('act', lambda e, c=c: e.activation(out=yag[:, c, 0:N], in_=yag[:, c, 0:N], func=AF.Gelu), reads=[f'yag{c}'], writes=[f'yag{c}'])
                    S.op('dve', lambda e, c=c: e.tensor_copy(out=yagb[:, c, 0:N], in_=yag[:, c, 0:N]), reads=[f'yag{c}'], writes=[f'yagb{c}'])
            tasks.append((None, scan_all))

            def fglu(view, key):
                for m in range(4):
                    bk, bkey = bank()
                    for k in range(4):
                        S.op('pe', lambda e, k=k, m=m, bk=bk: e.matmul(bk[:, 0:N], lhsT=view[:, k, m * 128:(m + 1) * 128], rhs=yagb[:, k, 0:N],
                                                                        start=(k == 0), stop=(k == 3)), reads=[key, f'yagb{k}'], writes=[bkey])
                    S.op('act', lambda e, m=m, bk=bk: e.activation(out=st1[:, 0:N], in_=bk[:, 0:N], func=AF.Sigmoid, bias=bglu(m)), reads=[bkey, 's5p'], writes=['st1'])
                    S.op('dve', lambda e, m=m: e.tensor_tensor(out=ya2b[:, m, 0:N], in0=yag[:, m, 0:N], in1=st1[:, 0:N], op=ALU.mult), reads=[f'yag{m}', 'st1'], writes=[f'ya2b{m}'])
            tasks.append(((W['glu'][l], 0, 4, 0, 512), fglu))

            if j == NTILE - 1:
                def state_out():
                    for (src, dst, nm) in [(cre, s5re_p, 'OUT_s5re'), (cim, s5im_p, 'OUT_s5im')]:
                        bk, bkey = bank()
                        S.op('pe', lambda e, src=src, bk=bk: e.transpose(out=bk[0:16, 0:128], in_=src[:, :], identity=ident[:, :]), reads=['cre', 'cim', 'ident'], writes=[bkey])
                        S.op('act', lambda e, bk=bk: e.copy(out=smallo[0:16, 0:128], in_=bk[0:16, 0:128]), reads=[bkey], writes=['smallo'])
                        S.dma('sp', 'os5', dst[l, :, :], smallo[0:16, 0:128], reads=['smallo'], writes=[nm])
                tasks.append((None, state_out))

        def conv_tile(tasks, l, j, N):
            Win = W['w_in'][l]
            wcol = lambda k_, c: s5p[:, 1080 + k_ * 4 + c:1081 + k_ * 4 + c]

            def fbc(view, key):
                for c8 in range(8):
                    bk, bkey = bank()
                    for k in range(8):
                        S.op('pe', lambda e, k=k, c8=c8, bk=bk: e.matmul(bk[:, 0:N], lhsT=view[:, k, c8 * 128:(c8 + 1) * 128], rhs=xAb[:, k, 0:N],
                                                                          start=(k == 0), stop=(k == 7)), reads=[key, f'xAb{k}'], writes=[bkey])
                    dst = cb32 if c8 < 4 else cc32
                    nm = ('cb' if c8 < 4 else 'cc') + str(c8 % 4)
                    S.op('act', lambda e, c8=c8, bk=bk, dst=dst: e.copy(out=dst[:, c8 % 4, 0:N], in_=bk[:, 0:N]), reads=[bkey], writes=[nm])
            tasks.append(((Win, 0, 8, 1816, 1024), fbc))

            def fh(view, key):
                for c in range(4):
                    bk, bkey = bank()
                    for k in range(8):
                        S.op('pe', lambda e, k=k, c=c, bk=bk: e.matmul(bk[:, 0:N], lhsT=view[:, k, c * 128:(c + 1) * 128], rhs=xAb[:, k, 0:N],
                                                                        start=(k == 0), stop=(k == 7)), reads=[key, f'xAb{k}'], writes=[bkey])
                    S.op('dve', lambda e, c=c, bk=bk: e.tensor_tensor(out=vbuf[:, c, 2:2 + N], in0=bk[:, 0:N], in1=cc32[:, c, 0:N], op=ALU.mult),
                         reads=[bkey, f'cc{c}', 'vbuf'], writes=['vbuf'])
                    S.op('act', lambda e, c=c: e.activation(out=ctm[:, 0:N], in_=vbuf[:, c, 2:2 + N], func=AF.Copy, scale=wcol(2, c)), reads=['vbuf', 's5p'], writes=['ctm'])
                    S.op('dve', lambda e, c=c: e.scalar_tensor_tensor(out=ctm[:, 0:N], in0=vbuf[:, c, 1:1 + N], scalar=wcol(1, c), in1=ctm[:, 0:N], op0=ALU.mult, op1=ALU.add),
                         reads=['vbuf', 's5p', 'ctm'], writes=['ctm'])
                    S.op('dve', lambda e, c=c: e.scalar_tensor_tensor(out=ctm[:, 0:N], in0=vbuf[:, c, 0:N], scalar=wcol(0, c), in1=ctm[:, 0:N], op0=ALU.mult, op1=ALU.add),
                         reads=['vbuf', 's5p', 'ctm'], writes=['ctm'])
                    S.op('dve', lambda e, c=c: e.tensor_tensor(out=ycb[:, c, 0:N], in0=ctm[:, 0:N], in1=cb32[:, c, 0:N], op=ALU.mult), reads=['ctm', f'cb{c}'], writes=[f'ycb{c}'])
                if j == NTILE - 1:
                    bk, bkey = bank()
                    for c in range(4):
                        S.op('pe', lambda e, c=c, bk=bk: e.transpose(out=bk[0:2, c * 128:(c + 1) * 128], in_=vbuf[:, c, N:N + 2], identity=ident[:, :]),
                             reads=['vbuf', 'ident'], writes=[bkey])
                    S.op('act', lambda e, bk=bk: e.copy(out=smallo[0:2, 0:512], in_=bk[0:2, 0:512]), reads=[bkey], writes=['smallo'])
                    S.dma('sp', 'oconv', conv_p[l, :, :], smallo[0:2, 0:512], reads=['smallo'], writes=['OUT_conv'])
                else:
                    S.op('dve', lambda e: e.tensor_copy(out=vbuf[:, :, 0:2], in_=vbuf[:, :, N:N + 2]), reads=['vbuf'], writes=['vbuf'])
            tasks.append(((Win, 0, 8, 2840, 512), fh))

        def spill_A(tasks, j):
            def fn():
                t0 = j * TT
                tkeys = [f'tokq{s_}' for s_ in range(NSUB)]
                S.dma('sp', 'sp0', x1_s[j], xA[:, :, :], reads=[f'xA{k}' for k in range(8)], writes=['OUT_x1s'])
                S.dma('sp', 'sp1', ya_s[j], ya2b[:, :, :], reads=[f'ya2b{c}' for c in range(4)], writes=['OUT_yas'])
                S.dma('sp', 'sp2', yc_s[j], ycb[:, :, :], reads=[f'ycb{c}' for c in range(4)], writes=['OUT_ycs'])
                S.dma('sp', 'sp3', qg_s[t0:t0 + TT, 0:512].rearrange("(s p) c -> p s c", p=128), tokq[:, :, 0:512], reads=tkeys, writes=['OUT_qgs'])
                S.dma('sp', 'sp4', qg_s[t0:t0 + TT, 512:536].rearrange("(s p) c -> p s c", p=128), tokq[:, :, 1280:1304], reads=tkeys, writes=['OUT_qgs2'])
                S.dma('sp', 'sp5', win_s[t0:t0 + TT, :].rearrange("(s p) c -> p s c", p=128), tokq[:, :, 1024:1280], reads=tkeys, writes=['OUT_wins'])
            tasks.append((None, fn))

        B = K()

        def alloc_B(stk):
            uid[0] += 1
            u_ = f"_{uid[0]}"
            cur[0] = stk
            B.ksT = sb("ksT" + u_, [128, T], BF16)
            B.kwT = sb("kwT" + u_, [128, T], BF16)
            B.vs = sb("vs" + u_, [128, 32, 2, 65], BF16)
            B.vw = sb("vw" + u_, [128, 32, 2, 65], BF16)
            B.kcT = sb("kcT" + u_, [128, 256], BF16)
            B.vcm = sb("vcm" + u_, [32, 8, 2, 129], BF16)
            B.w1 = [sb(f"w1{i}" + u_, [128, 32, 128], BF16) for i in range(2)]
            B.w2 = [sb(f"w2{i}" + u_, [128, 64], BF16) for i in range(2)]
            B.peT = [sb(f"peT{i}" + u_, [128, 32], BF16) for i in range(2)]
            B.hpe = [sb(f"hpe{i}" + u_, [128, 1]) for i in range(2)]
            B.XrT = [sb(f"XrT{i}" + u_, [128, 16 + 512], BF16) for i in range(2)]
            B.rows = [sb(f"rows{i}" + u_, [128, 4, 256]) for i in range(3)]
            B.cosC = sb("cosC" + u_, [32, 8, 8])
            B.sinC = sb("sinC" + u_, [32, 8, 8])
            B.mapc = sb("mapc" + u_, [32, 8, 64])
            B.kct = sb("kct" + u_, [32, 2, 64])
            B.kt = [sb(f"kt{i}" + u_, [32, 2, 8]) for i in range(4)]
            B.hact = sb("hact" + u_, [128, 32], BF16)
            B.qtok = sb("qtok" + u_, [128, NSUB, 536])
            B.qT = sb("qT" + u_, [128, 4, TT], BF16)
            B.gates = sb("gates" + u_, [128, NSUB, 24])
            B.wmaskb = sb("wmaskb" + u_, [128, 6, TT], BF16)
            B.cmaskb = sb("cmaskb" + u_, [32, 5, TT], BF16)
            B.ebigb = sb("ebigb" + u_, [64, T], BF16)
            B.fb = sb("fb" + u_, [128, NSUB, 64])
            B.selbT = sb("selbT" + u_, [64, 2, TT], BF16)
            B.pT = [sb(f"pT{i}" + u_, [128, TT], BF16) for i in range(2)]
            B.ybt = sb("ybt" + u_, [128, NSUB, 512])
            B.imp = sb("imp" + u_, [128, NSUB, 2, 64])
            B.sc = sb("sc" + u_, [128, 8])
            B.score = sb("score" + u_, [128, 64])
            B.scr2 = sb("scr2" + u_, [128, 64])
            B.m8 = sb("m8" + u_, [128, 16])
            B.tmp64 = sb("tmp64" + u_, [128, 64])
            B.ybT = sb("ybT" + u_, [128, 4, TT], BF16)
            B.stage = sb("stage" + u_, [128, 2048])
            B.kcT_s = sb("kcT_s" + u_, [128, 1024], BF16)
            B.vcm_s = sb("vcm_s" + u_, [32, 32, 2, 65], BF16)
            B.cosC2 = sb("cosC2" + u_, [32, 32, 8])
            B.sinC2 = sb("sinC2" + u_, [32, 32, 8])
            B.maploc = sb("maploc" + u_, [32, 16], BF16)
            B.lohi = sb("lohi" + u_, [1, 256], BF16)
            B.fbs = sb("fbs" + u_, [1, 264])
            B.pti = sb("pti" + u_, [128, 128], I32)
            B.idf = sb("idf" + u_, [128, 128])
            B.idx = sb("idx" + u_, [128, 128], I32)
            B.pcol = sb("pcol" + u_, [128, DEPTH])
            B.qtok_s = sb("qtok_s" + u_, [4, 536])
            B.qT_s = sb("qT_s" + u_, [128, 4, 4], BF16)
            B.gcol = sb("gcol" + u_, [4, 2, 3])
            B.yacc = sb("yacc" + u_, [4, 2, 64])
            B.ybt_s = sb("ybt_s" + u_, [4, 512])
            B.ybT_s = sb("ybT_s" + u_, [128, 4, 4], BF16)
            B.impsb = sb("impsb" + u_, [4, 264])
            B.imprw = sb("imprw" + u_, [1, 264])
            B.scs = sb("scs" + u_, [1, 264])
            B.scs2 = sb("scs2" + u_, [1, 264])
            B.m8s = sb("m8s" + u_, [1, 16])
            B.selbb = sb("selbb" + u_, [1, 2, 264], BF16)
            B.rd = sb("rd" + u_, [4, 2, 4])
            B.pgT = sb("pgT" + u_, [128, 512], BF16)
            B.vpg = sb("vpg" + u_, [128, 4, 2, 65], BF16)
            B.pTs = [sb(f"pTs{i}" + u_, [128, 4], BF16) for i in range(2)]
            B.zl = sb("zl" + u_, [32, 4], BF16)
            B.zr = sb("zr" + u_, [32, 272], BF16)

        HP = [0, 4, 1, 5, 2, 6, 3, 7]

        def phaseB_build(l):
            ph1 = [phik1_d, phiv1_d]
            ph2 = [phik2_d, phiv2_d]
            pe = [peTk_d, peTv_d]
            for x in range(2):
                src = ph1[x][l].rearrange("(s d) h -> d s h", d=64)
                S.dma('pool', f'bw{x}a', B.w1[x][0:64, :, :], src, writes=[f'w1_{x}'])
                S.dma('pool', f'bw{x}b', B.w1[x][64:128, :, :], src, writes=[f'w1_{x}'])
                S.dma('pool', f'bw{x}c', B.w2[x][:, :], ph2[x][l], writes=[f'w2_{x}'])
                S.dma('pool', f'bw{x}d', B.peT[x][0:64, :], pe[x][l], writes=[f'peT{x}'])
                bk, bkey = bank()
                for s_ in range(32):
                    S.op('pe', lambda e, s_=s_, x=x, bk=bk: e.matmul(bk[:, 0:1], lhsT=B.w1[x][0:64, s_, :], rhs=B.peT[x][0:64, s_:s_ + 1], start=(s_ == 0), stop=(s_ == 31)),
                         reads=[f'w1_{x}', f'peT{x}'], writes=[bkey])
                S.op('act', lambda e, x=x, bk=bk: e.copy(out=B.hpe[x][:, :], in_=bk[:, 0:1]), reads=[bkey], writes=[f'hpe{x}'])
                S.op('dve', lambda e, x=x: e.memset(B.XrT[x][:, 0:16], 0.0), writes=[f'XrT{x}'])
            S.dma('sp', 'bc0', B.cosC[:], ropeCc_d[:, :, :], writes=['cosC'])
            S.dma('sp', 'bc1', B.sinC[:], ropeCs_d[:, :, :], writes=['sinC'])
            S.dma('sp', 'bc2', B.mapc[:], mapc_d[:, :, :], writes=['mapc'])
            S.op('dve', lambda e: e.memset(B.vcm[:, :, :, 64:65], 1.0), writes=['vcm'])
            for h in range(2):
                S.op('dve', lambda e, h=h: e.tensor_copy(out=B.vcm[:, :, h, 65:129], in_=B.mapc[:, :, :]), reads=['mapc', 'vcm'], writes=['vcm'])
            S.op('dve', lambda e: e.memset(B.vs[:, :, :, 64:65], 1.0), writes=['vs'])
            S.op('dve', lambda e: e.memset(B.vw[:, :, :, 64:65], 1.0), writes=['vw'])
            S.dma('sp', 'bc3', B.stage[:, 0:6 * TT], wmask_d[:, :], writes=['stage'])
            S.op('dve', lambda e: e.tensor_copy(out=B.wmaskb[:, :, :].rearrange("p a b -> p (a b)"), in_=B.stage[:, 0:6 * TT]), reads=['stage'], writes=['wmaskb'])
            S.dma('sp', 'bc3', B.stage[0:32, 0:5 * TT], cmask_d[:, :], reads=['wmaskb'], writes=['stage'])
            S.op('dve', lambda e: e.tensor_copy(out=B.cmaskb[:, :, :].rearrange("p a b -> p (a b)"), in_=B.stage[0:32, 0:5 * TT]), reads=['stage'], writes=['cmaskb'])
            for hf in range(2):
                S.dma('sp', 'bc3', B.stage[0:64, 0:2048], ebig_d[:, hf * 2048:(hf + 1) * 2048], reads=['cmaskb', 'ebigb'], writes=['stage'])
                S.op('dve', lambda e, hf=hf: e.tensor_copy(out=B.ebigb[:, hf * 2048:(hf + 1) * 2048], in_=B.stage[0:64, 0:2048]), reads=['stage'], writes=['ebigb'])
            srcs = [cmp_rows_p, sel_rows_p, win_s]
            for jg in range(max(1, (ntile * TT) // 512)):
                t0 = jg * 512
                for r_ in range(3):
                    S.dma('sp', f'br{r_}', B.rows[r_][:, :, :], srcs[r_][l, t0:t0 + 512, :].rearrange("(s p) c -> p s c", p=128) if r_ < 2
                          else win_s[t0:t0 + 512, :].rearrange("(s p) c -> p s c", p=128), writes=[f'rows{r_}'])
                for (r_, dst, dkey, col0) in [(0, B.XrT[0][:, 16:528], 'XrT0', 0), (0, B.XrT[1][:, 16:528], 'XrT1', 128),
                                              (1, B.ksT[:, t0:t0 + 512], 'ksT', 0), (2, B.kwT[:, t0:t0 + 512], 'kwT', 0)]:
                    bk, bkey = bank()
                    for sub in range(4):
                        S.op('pe', lambda e, r_=r_, sub=sub, col0=col0, bk=bk: e.transpose(out=bk[:, sub * 128:(sub + 1) * 128], in_=B.rows[r_][:, sub, col0:col0 + 128], identity=ident[:, :]),
                             reads=[f'rows{r_}', 'ident'], writes=[bkey])
                    S.op('act', lambda e, dst=dst, bk=bk: e.copy(out=dst, in_=bk[:, :]), reads=[bkey], writes=[dkey])
                S.op('dve', lambda e, jg=jg: e.tensor_copy(out=B.vs[:, jg * 4:(jg + 1) * 4, :, 0:64], in_=B.rows[1][:, :, 128:256].rearrange("p s (h d) -> p s h d", h=2)),
                     reads=['rows1'], writes=['vs'])
                S.op('dve', lambda e, jg=jg: e.tensor_copy(out=B.vw[:, jg * 4:(jg + 1) * 4, :, 0:64], in_=B.rows[2][:, :, 128:256].rearrange("p s (h d) -> p s h d", h=2)),
                     reads=['rows2'], writes=['vw'])
                for x in range(2):
                    for h in range(2):
                        bk, bkey = bank()
                        for s_ in range(32):
                            S.op('pe', lambda e, s_=s_, x=x, h=h, bk=bk: e.matmul(bk[:, 0:32], lhsT=B.w1[x][64 * h:64 * h + 64, s_, :],
                                                                                   rhs=B.XrT[x][64 * h:64 * h + 64, s_:s_ + 497:16], start=(s_ == 0), stop=(s_ == 31)),
                                 reads=[f'w1_{x}', f'XrT{x}'], writes=[bkey])
                        S.op('act', lambda e, x=x, bk=bk: e.activation(out=B.hact[:, :], in_=bk[:, 0:32], func=AF.Gelu, bias=B.hpe[x][:, 0:1]), reads=[bkey, f'hpe{x}'], writes=['hact'])
                        bk2, bk2key = bank()
                        S.op('pe', lambda e, x=x, bk2=bk2: e.matmul(bk2[0:32, 0:64], lhsT=B.hact[:, :], rhs=B.w2[x][:, :], start=True, stop=True),
                             reads=['hact', f'w2_{x}'], writes=[bk2key])
                        if x == 0:
                            S.op('act', lambda e, h=h, bk2=bk2: e.copy(out=B.kct[:, h, :], in_=bk2[0:32, 0:64]), reads=[bk2key], writes=['kct'])
                        else:
                            S.op('act', lambda e, h=h, jg=jg, bk2=bk2: e.copy(out=B.vcm[:, jg, h, 0:64], in_=bk2[0:32, 0:64]), reads=[bk2key], writes=['vcm'])
                    if x == 0:
                        x1 = B.kct[:, :, 0:8]
                        x2 = B.kct[:, :, 8:16]
                        cb_ = B.cosC[:, jg, :].unsqueeze(1).broadcast_to([32, 2, 8])
                        sb_ = B.sinC[:, jg, :].unsqueeze(1).broadcast_to([32, 2, 8])
                        tm = [t_[:, :, :] for t_ in B.kt]
                        S.op('dve', lambda e: e.tensor_tensor(out=tm[0], in0=x1, in1=cb_, op=ALU.mult), reads=['kct', 'cosC'], writes=['kt0'])
                        S.op('dve', lambda e: e.tensor_tensor(out=tm[1], in0=x2, in1=sb_, op=ALU.mult), reads=['kct', 'sinC'], writes=['kt1'])
                        S.op('dve', lambda e: e.tensor_tensor(out=tm[2], in0=x2, in1=cb_, op=ALU.mult), reads=['kct', 'cosC'], writes=['kt2'])
                        S.op('dve', lambda e: e.tensor_tensor(out=tm[3], in0=x1, in1=sb_, op=ALU.mult), reads=['kct', 'sinC'], writes=['kt3'])
                        S.op('dve', lambda e: e.tensor_tensor(out=x1, in0=tm[0], in1=tm[1], op=ALU.subtract), reads=['kt0', 'kt1'], writes=['kct'])
                        S.op('dve', lambda e: e.tensor_tensor(out=x2, in0=tm[2], in1=tm[3], op=ALU.add), reads=['kt2', 'kt3'], writes=['kct'])
                        bk, bkey = bank()
                        S.op('pe', lambda e, bk=bk: e.transpose(out=bk[:, 0:32], in_=B.kct[:, :, :].rearrange("p h d -> p (h d)"), identity=ident[0:32, 0:32]),
                             reads=['kct', 'ident'], writes=[bkey])
                        S.op('act', lambda e, jg=jg, bk=bk: e.copy(out=B.kcT[:, jg * 32:(jg + 1) * 32], in_=bk[:, 0:32]), reads=[bkey], writes=['kcT'])
                for x in range(2):
                    S.op('dve', lambda e, x=x: e.tensor_copy(out=B.XrT[x][:, 0:16], in_=B.XrT[x][:, 512:528]), reads=[f'XrT{x}'], writes=[f'XrT{x}'])

        ACC = [banks[5], banks[6]]
        ACCK = ['bank5', 'bank6']
        rb = [0]

        def bankB():
            i = rb[0] % 5
            rb[0] += 1
            return banks[i], f"bank{i}"

        def finish_branch(j, hq, br, ncols, first_branch):
            h = hq // 4
            pos = 2 * (hq % 4) + h
            for sub in range(NSUB):
                acc = ACC[sub]
                S.op('dve', lambda e, acc=acc: e.tensor_scalar(out=B.sc[:, 0:1], in0=acc[:, 64:65], scalar1=1e-30, scalar2=None, op0=ALU.max), reads=[ACCK[sub]], writes=['sc'])
                S.op('dve', lambda e: e.reciprocal(out=B.sc[:, 1:2], in_=B.sc[:, 0:1]), reads=['sc'], writes=['sc'])
                S.op('dve', lambda e, sub=sub: e.tensor_tensor(out=B.sc[:, 2:3], in0=B.sc[:, 1:2], in1=B.gates[:, sub, hq * 3 + br:hq * 3 + br + 1], op=ALU.mult),
                     reads=['sc', 'gates'], writes=['sc'])
                dst = B.ybt[:, sub, pos * 64:(pos + 1) * 64]
                if first_branch:
                    S.op('dve', lambda e, acc=acc, dst=dst: e.tensor_scalar(out=dst, in0=acc[:, 0:64], scalar1=B.sc[:, 2:3], scalar2=None, op0=ALU.mult),
                         reads=[ACCK[sub], 'sc'], writes=[f'ybt{sub}'])
                else:
                    S.op('dve', lambda e, acc=acc, dst=dst: e.scalar_tensor_tensor(out=dst, in0=acc[:, 0:64], scalar=B.sc[:, 2:3], in1=dst, op0=ALU.mult, op1=ALU.add),
                         reads=[ACCK[sub], 'sc', f'ybt{sub}'], writes=[f'ybt{sub}'])
                if ncols == 129:
                    di = B.imp[:, sub, h, :]
                    if hq % 4 == 0:
                        S.op('dve', lambda e, acc=acc, di=di: e.tensor_scalar(out=di, in0=acc[:, 65:129], scalar1=B.sc[:, 1:2], scalar2=None, op0=ALU.mult),
                             reads=[ACCK[sub], 'sc'], writes=['imp'])
                    else:
                        S.op('dve', lambda e, acc=acc, di=di: e.scalar_tensor_tensor(out=di, in0=acc[:, 65:129], scalar=B.sc[:, 1:2], in1=di, op0=ALU.mult, op1=ALU.add),
                             reads=[ACCK[sub], 'sc', 'imp'], writes=['imp'])

        def key_tile(qh, kT_ap, nk, extra, v_ap, ncols, subs_ok, first, last, pi, qtag=0):
            st, stk = bankB()
            n_e = len(extra)
            S.op('pe', lambda e: e.matmul(st[0:nk, 0:TT], lhsT=kT_ap, rhs=qh, start=True, stop=(n_e == 0)), reads=['qT', 'kv'], writes=[stk], tag=('qk', qtag))
            for i_, (lt, rh) in enumerate(extra):
                S.op('pe', lambda e, lt=lt, rh=rh, i_=i_: e.matmul(st[0:nk, 0:TT], lhsT=lt, rhs=rh, start=False, stop=(i_ == n_e - 1)), reads=['masks', 'selbT'], writes=[stk], tag=('ex', i_, nk))
            pT = B.pT[pi % 2]
            S.op('act', lambda e: e.activation(out=pT[0:nk, :], in_=st[0:nk, 0:TT], func=AF.Exp, scale=0.125), reads=[stk], writes=[f'pT{pi % 2}'])
            for sub in range(NSUB):
                if not subs_ok[sub]:
                    continue
                S.op('pe', lambda e, sub=sub: e.matmul(ACC[sub][:, 0:ncols], lhsT=pT[0:nk, sub * 128:(sub + 1) * 128], rhs=v_ap, start=first[sub], stop=last[sub]),
                     reads=[f'pT{pi % 2}', 'kv'], writes=[ACCK[sub]])

        def phaseB_tile(l, j):
            t0 = j * TT
            S.dma('sp', 'bq', B.qtok[:, :, :], qg_s[t0:t0 + TT, :].rearrange("(s p) c -> p s c", p=128), writes=['qtok'])
            S.dma('sp', 'bf', B.fb[:, :, :], fbias_d[t0:t0 + TT, :].rearrange("(s p) c -> p s c", p=128), writes=['fb'])
            for c in range(4):
                bk, bkey = bankB()
                for sub in range(NSUB):
                    S.op('pe', lambda e, c=c, sub=sub, bk=bk: e.transpose(out=bk[:, sub * 128:(sub + 1) * 128], in_=B.qtok[:, sub, c * 128:(c + 1) * 128], identity=ident[:, :]),
                         reads=['qtok', 'ident'], writes=[bkey])
                S.op('act', lambda e, c=c, bk=bk: e.copy(out=B.qT[:, c, :], in_=bk[:, 0:TT]), reads=[bkey], writes=['qT'])
            S.op('act', lambda e: e.activation(out=B.gates[:, :, :], in_=B.qtok[:, :, 512:536], func=AF.Sigmoid), reads=['qtok'], writes=['gates'])
            pi = [0]
            jgd = j // 2
            for hq in range(8):
                h = hq // 4
                qh = B.qT[64 * h:64 * h + 64, hq % 4, :]
                for jg in range(jgd + 1):
                    extra = []
                    if jg == jgd:
                        idx = (j % 2) + (2 if jg == 0 else 0)
                        extra.append((identb[0:32, 0:32], B.cmaskb[:, idx, :]))
                    elif jg == 0:
                        extra.append((identb[0:32, 0:32], B.cmaskb[:, 4, :]))
                    key_tile(qh, B.kcT[64 * h:64 * h + 64, jg * 32:(jg + 1) * 32], 32, extra, B.vcm[:, jg, h, :], 129,
                             [True] * NSUB, [jg == 0] * NSUB, [jg == jgd] * NSUB, pi[0], qtag=h)
                    pi[0] += 1
                finish_branch(j, hq, 0, 129, True)
            for sub in range(NSUB):
                for h in range(2):
                    S.op('dve', lambda e, sub=sub, h=h: e.tensor_tensor(out=B.score[:, :], in0=B.imp[:, sub, h, :], in1=B.fb[:, sub, :], op=ALU.add), reads=['imp', 'fb'], writes=['score'])
                    S.op('dve', lambda e: e.max(out=B.m8[:, 0:8], in_=B.score[:, :]), reads=['score'], writes=['m8'])
                    S.op('dve', lambda e: e.match_replace(out=B.scr2[:, :], in_to_replace=B.m8[:, 0:8], in_values=B.score[:, :], imm_value=-3.0e38), reads=['score', 'm8'], writes=['scr2'])
                    S.op('dve', lambda e: e.max(out=B.m8[:, 8:16], in_=B.scr2[:, :]), reads=['scr2'], writes=['m8'])
                    S.op('dve', lambda e: e.tensor_scalar(out=B.tmp64[:, :], in0=B.score[:, :], scalar1=B.m8[:, 15:16], scalar2=-30000.0, op0=ALU.is_lt, op1=ALU.mult),
                         reads=['score', 'm8'], writes=['tmp64'])
                    bk, bkey = bankB()
                    S.op('pe', lambda e, bk=bk: e.transpose(out=bk[0:64, 0:128], in_=B.tmp64[:, :], identity=ident[:, :]), reads=['tmp64', 'ident'], writes=[bkey])
                    S.op('act', lambda e, sub=sub, h=h, bk=bk: e.copy(out=B.selbT[:, h, sub * 128:(sub + 1) * 128], in_=bk[0:64, 0:128]), reads=[bkey], writes=['selbT'])
            for hq in range(8):
                h = hq // 4
                qh = B.qT[64 * h:64 * h + 64, hq % 4, :]
                nkt = NSUB * j + NSUB
                first = [True] * NSUB
                for kt in range(nkt):
                    d = kt - NSUB * j
                    extra = [(B.ebigb[:, kt * 128:(kt + 1) * 128], B.selbT[:, h, :])]
                    if d >= 0:
                        extra.append((identb[:, :], B.wmaskb[:, d + 4, :]))
                    ok = [(kt <= NSUB * j + sub) for sub in range(NSUB)]
                    last = [(kt == NSUB * j + sub) for sub in range(NSUB)]
                    key_tile(qh, B.ksT[64 * h:64 * h + 64, kt * 128:(kt + 1) * 128], 128, extra, B.vs[:, kt, h, :], 65, ok, list(first), last, pi[0], qtag=h)
                    first = [f and not o for f, o in zip(first, ok)]
                    pi[0] += 1
                finish_branch(j, hq, 1, 65, False)
                first = [True] * NSUB
                for kt in range(max(0, NSUB * j - 4), nkt):
                    d = kt - NSUB * j
                    extra = []
                    if d in (-4, -3, 0, 1):
                        extra.append((identb[:, :], B.wmaskb[:, d + 4, :]))
                    ok = [(kt <= NSUB * j + sub) and (kt >= NSUB * j + sub - 4) for sub in range(NSUB)]
                    last = [(kt == NSUB * j + sub) for sub in range(NSUB)]
                    key_tile(qh, B.kwT[64 * h:64 * h + 64, kt * 128:(kt + 1) * 128], 128, extra, B.vw[:, kt, h, :], 65, ok, list(first), last, pi[0], qtag=h)
                    first = [f and not o for f, o in zip(first, ok)]
                    pi[0] += 1
                finish_branch(j, hq, 2, 65, False)
            for c in range(4):
                bk, bkey = bankB()
                for sub in range(NSUB):
                    S.op('pe', lambda e, c=c, sub=sub, bk=bk: e.transpose(out=bk[:, sub * 128:(sub + 1) * 128], in_=B.ybt[:, sub, c * 128:(c + 1) * 128], identity=ident[:, :]),
                         reads=[f'ybt{sub}', 'ident'], writes=[bkey])
                S.op('act', lambda e, c=c, bk=bk: e.copy(out=B.ybT[:, c, :], in_=bk[:, 0:TT]), reads=[bkey], writes=['ybT'])
            S.dma('sp', 'byb', yb_s[j], B.ybT[:, :, :], reads=['ybT'], writes=['OUT_ybs'])


        def sample_B(l):
            rs = [0]

            def bankS():
                i = rs[0] % 4
                rs[0] += 1
                return banks[i], f"bank{i}"
            ACCs = [banks[5], banks[6]]
            ACCsK = ['bank5', 'bank6']
            IMP = [banks[7], banks[4]]
            IMPK = ['bank7', 'bank4']
            S.dma('sp', 'sq0', B.cosC2[:], ropeCc2_d[:, :, :], writes=['cosC2'])
            S.dma('sp', 'sq1', B.sinC2[:], ropeCs2_d[:, :, :], writes=['sinC2'])
            S.dma('pool', 'sq2', B.maploc[:, :], maploc_d[:, :], writes=['maploc'])
            S.dma('pool', 'sq3', B.lohi[:, :], lohi_d[:, :], writes=['lohi'])
            S.dma('sp', 'sq4', B.fbs[:, :], fbs_d[:, :], writes=['fbs'])
            S.dma('sp', 'sq4b', B.pcol[:, :], pcol_d[:, :], writes=['pcol'])
            S.op('dve', lambda e: e.memset(B.vcm_s[:, :, :, 64:65], 1.0), writes=['vcm_s'])
            S.op('dve', lambda e: e.memset(B.vpg[:, :, :, 64:65], 1.0), writes=['vpg'])
            S.op('dve', lambda e: e.memset(B.zl[:, :], 0.0), writes=['zl'])
            S.op('dve', lambda e: e.memset(B.zr[:, :], 0.0), writes=['zr'])
            S.dma('sp', 'sq5', B.qtok_s[:, :], qgs_s[:, :], writes=['qtok_s'])
            bk, bkey = bankS()
            for c in range(4):
                S.op('pe', lambda e, c=c, bk=bk: e.transpose(out=bk[:, c * 4:(c + 1) * 4], in_=B.qtok_s[0:4, c * 128:(c + 1) * 128], identity=ident[0:4, 0:4]), reads=['qtok_s', 'ident'], writes=[bkey])
            S.op('act', lambda e, bk=bk: e.copy(out=B.qT_s[:, :, :].rearrange("p c b -> p (c b)"), in_=bk[:, 0:16]), reads=[bkey], writes=['qT_s'])
            S.barrier()

            def gather(pool_ap, n, dst, key, slot):
                S.idma(slot, dst, pool_ap.rearrange("l n s c -> (l n s) c"), B.idx[:, n:n + 1], reads=['idx'], writes=[key])

            def ktile(b, h, kT_ap, nk, extra, v_ap, first, last, qtag, kk=('pgT', 'vpg')):
                qh = B.qT_s[64 * h:64 * h + 64, :, b]
                st, stk = bankS()
                n_e = len(extra)
                S.op('pe', lambda e: e.matmul(st[0:nk, 0:4], lhsT=kT_ap, rhs=qh, start=True, stop=(n_e == 0)), reads=['qT_s', kk[0]], writes=[stk], tag=('qk', qtag))
                for i_, (lt, rh) in enumerate(extra):
                    S.op('pe', lambda e, lt=lt, rh=rh, i_=i_: e.matmul(st[0:nk, 0:4], lhsT=lt, rhs=rh, start=False, stop=(i_ == n_e - 1)), reads=['selbb'], writes=[stk], tag=('ex', i_, nk))
                pi_ = rs[0] % 2
                pT = B.pTs[pi_]
                S.op('act', lambda e: e.activation(out=pT[0:nk, :], in_=st[0:nk, 0:4], func=AF.Exp, scale=0.125), reads=[stk], writes=[f'pTs{pi_}'])
                S.op('pe', lambda e: e.matmul(ACCs[h][0:4, 0:65], lhsT=pT[0:nk, 0:4], rhs=v_ap, start=first, stop=last), reads=[f'pTs{pi_}', kk[1]], writes=[ACCsK[h]], tag=('pv', nk))
                return pT, f'pTs{pi_}'

            def finish(b, h, br, first_branch):
                S.op('dve', lambda e: e.tensor_scalar(out=B.rd[:, h, 0:1], in0=ACCs[h][0:4, 64:65], scalar1=1e-30, scalar2=None, op0=ALU.max), reads=[ACCsK[h]], writes=['rd'])
                S.op('dve', lambda e: e.reciprocal(out=B.rd[:, h, 1:2], in_=B.rd[:, h, 0:1]), reads=['rd'], writes=['rd'])
                S.op('dve', lambda e: e.tensor_tensor(out=B.rd[:, h, 2:3], in0=B.rd[:, h, 1:2], in1=B.gcol[:, h, br:br + 1], op=ALU.mult), reads=['rd', 'gcol'], writes=['rd'])
                if first_branch:
                    S.op('dve', lambda e: e.tensor_scalar(out=B.yacc[:, h, :], in0=ACCs[h][0:4, 0:64], scalar1=B.rd[:, h, 2:3], scalar2=None, op0=ALU.mult), reads=[ACCsK[h], 'rd'], writes=['yacc'])
                else:
                    S.op('dve', lambda e: e.scalar_tensor_tensor(out=B.yacc[:, h, :], in0=ACCs[h][0:4, 0:64], scalar=B.rd[:, h, 2:3], in1=B.yacc[:, h, :], op0=ALU.mult, op1=ALU.add),
                         reads=[ACCsK[h], 'rd', 'yacc'], writes=['yacc'])

            for b in range(4):
                S.dma('sp', 'sq6', B.pti[:, :], ptab_d[b:b + 1, :].partition_broadcast(128), writes=['pti'])
                S.op('dve', lambda e: e.tensor_copy(out=B.idf[:, :], in_=B.pti[:, :]), reads=['pti'], writes=['idf'])
                S.op('dve', lambda e: e.tensor_scalar(out=B.idx[:, :], in0=B.idf[:, :], scalar1=128.0, scalar2=B.pcol[:, l:l + 1], op0=ALU.mult, op1=ALU.add),
                     reads=['idf', 'pcol'], writes=['idx'])
                for h in range(2):
                    S.dma('sp', f'sq7{h}', B.gcol[:, h, :], qgs_s[b, 512 + 12 * h:512 + 12 * h + 12].rearrange("(c r) -> c r", r=3), writes=['gcol'])
                S.op('act', lambda e: e.activation(out=B.gcol[:, :, :], in_=B.gcol[:, :, :], func=AF.Sigmoid), reads=['gcol'], writes=['gcol'])
                for x in range(2):
                    S.op('dve', lambda e, x=x: e.memset(B.XrT[x][:, 0:16], 0.0), writes=[f'XrT{x}'])
                for jg in range(32):
                    for s4 in range(4):
                        gather(ccmp_d, jg * 4 + s4, B.rows[0][:, s4, :], f'rows0_{s4}', f'pg{s4}')
                    for (dst, dkey, col0) in [(B.XrT[0][:, 16:528], 'XrT0', 0), (B.XrT[1][:, 16:528], 'XrT1', 128)]:
                        bk, bkey = bankS()
                        for sub in range(4):
                            S.op('pe', lambda e, sub=sub, col0=col0, bk=bk: e.transpose(out=bk[:, sub * 128:(sub + 1) * 128], in_=B.rows[0][:, sub, col0:col0 + 128], identity=ident[:, :]),
                                 reads=[f'rows0_{sub}', 'ident'], writes=[bkey])
                        S.op('act', lambda e, dst=dst, bk=bk: e.copy(out=dst, in_=bk[:, :]), reads=[bkey], writes=[dkey])
                    for x in range(2):
                        for h in range(2):
                            bk, bkey = bankS()
                            for s_ in range(32):
                                S.op('pe', lambda e, s_=s_, x=x, h=h, bk=bk: e.matmul(bk[:, 0:32], lhsT=B.w1[x][64 * h:64 * h + 64, s_, :],
                                                                                       rhs=B.XrT[x][64 * h:64 * h + 64, s_:s_ + 497:16], start=(s_ == 0), stop=(s_ == 31)),
                                     reads=[f'w1_{x}', f'XrT{x}'], writes=[bkey], tag=('cp', h))
                            S.op('act', lambda e, x=x, bk=bk: e.activation(out=B.hact[:, :], in_=bk[:, 0:32], func=AF.Gelu, bias=B.hpe[x][:, 0:1]), reads=[bkey, f'hpe{x}'], writes=['hact'])
                            bk2, bk2key = bankS()
                            S.op('pe', lambda e, x=x, bk2=bk2: e.matmul(bk2[0:32, 0:64], lhsT=B.hact[:, :], rhs=B.w2[x][:, :], start=True, stop=True), reads=['hact', f'w2_{x}'], writes=[bk2key])
                            if x == 0:
                                S.op('act', lambda e, h=h, bk2=bk2: e.copy(out=B.kct[:, h, :], in_=bk2[0:32, 0:64]), reads=[bk2key], writes=['kct'])
                            else:
                                S.op('act', lambda e, h=h, jg=jg, bk2=bk2: e.copy(out=B.vcm_s[:, jg, h, 0:64], in_=bk2[0:32, 0:64]), reads=[bk2key], writes=['vcm_s'])
                        if x == 0:
                            x1 = B.kct[:, :, 0:8]
                            x2 = B.kct[:, :, 8:16]
                            cb_ = B.cosC2[:, jg, :].unsqueeze(1).broadcast_to([32, 2, 8])
                            sb_ = B.sinC2[:, jg, :].unsqueeze(1).broadcast_to([32, 2, 8])
                            tm = [t_[:, :, :] for t_ in B.kt]
                            S.op('dve', lambda e: e.tensor_tensor(out=tm[0], in0=x1, in1=cb_, op=ALU.mult), reads=['kct', 'cosC2'], writes=['kt0'])
                            S.op('dve', lambda e: e.tensor_tensor(out=tm[1], in0=x2, in1=sb_, op=ALU.mult), reads=['kct', 'sinC2'], writes=['kt1'])
                            S.op('dve', lambda e: e.tensor_tensor(out=tm[2], in0=x2, in1=cb_, op=ALU.mult), reads=['kct', 'cosC2'], writes=['kt2'])
                            S.op('dve', lambda e: e.tensor_tensor(out=tm[3], in0=x1, in1=sb_, op=ALU.mult), reads=['kct', 'sinC2'], writes=['kt3'])
                            S.op('dve', lambda e: e.tensor_tensor(out=x1, in0=tm[0], in1=tm[1], op=ALU.subtract), reads=['kt0', 'kt1'], writes=['kct'])
                            S.op('dve', lambda e: e.tensor_tensor(out=x2, in0=tm[2], in1=tm[3], op=ALU.add), reads=['kt2', 'kt3'], writes=['kct'])
                            bk, bkey = bankS()
                            S.op('pe', lambda e, bk=bk: e.transpose(out=bk[:, 0:32], in_=B.kct[:, :, :].rearrange("p h d -> p (h d)"), identity=ident[0:32, 0:32]), reads=['kct', 'ident'], writes=[bkey])
                            S.op('act', lambda e, jg=jg, bk=bk: e.copy(out=B.kcT_s[:, jg * 32:(jg + 1) * 32], in_=bk[:, 0:32]), reads=[bkey], writes=['kcT_s'])
                    for x in range(2):
                        S.op('dve', lambda e, x=x: e.tensor_copy(out=B.XrT[x][:, 0:16], in_=B.XrT[x][:, 512:528]), reads=[f'XrT{x}'], writes=[f'XrT{x}'])
                for h in range(2):
                    S.op('pe', lambda e, h=h: e.matmul(IMP[h][0:4, 0:272], lhsT=B.zl[:, :], rhs=B.zr[:, :], start=True, stop=True), reads=['zl', 'zr'], writes=[IMPK[h]], tag=('z',))
                    for jg in range(32):
                        extra = [(identb[0:32, 0:32], B.cmaskb[:, 4, 0:4])] if jg == 0 else []
                        pT, pk = ktile(b, h, B.kcT_s[64 * h:64 * h + 64, jg * 32:(jg + 1) * 32], 32, extra, B.vcm_s[:, jg, h, :], jg == 0, jg == 31, h, kk=('kcT_s', 'vcm_s'))
                        S.op('pe', lambda e, jg=jg, h=h, pT=pT: e.matmul(IMP[h][0:4, 8 * jg:8 * jg + 9], lhsT=pT[0:32, 0:4], rhs=B.maploc[:, 0:9], start=False, stop=True, skip_group_check=True),
                             reads=[pk, 'maploc'], writes=[IMPK[h]], tag=('z',))
                    finish(b, h, 0, True)
                    S.op('act', lambda e, h=h: e.copy(out=B.impsb[:, :], in_=IMP[h][0:4, 0:264]), reads=[IMPK[h]], writes=['impsb'])
                    rw, rwk = bankS()
                    S.op('pe', lambda e, h=h, rw=rw: e.matmul(rw[0:1, 0:264], lhsT=B.rd[:, h, 1:2], rhs=B.impsb[:, :], start=True, stop=True), reads=['rd', 'impsb'], writes=[rwk])
                    S.op('dve', lambda e, rw=rw: e.tensor_tensor(out=B.scs[:, 0:257], in0=rw[0:1, 1:258], in1=B.fbs[:, 0:257], op=ALU.add), reads=[rwk, 'fbs'], writes=['scs'])
                    S.op('dve', lambda e: e.max(out=B.m8s[:, 0:8], in_=B.scs[:, 0:257]), reads=['scs'], writes=['m8s'])
                    S.op('dve', lambda e: e.match_replace(out=B.scs2[:, 0:257], in_to_replace=B.m8s[:, 0:8], in_values=B.scs[:, 0:257], imm_value=-3.0e38), reads=['scs', 'm8s'], writes=['scs2'])
                    S.op('dve', lambda e: e.max(out=B.m8s[:, 8:16], in_=B.scs2[:, 0:257]), reads=['scs2'], writes=['m8s'])
                    S.op('dve', lambda e, h=h: e.tensor_scalar(out=B.selbb[:, h, 0:257], in0=B.scs[:, 0:257], scalar1=B.m8s[:, 15:16], scalar2=-30000.0, op0=ALU.is_lt, op1=ALU.mult),
                         reads=['scs', 'm8s'], writes=['selbb'])
                for n4 in range(32):
                    for s4 in range(4):
                        gather(csel_d, n4 * 4 + s4, B.rows[1][:, s4, :], f'rows1_{s4}', f'pg{s4}')
                    bk, bkey = bankS()
                    for s4 in range(4):
                        S.op('pe', lambda e, s4=s4, bk=bk: e.transpose(out=bk[:, s4 * 128:(s4 + 1) * 128], in_=B.rows[1][:, s4, 0:128], identity=ident[:, :]), reads=[f'rows1_{s4}', 'ident'], writes=[bkey])
                    S.op('act', lambda e, bk=bk: e.copy(out=B.pgT[:, :], in_=bk[:, :]), reads=[bkey], writes=['pgT'])
                    S.op('dve', lambda e: e.tensor_copy(out=B.vpg[:, :, :, 0:64], in_=B.rows[1][:, :, 128:256].rearrange("p s (h d) -> p s h d", h=2)), reads=[f'rows1_{q_}' for q_ in range(4)], writes=['vpg'])
                    for s4 in range(4):
                        n = n4 * 4 + s4
                        for h in range(2):
                            extra = [(B.lohi[0:1, 0:128], B.selbb[0:1, h, 2 * n:2 * n + 1].broadcast_to([1, 4])),
                                     (B.lohi[0:1, 128:256], B.selbb[0:1, h, 2 * n + 1:2 * n + 2].broadcast_to([1, 4]))]
                            ktile(b, h, B.pgT[64 * h:64 * h + 64, s4 * 128:(s4 + 1) * 128], 128, extra, B.vpg[:, s4, h, :], n == 0, False, h)
                S.dma('sp', 'sq8', B.rows[1][0:1, 0, :], sel_rows_s[l, b:b + 1, :], writes=['rows1_0'])
                bk, bkey = bankS()
                S.op('pe', lambda e, bk=bk: e.transpose(out=bk[:, 0:1], in_=B.rows[1][0:1, 0, 0:128], identity=ident[0:1, 0:1]), reads=['rows1_0', 'ident'], writes=[bkey])
                S.op('act', lambda e, bk=bk: e.copy(out=B.pgT[:, 0:1], in_=bk[:, 0:1]), reads=[bkey], writes=['pgT'])
                S.op('dve', lambda e: e.tensor_copy(out=B.vpg[0:1, 0, :, 0:64], in_=B.rows[1][0:1, 0, 128:256].rearrange("p (h d) -> p h d", h=2)), reads=['rows1_0'], writes=['vpg'])
                for h in range(2):
                    ktile(b, h, B.pgT[64 * h:64 * h + 64, 0:1], 1, [], B.vpg[0:1, 0, h, :], False, True, h)
                    finish(b, h, 1, False)
                S.dma('sp', 'sq9', B.rows[2][:, :, :], win_so[l, b, :, :].rearrange("(s p) c -> p s c", p=128), reads=['OUT_win_s', 'OUT_win_s2'], writes=['rows2'])
                bk, bkey = bankS()
                for s4 in range(4):
                    S.op('pe', lambda e, s4=s4, bk=bk: e.transpose(out=bk[:, s4 * 128:(s4 + 1) * 128], in_=B.rows[2][:, s4, 0:128], identity=ident[:, :]), reads=['rows2', 'ident'], writes=[bkey])
                S.op('act', lambda e, bk=bk: e.copy(out=B.pgT[:, :], in_=bk[:, :]), reads=[bkey], writes=['pgT'])
                S.op('dve', lambda e: e.tensor_copy(out=B.vpg[:, :, :, 0:64], in_=B.rows[2][:, :, 128:256].rearrange("p s (h d) -> p s h d", h=2)), reads=['rows2'], writes=['vpg'])
                for h in range(2):
                    for s4 in range(4):
                        ktile(b, h, B.pgT[64 * h:64 * h + 64, s4 * 128:(s4 + 1) * 128], 128, [], B.vpg[:, s4, h, :], s4 == 0, s4 == 3, h)
                    finish(b, h, 2, False)
                    S.dma('sp', f'sqa{h}', ybr_s[b, :].rearrange("(c two d) -> c two d", two=2, d=64)[:, h, :], B.yacc[:, h, :], reads=['yacc'], writes=['OUT_ybr_s'])
            S.barrier()
            S.dma('sp', 'sqb', B.ybt_s[:, :], ybr_s[:, :], writes=['ybt_s'])
            bk, bkey = bankS()
            for c in range(4):
                S.op('pe', lambda e, c=c, bk=bk: e.transpose(out=bk[:, c * 4:(c + 1) * 4], in_=B.ybt_s[0:4, c * 128:(c + 1) * 128], identity=ident[0:4, 0:4]), reads=['ybt_s', 'ident'], writes=[bkey])
            S.op('act', lambda e, bk=bk: e.copy(out=B.ybT_s[:, :, :].rearrange("p c b -> p (c b)"), in_=bk[:, 0:16]), reads=[bkey], writes=['ybT_s'])
            S.dma('sp', 'sqc', ybs_s[:, :, :], B.ybT_s[:, :, :], reads=['ybT_s'], writes=['OUT_ybs_s'])

        C_ = K()

        def alloc_C(stk):
            uid[0] += 1
            u_ = f"_{uid[0]}"
            cur[0] = stk
            C_.gm = sb("gm" + u_, [128, 3, 2, TT], BF16)
            C_.m32 = sb("m32" + u_, [128, 2, TT])
            C_.mt = sb("mt" + u_, [128, TT])
            C_.mb = sb("mb" + u_, [128, 8, TT], BF16)
            C_.yin = sb("yin" + u_, [128, 3, 4, TT], BF16)

        def phaseC_tile(tasks, l, j, N=TT, srcs=None):
            x1src, ysrcs = srcs if srcs is not None else (x1_s[j], [ya_s[j], yb_s[j], yc_s[j]])
            Win = W['w_in'][l]

            def ld():
                S.dma('sp', 'cx', xA[:, :, 0:N], x1src, writes=[f'xA{k}' for k in range(8)])
                for k in range(8):
                    S.op('dve', lambda e, k=k: e.tensor_copy(out=xAb[:, k, 0:N], in_=xA[:, k, 0:N]), reads=[f'xA{k}'], writes=[f'xAb{k}'])
                for i_, src in enumerate(ysrcs):
                    S.dma('sp', f'cy{i_}', C_.yin[:, i_, :, 0:N], src, writes=[f'yin{i_}'])
            tasks.append((None, ld))
            Wp = [W['pa'][l], W['pb'][l], W['pc'][l]]
            for b_ in range(4):
                for br in range(3):
                    def fg(view, key, br=br):
                        for ml in range(2):
                            bk, bkey = bank()
                            for k in range(8):
                                S.op('pe', lambda e, k=k, ml=ml, bk=bk: e.matmul(bk[:, 0:N], lhsT=view[:, k, ml * 128:(ml + 1) * 128], rhs=xAb[:, k, 0:N], start=(k == 0), stop=(k == 7)),
                                     reads=[key, f'xAb{k}'], writes=[bkey])
                            S.op('act', lambda e, ml=ml, bk=bk, br=br: e.activation(out=C_.gm[:, br, ml, 0:N], in_=bk[:, 0:N], func=AF.Sigmoid), reads=[bkey], writes=[f'gm{br}{ml}'])
                    tasks.append(((Win, 0, 8, 3352 + br * 1024 + 256 * b_, 256), fg))
                for br in range(3):
                    def fp(view, key, br=br, b_=b_):
                        for ml in range(2):
                            bk, bkey = bank()
                            for k in range(4):
                                S.op('pe', lambda e, k=k, ml=ml, bk=bk: e.matmul(bk[:, 0:N], lhsT=view[:, k, ml * 128:(ml + 1) * 128], rhs=C_.yin[:, br, k, 0:N], start=(k == 0), stop=(k == 3)),
                                     reads=[key, f'yin{br}'], writes=[bkey])
                            if br == 0:
                                S.op('dve', lambda e, ml=ml, bk=bk: e.tensor_tensor(out=C_.m32[:, ml, 0:N], in0=bk[:, 0:N], in1=C_.gm[:, 0, ml, 0:N], op=ALU.mult),
                                     reads=[bkey, f'gm0{ml}'], writes=[f'm32{ml}'])
                            else:
                                S.op('dve', lambda e, ml=ml, bk=bk, br=br: e.tensor_tensor(out=C_.mt[:, 0:N], in0=bk[:, 0:N], in1=C_.gm[:, br, ml, 0:N], op=ALU.mult),
                                     reads=[bkey, f'gm{br}{ml}'], writes=['mt'])
                                if br == 1:
                                    S.op('dve', lambda e, ml=ml: e.tensor_tensor(out=C_.m32[:, ml, 0:N], in0=C_.m32[:, ml, 0:N], in1=C_.mt[:, 0:N], op=ALU.add),
                                         reads=[f'm32{ml}', 'mt'], writes=[f'm32{ml}'])
                                else:
                                    S.op('dve', lambda e, ml=ml, b_=b_: e.tensor_tensor(out=C_.mb[:, 2 * b_ + ml, 0:N], in0=C_.m32[:, ml, 0:N], in1=C_.mt[:, 0:N], op=ALU.add),
                                         reads=[f'm32{ml}', 'mt'], writes=[f'mb{2 * b_ + ml}'])
                    tasks.append(((Wp[br], 0, 4, 256 * b_, 256), fp))

            def fo(view, key):
                for m in range(8):
                    bk, bkey = bank()
                    for k in range(8):
                        S.op('pe', lambda e, k=k, m=m, bk=bk: e.matmul(bk[:, 0:N], lhsT=view[:, k, m * 128:(m + 1) * 128], rhs=C_.mb[:, k, 0:N], start=(k == 0), stop=(k == 7)),
                             reads=[key, f'mb{k}'], writes=[bkey])
                    S.op('act', lambda e, m=m: e.activation(out=rS[:, m, 0:N], in_=xA[:, m, 0:N], func=AF.Copy, scale=ALPHA), reads=[f'xA{m}'], writes=[f'rS{m}'])
                    S.op('dve', lambda e, m=m, bk=bk: e.tensor_tensor(out=rS[:, m, 0:N], in0=bk[:, 0:N], in1=rS[:, m, 0:N], op=ALU.add), reads=[bkey, f'rS{m}'], writes=[f'rS{m}'])
            tasks.append(((W['wo'][l], 0, 8, 0, 1024), fo))

        def sample_A(l):
            N = 4
            tasks = []
            load_x_tile(tasks, xs_d if l == 0 else xmid_s, N, [(0, 4)], ['OUT_xmid_s'] if l else [])
            ffn(tasks, N, W['ffn1_gu'][l], W['ffn1_dn'][l])
            layernorm(tasks, N, l, 0)
            Win = W['w_in'][l]
            tk = ['tokq0']

            def proj(view, key, c_lo, pieces):
                for (p0, pn) in pieces:
                    bk, bkey = bank()
                    for k in range(8):
                        S.op('pe', lambda e, k=k, p0=p0, pn=pn, bk=bk: e.matmul(bk[0:4, 0:pn], lhsT=xAb[:, k, 0:4], rhs=view[:, k, p0:p0 + pn], start=(k == 0), stop=(k == 7)),
                             reads=[key, f'xAb{k}'], writes=[bkey])
                    S.op('act', lambda e, p0=p0, pn=pn, bk=bk: e.copy(out=tokq[0:4, 0, c_lo + p0:c_lo + p0 + pn], in_=bk[0:4, 0:pn]), reads=[bkey], writes=tk)
            tasks.append(((Win, 0, 8, 512, 1024), lambda view, key: proj(view, key, 0, [(0, 512), (512, 512)])))
            tasks.append(((Win, 0, 8, 1536, 280), lambda view, key: proj(view, key, 1024, [(0, 280)])))

            def rope_store():
                S.dma('sp', 'sa0', SA.ropeS[:], ropeS_d[:, :, :], writes=['ropeS'])
                S.dma('sp', 'sa1', cmp_rows_s[l], tokq[0:4, 0, 512:768], reads=tk, writes=['OUT_cmp_s'])
                for (c0, nh) in [(0, 8), (768, 2), (1024, 2)]:
                    V = tokq[0:4, 0, c0:c0 + nh * 64].rearrange("p (h d) -> p h d", h=nh)
                    x1 = V[:, :, 0:8]
                    x2 = V[:, :, 8:16]
                    cb_ = SA.ropeS[0:4, 0, :].unsqueeze(1).broadcast_to([4, nh, 8])
                    sb_ = SA.ropeS[0:4, 1, :].unsqueeze(1).broadcast_to([4, nh, 8])
                    tm = [rt[i][0:4, 0, 0:nh, :] for i in range(4)]
                    S.op('dve', lambda e: e.tensor_tensor(out=tm[0], in0=x1, in1=cb_, op=ALU.mult), reads=tk + ['ropeS'], writes=['rt0'])
                    S.op('dve', lambda e: e.tensor_tensor(out=tm[1], in0=x2, in1=sb_, op=ALU.mult), reads=tk + ['ropeS'], writes=['rt1'])
                    S.op('dve', lambda e: e.tensor_tensor(out=tm[2], in0=x2, in1=cb_, op=ALU.mult), reads=tk + ['ropeS'], writes=['rt2'])
                    S.op('dve', lambda e: e.tensor_tensor(out=tm[3], in0=x1, in1=sb_, op=ALU.mult), reads=tk + ['ropeS'], writes=['rt3'])
                    S.op('dve', lambda e: e.tensor_tensor(out=x1, in0=tm[0], in1=tm[1], op=ALU.subtract), reads=['rt0', 'rt1'], writes=tk)
                    S.op('dve', lambda e: e.tensor_tensor(out=x2, in0=tm[2], in1=tm[3], op=ALU.add), reads=['rt2', 'rt3'], writes=tk)
                S.dma('sp', 'sa2', sel_rows_s[l], tokq[0:4, 0, 768:1024], reads=tk, writes=['OUT_sel_s'])
                S.dma('sp', 'sa3', win_so[l, :, 511, :], tokq[0:4, 0, 1024:1280], reads=tk, writes=['OUT_win_s'])
                S.dma('sp', 'sa4', win_so[l, :, 0:511, :], cwin_d[l, :, 1:512, :], writes=['OUT_win_s2'])
                S.dma('sp', 'sa5', qgs_s[:, 0:512], tokq[0:4, 0, 0:512], reads=tk, writes=['OUT_qgs_s'])
                S.dma('sp', 'sa6', qgs_s[:, 512:536], tokq[0:4, 0, 1280:1304], reads=tk, writes=['OUT_qgs_s2'])
                S.dma('sp', 'sa7', x1s_s[:, :, :], xA[:, :, 0:4], reads=[f'xA{k}' for k in range(8)], writes=['OUT_x1s_s'])
            tasks.append((None, rope_store))
            dcol = lambda c: s5p[:, 1072 + c:1073 + c]
            bglu = lambda c: s5p[:, 1076 + c:1077 + c]

            def fu(view, key):
                for c in range(4):
                    bk, bkey = bank()
                    for k in range(8):
                        S.op('pe', lambda e, k=k, c=c, bk=bk: e.matmul(bk[:, 0:N], lhsT=view[:, k, c * 128:(c + 1) * 128], rhs=xAb[:, k, 0:N], start=(k == 0), stop=(k == 7)),
                             reads=[key, f'xAb{k}'], writes=[bkey])
                    S.op('act', lambda e, c=c, bk=bk: e.copy(out=u32[:, c, 0:N], in_=bk[:, 0:N]), reads=[bkey], writes=[f'u32_{c}'])
                    S.op('dve', lambda e, c=c: e.tensor_copy(out=ub[:, c, 0:N], in_=u32[:, c, 0:N]), reads=[f'u32_{c}'], writes=[f'ub{c}'])
            tasks.append(((Win, 0, 8, 0, 512), fu))

            def step():
                S.dma('sp', 'sb0', SA.s0t[:, 0, :, :], s5re0_d[l].rearrange("b s c -> s b c"), writes=['s0t'])
                S.dma('sp', 'sb1', SA.s0t[:, 1, :, :], s5im0_d[l].rearrange("b s c -> s b c"), writes=['s0t'])
                bk, bkey = bank()
                for ri_ in range(2):
                    for b_ in range(4):
                        o_ = (ri_ * 4 + b_) * 16
                        S.op('pe', lambda e, ri_=ri_, b_=b_, o_=o_, bk=bk: e.transpose(out=bk[:, o_:o_ + 16], in_=SA.s0t[0:16, ri_, b_, :], identity=ident[0:16, 0:16]),
                             reads=['s0t', 'ident'], writes=[bkey])
                S.op('act', lambda e, bk=bk: e.copy(out=SA.s0[:, :, :, :].rearrange("p r s b -> p r b s"), in_=bk[:, 0:128].rearrange("p (r b s) -> p r b s", r=2, b=4)),
                     reads=[bkey], writes=['s0'])
                bA, bAk = bank()
                bB, bBk = bank()
                for sbi in range(16):
                    S.op('pe', lambda e, sbi=sbi: e.matmul(bA[:, sbi * 4:(sbi + 1) * 4], lhsT=BTr[:, sbi, :], rhs=ub[:, sbi // 4, 0:4], start=True, stop=True),
                         reads=['BTr', f'ub{sbi // 4}'], writes=[bAk])
                    S.op('pe', lambda e, sbi=sbi: e.matmul(bB[:, sbi * 4:(sbi + 1) * 4], lhsT=BTi[:, sbi, :], rhs=ub[:, sbi // 4, 0:4], start=True, stop=True),
                         reads=['BTi', f'ub{sbi // 4}'], writes=[bBk])
                are_b = s5t[4][:, :].unsqueeze(2).broadcast_to([128, 16, 4])
                aim_b = s5t[5][:, :].unsqueeze(2).broadcast_to([128, 16, 4])
                A3 = bA[:, 0:64].rearrange("p (s b) -> p s b", b=4)
                B3 = bB[:, 0:64].rearrange("p (s b) -> p s b", b=4)
                s0r = SA.s0[:, 0, :, :]
                s0i = SA.s0[:, 1, :, :]
                t1 = SA.t[0][:, :, :]
                t2 = SA.t[1][:, :, :]
                S.op('dve', lambda e: e.tensor_tensor(out=t1, in0=s0r, in1=are_b, op=ALU.mult), reads=['s0', 's5t4'], writes=['sat0'])
                S.op('dve', lambda e: e.tensor_tensor(out=t2, in0=s0i, in1=aim_b, op=ALU.mult), reads=['s0', 's5t5'], writes=['sat1'])
                S.op('dve', lambda e: e.tensor_tensor(out=t1, in0=t1, in1=t2, op=ALU.subtract), reads=['sat0', 'sat1'], writes=['sat0'])
                S.op('dve', lambda e: e.tensor_tensor(out=SA.sn[:, 0, :, :], in0=A3, in1=t1, op=ALU.add), reads=[bAk, 'sat0'], writes=['sn'])
                S.op('dve', lambda e: e.tensor_tensor(out=t1, in0=s0i, in1=are_b, op=ALU.mult), reads=['s0', 's5t4'], writes=['sat0'])
                S.op('dve', lambda e: e.tensor_tensor(out=t2, in0=s0r, in1=aim_b, op=ALU.mult), reads=['s0', 's5t5'], writes=['sat1'])
                S.op('dve', lambda e: e.tensor_tensor(out=t1, in0=t1, in1=t2, op=ALU.add), reads=['sat0', 'sat1'], writes=['sat0'])
                S.op('dve', lambda e: e.tensor_tensor(out=SA.sn[:, 1, :, :], in0=B3, in1=t1, op=ALU.add), reads=[bBk, 'sat0'], writes=['sn'])
                S.op('dve', lambda e: e.tensor_copy(out=SA.snb[:, :, :, :], in_=SA.sn[:, :, :, :]), reads=['sn'], writes=['snb'])
                for c in range(4):
                    for a in range(4):
                        sbi = c * 4 + a
                        S.op('pe', lambda e, sbi=sbi, a=a: e.matmul(YB[:, 0:4], lhsT=CTr[:, sbi, :], rhs=SA.snb[:, 0, sbi, :], start=(a == 0), stop=False), reads=['CTr', 'snb'], writes=[YBK])
                        S.op('pe', lambda e, sbi=sbi, a=a: e.matmul(YB[:, 0:4], lhsT=CTi[:, sbi, :], rhs=SA.snb[:, 1, sbi, :], start=False, stop=(a == 3)), reads=['CTi', 'snb'], writes=[YBK])
                    S.op('dve', lambda e, c=c: e.scalar_tensor_tensor(out=yag[:, c, 0:N], in0=u32[:, c, 0:N], scalar=dcol(c), in1=YB[:, 0:N], op0=ALU.mult, op1=ALU.add),
                         reads=[f'u32_{c}', 's5p', YBK], writes=[f'yag{c}'])
                    S.op('act', lambda e, c=c: e.activation(out=yag[:, c, 0:N], in_=yag[:, c, 0:N], func=AF.Gelu), reads=[f'yag{c}'], writes=[f'yag{c}'])
                    S.op('dve', lambda e, c=c: e.tensor_copy(out=yagb[:, c, 0:N], in_=yag[:, c, 0:N]), reads=[f'yag{c}'], writes=[f'yagb{c}'])
                for ri_, dst, nm in [(0, s5re_so, 'OUT_s5re_s'), (1, s5im_so, 'OUT_s5im_s')]:
                    bk, bkey = bank()
                    for b_ in range(4):
                        S.op('pe', lambda e, ri_=ri_, b_=b_, bk=bk: e.transpose(out=bk[0:16, b_ * 128:(b_ + 1) * 128], in_=SA.sn[:, ri_, :, b_], identity=ident[:, :]),
                             reads=['sn', 'ident'], writes=[bkey])
                    S.op('act', lambda e, ri_=ri_, bk=bk: e.copy(out=SA.s0t[0:16, ri_, :, :], in_=bk[0:16, 0:512].rearrange("p (b c) -> p b c", b=4)), reads=[bkey], writes=['s0t'])
                    S.dma('sp', f'sc{ri_}', dst[l].rearrange("b s c -> s b c"), SA.s0t[0:16, ri_, :, :], reads=['s0t'], writes=[nm])
            tasks.append((None, step))

            def fglu(view, key):
                for m in range(4):
                    bk, bkey = bank()
                    for k in range(4):
                        S.op('pe', lambda e, k=k, m=m, bk=bk: e.matmul(bk[:, 0:N], lhsT=view[:, k, m * 128:(m + 1) * 128], rhs=yagb[:, k, 0:N], start=(k == 0), stop=(k == 3)),
                             reads=[key, f'yagb{k}'], writes=[bkey])
                    S.op('act', lambda e, m=m, bk=bk: e.activation(out=st1[:, 0:N], in_=bk[:, 0:N], func=AF.Sigmoid, bias=bglu(m)), reads=[bkey, 's5p'], writes=['st1'])
                    S.op('dve', lambda e, m=m: e.tensor_tensor(out=ya2b[:, m, 0:N], in0=yag[:, m, 0:N], in1=st1[:, 0:N], op=ALU.mult), reads=[f'yag{m}', 'st1'], writes=[f'ya2b{m}'])
            tasks.append(((W['glu'][l], 0, 4, 0, 512), fglu))
            wcol = lambda k_, c: s5p[:, 1080 + k_ * 4 + c:1081 + k_ * 4 + c]

            def fbc(view, key):
                for c8 in range(8):
                    bk, bkey = bank()
                    for k in range(8):
                        S.op('pe', lambda e, k=k, c8=c8, bk=bk: e.matmul(bk[:, 0:N], lhsT=view[:, k, c8 * 128:(c8 + 1) * 128], rhs=xAb[:, k, 0:N], start=(k == 0), stop=(k == 7)),
                             reads=[key, f'xAb{k}'], writes=[bkey])
                    dst = cb32 if c8 < 4 else cc32
                    nm = ('cb' if c8 < 4 else 'cc') + str(c8 % 4)
                    S.op('act', lambda e, c8=c8, bk=bk, dst=dst: e.copy(out=dst[:, c8 % 4, 0:N], in_=bk[:, 0:N]), reads=[bkey], writes=[nm])
            tasks.append(((Win, 0, 8, 1816, 1024), fbc))

            def fh(view, key):
                S.dma('sp', 'sd0', SA.cvt[:, :], cconv_d[l].rearrange("b t c -> (b t) c"), writes=['cvt'])
                S.dma('sp', 'sd1', conv_so[l, :, 0, :], cconv_d[l, :, 1, :], writes=['OUT_conv_s0'])
                bq, bqk = bank()
                for c in range(4):
                    S.op('pe', lambda e, c=c: e.transpose(out=bq[:, c * 8:(c + 1) * 8], in_=SA.cvt[0:8, c * 128:(c + 1) * 128], identity=ident[0:8, 0:8]), reads=['cvt', 'ident'], writes=[bqk])
                S.op('act', lambda e: e.copy(out=SA.cbuf[:, :, :, :].rearrange("p c b t -> p (c b t)"), in_=bq[:, 0:32]), reads=[bqk], writes=['cbuf'])
                for c in range(4):
                    bk, bkey = bank()
                    for k in range(8):
                        S.op('pe', lambda e, k=k, c=c, bk=bk: e.matmul(bk[:, 0:N], lhsT=view[:, k, c * 128:(c + 1) * 128], rhs=xAb[:, k, 0:N], start=(k == 0), stop=(k == 7)),
                             reads=[key, f'xAb{k}'], writes=[bkey])
                    S.op('dve', lambda e, c=c, bk=bk: e.tensor_tensor(out=SA.vn[:, c, :], in0=bk[:, 0:N], in1=cc32[:, c, 0:N], op=ALU.mult), reads=[bkey, f'cc{c}'], writes=['vn'])
                    S.op('act', lambda e, c=c: e.activation(out=ctm[:, 0:N], in_=SA.vn[:, c, :], func=AF.Copy, scale=wcol(2, c)), reads=['vn', 's5p'], writes=['ctm'])
                    S.op('dve', lambda e, c=c: e.scalar_tensor_tensor(out=ctm[:, 0:N], in0=SA.cbuf[:, c, :, 1], scalar=wcol(1, c), in1=ctm[:, 0:N], op0=ALU.mult, op1=ALU.add),
                         reads=['cbuf', 's5p', 'ctm'], writes=['ctm'])
                    S.op('dve', lambda e, c=c: e.scalar_tensor_tensor(out=ctm[:, 0:N], in0=SA.cbuf[:, c, :, 0], scalar=wcol(0, c), in1=ctm[:, 0:N], op0=ALU.mult, op1=ALU.add),
                         reads=['cbuf', 's5p', 'ctm'], writes=['ctm'])
                    S.op('dve', lambda e, c=c: e.tensor_tensor(out=ycb[:, c, 0:N], in0=ctm[:, 0:N], in1=cb32[:, c, 0:N], op=ALU.mult), reads=['ctm', f'cb{c}'], writes=[f'ycb{c}'])
                bk, bkey = bank()
                for c in range(4):
                    S.op('pe', lambda e, c=c, bk=bk: e.transpose(out=bk[0:4, c * 128:(c + 1) * 128], in_=SA.vn[:, c, :], identity=ident[:, :]), reads=['vn', 'ident'], writes=[bkey])
                S.op('act', lambda e, bk=bk: e.copy(out=smallo[0:4, 0:512], in_=bk[0:4, 0:512]), reads=[bkey], writes=['smallo'])
                S.dma('sp', 'sd2', conv_so[l, :, 1, :], smallo[0:4, 0:512], reads=['smallo'], writes=['OUT_conv_s1'])
                S.dma('sp', 'sd3', yas_s[:, :, :], ya2b[:, :, 0:4], reads=[f'ya2b{c}' for c in range(4)], writes=['OUT_yas_s'])
                S.dma('sp', 'sd4', ycs_s[:, :, :], ycb[:, :, 0:4], reads=[f'ycb{c}' for c in range(4)], writes=['OUT_ycs_s'])
            tasks.append(((Win, 0, 8, 2840, 512), fh))
            run_tasks(tasks)

        def dbg_dump(tasks, name, tile_ap, keys):
            if name not in dbg_out:
                return

            def fn():
                S.dma('sp', 'dbg_' + name, dbg_out[name], tile_ap, reads=keys, writes=['OUT_dbg_' + name])
            tasks.append((None, fn))

        subs512 = [(a * 128, 128) for a in range(NSUB)]
        nlayer = FLAGS.get('nlayer', DEPTH)
        for l in range(nlayer):
            with ExitStack() as pa:
                alloc_ffn(pa)
                alloc_A(pa)
                s5_setup(l)
                for j in range(ntile):
                    tasks = []
                    load_x_tile(tasks, (xp if l == 0 else xmid)[j * TT:(j + 1) * TT, :], TT, subs512, ['OUT_xmid'] if l else [])
                    ffn(tasks, TT, W['ffn1_gu'][l], W['ffn1_dn'][l])
                    layernorm(tasks, TT, l, 0)
                    if j == 0 and l == 0:
                        dbg_dump(tasks, 'x1', xA[:, :, :], [f'xA{k}' for k in range(8)])
                    win_tokmajor(tasks, l, j)
                    s5_tile(tasks, l, j, TT)
                    conv_tile(tasks, l, j, TT)
                    if j == 0 and l == 0:
                        dbg_dump(tasks, 'ya', yag[:, :, :], [f'yag{c}' for c in range(4)])
                    spill_A(tasks, j)
                    run_tasks(tasks)
                if FLAGS.get('sample', True):
                    sample_A(l)
                S.barrier()
            if not FLAGS.get('phaseB', True) or (l == nlayer - 1 and FLAGS.get('skip_last_BC', False)):
                continue
            with ExitStack() as pb:
                alloc_B(pb)
                phaseB_build(l)
                S.barrier()
                for j in range(ntile):
                    phaseB_tile(l, j)
                S.barrier()
                if FLAGS.get('sample', True):
                    sample_B(l)
                    S.barrier()
            with ExitStack() as pc:
                alloc_ffn(pc)
                alloc_C(pc)
                for j in range(ntile):
                    tasks = []
                    phaseC_tile(tasks, l, j)
                    layernorm(tasks, TT, l, 1)
                    if j == 0 and l == 0:
                        dbg_dump(tasks, 'x2', xA[:, :, :], [f'xA{k}' for k in range(8)])
                    ffn(tasks, TT, W['ffn2_gu'][l], W['ffn2_dn'][l])
                    layernorm(tasks, TT, l, 2)
                    if l == DEPTH - 1:
                        store_x_tile(tasks, y_prompt[j * TT:(j + 1) * TT, :], TT, subs512, 'y_prompt')
                    else:
                        store_x_tile(tasks, xmid[j * TT:(j + 1) * TT, :], TT, subs512, 'xmid')
                    run_tasks(tasks)
                if FLAGS.get('sample', True):
                    tasks = []
                    phaseC_tile(tasks, l, 0, 4, (x1s_s[:, :, :], [yas_s[:, :, :], ybs_s[:, :, :], ycs_s[:, :, :]]))
                    layernorm(tasks, 4, l, 1)
                    ffn(tasks, 4, W['ffn2_gu'][l], W['ffn2_dn'][l])
                    layernorm(tasks, 4, l, 2)
                    store_x_tile(tasks, y_sample if l == DEPTH - 1 else xmid_s, 4, [(0, 4)], 'y_sample' if l == DEPTH - 1 else 'xmid_s')
                    run_tasks(tasks)
                S.barrier()
        outs = [k for k in S.bufs if k.startswith('OUT_')]
        S.wait_all('sp', outs)
        print(f"[build] ops={S.nops} waits={S.nwaits} sems={S.nsem}")
    return nc


def host_consts():
    c = {}
    c['ident'] = np.eye(128, dtype=np.float32)
    c['onesm'] = np.full((128, 128), 1.0 / D, dtype=np.float32)
    mm = np.zeros((128, 16, 8), np.float32)
    for g2 in range(2):
        for sbi in range(16):
            mm[g2 * 64:(g2 + 1) * 64, sbi, (2 * sbi + g2) % 8] = 1.0
    c['mmask'] = mm
    c['iota128'] = np.tile(np.arange(1, LCH + 1, dtype=np.float32)[None, :], (128, 1))
    return c


def pack_s5(inputs):
    out = np.zeros((DEPTH, 128, S5X), np.float32)
    for l in range(DEPTH):
        def gp(a):
            return a.reshape(16, 2, 64).transpose(1, 2, 0).reshape(128, 16)
        out[l, :, 0:16] = gp(inputs['s5_lambda_re'][l])
        out[l, :, 16:32] = gp(inputs['s5_lambda_im'][l])
        out[l, :, 32:48] = gp(np.repeat(inputs['s5_log_dt'][l][:, None], 64, axis=1))
        out[l, :, 48:304] = inputs['s5_b_re'][l].reshape(16, 2, 64, 16).transpose(1, 2, 0, 3).reshape(128, 256)
        out[l, :, 304:560] = inputs['s5_b_im'][l].reshape(16, 2, 64, 16).transpose(1, 2, 0, 3).reshape(128, 256)
        out[l, :, 560:816] = inputs['s5_c_re'][l].reshape(16, 2, 16, 64).transpose(1, 3, 0, 2).reshape(128, 256)
        out[l, :, 816:1072] = inputs['s5_c_im'][l].reshape(16, 2, 16, 64).transpose(1, 3, 0, 2).reshape(128, 256)
        out[l, :, 1072:1076] = inputs['s5_d'][l].reshape(4, 128).T
        out[l, :, 1076:1080] = inputs['s5_b_glu'][l].reshape(4, 128).T
        out[l, :, 1080:1092] = inputs['conv_w'][l].reshape(3, 4, 128).transpose(2, 0, 1).reshape(128, 12)
    return out


HP_ = [0, 4, 1, 5, 2, 6, 3, 7]


def nsa_consts():
    c = {}
    half = 8
    inv_freq = (np.float32(500000.0) ** (-np.arange(half, dtype=np.float32) / np.float32(half))).astype(np.float32)
    npr = np.arange(256)
    pos = (16 * npr + 15).astype(np.float32)
    ang = (pos[:, None] * inv_freq[None, :]).astype(np.float32)
    c['ropeCc'] = np.ascontiguousarray(np.cos(ang).astype(np.float32).reshape(8, 32, 8).transpose(1, 0, 2))
    c['ropeCs'] = np.ascontiguousarray(np.sin(ang).astype(np.float32).reshape(8, 32, 8).transpose(1, 0, 2))
    jb = np.arange(64)
    mp = ((npr[:, None] >= 4 * jb[None, :]) & (npr[:, None] <= 4 * jb[None, :] + 4)).astype(np.float32)
    c['mapc'] = np.ascontiguousarray(mp.reshape(8, 32, 64).transpose(1, 0, 2))
    NEG = np.float32(-30000.0)
    r = np.arange(128)[:, None]
    cc = np.arange(TT)[None, :]
    wm = np.zeros((128, 6, TT), np.float32)
    for di, d in enumerate(range(-4, 2)):
        dp = cc - (128 * d + r)
        wm[:, di, :] = np.where((dp >= 0) & (dp < 512), 0.0, NEG)
    c['wmask'] = wm.reshape(128, 6 * TT)
    r32 = np.arange(32)[:, None]
    cm = np.zeros((32, 5, TT), np.float32)
    cm[:, 0, :] = np.where(16 * r32 + 15 <= cc, 0.0, NEG)
    cm[:, 1, :] = np.where(16 * r32 + 15 <= cc + 256, 0.0, NEG)
    cm[:, 2, :] = cm[:, 0, :]
    cm[0, 2, :] = NEG
    cm[:, 3, :] = cm[:, 1, :]
    cm[0, 3, :] = NEG
    cm[0, 4, :] = NEG
    c['cmask'] = cm.reshape(32, 5 * TT)
    k = np.arange(T)[None, :]
    c['ebig'] = (k // 64 == np.arange(64)[:, None]).astype(np.float32)
    q = np.arange(T)[:, None]
    cur = q // 64
    blk = np.arange(64)[None, :]
    valid = blk <= cur
    forced = valid & ((blk == 0) | (blk >= cur - 1))
    c['fbias'] = np.where(forced, np.float32(1e9), np.where(valid, np.float32(0.0), np.float32(-1e30))).astype(np.float32)
    return c


def rope_tables():
    half = 8
    inv_freq = (np.float32(500000.0) ** (-np.arange(half, dtype=np.float32) / np.float32(half))).astype(np.float32)
    pos = np.arange(T, dtype=np.float32)
    ang = (pos[:, None] * inv_freq[None, :]).astype(np.float32)
    c = np.cos(ang).astype(np.float32).reshape(32, 128, 8).transpose(1, 0, 2)
    s_ = np.sin(ang).astype(np.float32).reshape(32, 128, 8).transpose(1, 0, 2)
    return np.ascontiguousarray(c), np.ascontiguousarray(s_)


def sample_consts():
    c = {}
    half = 8
    inv_freq = (np.float32(500000.0) ** (-np.arange(half, dtype=np.float32) / np.float32(half))).astype(np.float32)
    ang = (np.float32(16384.0) * inv_freq).astype(np.float32)
    rs = np.zeros((4, 2, 8), np.float32)
    rs[:, 0, :] = np.cos(ang).astype(np.float32)[None, :]
    rs[:, 1, :] = np.sin(ang).astype(np.float32)[None, :]
    c['ropeS'] = rs
    lohi = np.zeros((1, 256), np.float32)
    lohi[0, 0:64] = 1.0
    lohi[0, 128 + 64:256] = 1.0
    c['lohi'] = lohi
    fb = np.zeros((1, 264), np.float32)
    fb[0, [0, 255, 256]] = 1e9
    c['fbias_s'] = fb
    ml = np.zeros((32, 16), np.float32)
    r = np.arange(32)[:, None]
    i = np.arange(9)[None, :]
    ml[:, 0:9] = ((r >= 4 * (i - 1)) & (r <= 4 * (i - 1) + 4)).astype(np.float32)
    c['maploc'] = ml
    c['pcol'] = (np.arange(128, dtype=np.float32)[:, None] + np.arange(DEPTH, dtype=np.float32)[None, :] * np.float32(5120 * 128)).astype(np.float32)
    npr = np.arange(1024)
    pos = (16 * npr + 15).astype(np.float32)
    ang2 = (pos[:, None] * inv_freq[None, :]).astype(np.float32)
    c['ropeCc2'] = np.ascontiguousarray(np.cos(ang2).astype(np.float32).reshape(32, 32, 8).transpose(1, 0, 2))
    c['ropeCs2'] = np.ascontiguousarray(np.sin(ang2).astype(np.float32).reshape(32, 32, 8).transpose(1, 0, 2))
    return c


def kernel(**inputs):
    dbg = inputs.pop('_dbg', None)
    ntile = inputs.pop('_ntile', NTILE)
    nc = build_program(dbg, ntile)
    consts = host_consts()
    lng = np.ascontiguousarray(inputs['ln_g'].reshape(DEPTH * 3, 8, 128).transpose(2, 0, 1).reshape(128, 48))
    lnb = np.ascontiguousarray(inputs['ln_b'].reshape(DEPTH * 3, 8, 128).transpose(2, 0, 1).reshape(128, 48))
    rc, rs_ = rope_tables()
    s5pk = pack_s5(inputs)
    nsac = nsa_consts()
    smc = sample_consts()
    qperm = np.concatenate([np.arange(512 + h * 64, 512 + (h + 1) * 64) for h in HP_])
    w_in_p = np.array(inputs['w_in'], copy=True)
    w_in_p[:, :, 512:1024] = inputs['w_in'][:, :, qperm]
    bperm = np.concatenate([np.arange(h * 64, (h + 1) * 64) for h in HP_])
    w_pb_p = np.ascontiguousarray(inputs['w_proj_b'][:, bperm, :])
    peTk = np.ascontiguousarray(inputs['nsa_pe_k'].transpose(0, 2, 1))
    peTv = np.ascontiguousarray(inputs['nsa_pe_v'].transpose(0, 2, 1))
    use_sample = FLAGS.get('sample', True)
    npool = inputs['cache_cmp_kv'].shape[1]
    ccmp = inputs['cache_cmp_kv'].reshape(DEPTH, npool, 128, 256)
    csel = inputs['cache_sel_kv'].reshape(DEPTH, npool, 128, 256)
    in_maps = []
    for c in range(NCORES):
        m = dict(consts)
        m['xp'] = np.ascontiguousarray(inputs['x_prompt'][c])
        m['lng'] = lng
        m['lnb'] = lnb
        for k in ['ffn1_w_gu', 'ffn1_w_down', 'ffn2_w_gu', 'ffn2_w_down', 's5_w_glu',
                  'nsa_phi_k1', 'nsa_phi_k2', 'nsa_phi_v1', 'nsa_phi_v2', 'w_proj_a', 'w_proj_c', 'w_o']:
            m[k] = inputs[k]
        m['w_in'] = w_in_p
        m['w_proj_b'] = w_pb_p
        m['ropec'], m['ropes'] = rc, rs_
        m['s5p'] = s5pk
        m.update(nsac)
        m['peTk'] = peTk
        m['peTv'] = peTv
        m.update(smc)
        sl = slice(4 * c, 4 * c + 4)
        m['xs'] = np.ascontiguousarray(inputs['x_sample'][sl, 0, :])
        m['cache_cmp'] = ccmp
        m['cache_sel'] = csel
        m['cwin'] = np.ascontiguousarray(inputs['cache_win_kv'][:, sl].reshape(DEPTH, 4, 512, 256))
        m['cconv'] = np.ascontiguousarray(inputs['cache_conv'][:, sl])
        m['s5re0'] = np.ascontiguousarray(inputs['state_s5_re'][:, sl].reshape(DEPTH, 4, 16, 128))
        m['s5im0'] = np.ascontiguousarray(inputs['state_s5_im'][:, sl].reshape(DEPTH, 4, 16, 128))
        m['ptab'] = np.ascontiguousarray(inputs['page_table'][sl].astype(np.int32))
        in_maps.append(m)
    res = run_bass_kernel_spmd(nc, in_maps, core_ids=list(range(NCORES)))
    R = res.results
    DEBUG['res'] = R

    def st(name, axis):
        return np.stack([R[c][name] for c in range(NCORES)], axis=axis)

    def cat(name, axis):
        return np.concatenate([R[c][name] for c in range(NCORES)], axis=axis)
    y_prompt = st('y_prompt', 0)
    y_sample = cat('y_sample', 0).reshape(32, 1, D)
    cmp_p = st('cmp_rows_p', 1).reshape(DEPTH, NCORES, T, 2, 2, 64)
    cmp_s = cat('cmp_rows_s', 1).reshape(DEPTH, 32, 1, 2, 2, 64)
    sel_p = st('sel_rows_p', 1).reshape(DEPTH, NCORES, T, 2, 2, 64)
    sel_s = cat('sel_rows_s', 1).reshape(DEPTH, 32, 1, 2, 2, 64)
    win_p = st('win_p', 1).reshape(DEPTH, NCORES, 512, 2, 2, 64)
    win_s = cat('win_so', 1).reshape(DEPTH, 32, 512, 2, 2, 64)
    conv_p = st('conv_p', 1)
    conv_s = cat('conv_so', 1)
    re_p = st('s5re_p', 1).reshape(DEPTH, NCORES, 32, 64)
    re_s = cat('s5re_so', 1).reshape(DEPTH, 32, 32, 64)
    im_p = st('s5im_p', 1).reshape(DEPTH, NCORES, 32, 64)
    im_s = cat('s5im_so', 1).reshape(DEPTH, 32, 32, 64)
    outs = (y_prompt, y_sample, cmp_p, cmp_s, sel_p, sel_s, win_p, win_s, conv_p, conv_s, re_p, re_s, im_p, im_s)
    return tuple(np.ascontiguousarray(o, dtype=np.float32) for o in outs)
```
